# Optimizing a Trainium2 kernel written in Bass

```python
import math
import jax, jax.numpy as jnp
from jax import lax
import numpy as np

D_MODEL = 1024
BATCH = 4
SEQ = 4096
DEPTH = 2
DEC_BATCH = 8
DEC_SEQ = 64
PAST_LEN = 1024

CHUNK = 64
DA_HEADS = 4
DA_HEAD_DIM = 64
DA_VDIM = 2 * DA_HEAD_DIM
DA_WIDTH = DA_HEADS * DA_VDIM
ML_HEADS = 4
ML_HEAD_DIM = 64
ML_WIDTH = ML_HEADS * ML_HEAD_DIM
ML_CONV = 4
CM_GROUPS = 4
CM_WIDTH = 256
CM_GROUP_DIM = CM_WIDTH // CM_GROUPS
CM_CHUNK = 128
MIX_WIDTH = DA_WIDTH + ML_WIDTH + CM_WIDTH
SPLIT_SIZES = (DA_WIDTH, DA_WIDTH, DA_WIDTH, ML_WIDTH, ML_WIDTH, ML_WIDTH, ML_HEADS, ML_HEADS, CM_WIDTH, CM_WIDTH)
IN_WIDTH = sum(SPLIT_SIZES)
PEER_HEADS = 8
PEER_KEYS = 128
PEER_EXPERTS = PEER_KEYS * PEER_KEYS
PEER_QDIM = 256
PEER_HALF = PEER_QDIM // 2
PEER_TOPK = 16
PEER_BLOCK = 128
REL_BUCKETS = 32
REL_MAX_DIST = 128
QBLOCK = 128
EPS = 1e-6
NEG_INF = -1e30

kernel_name = "hybrid_stream_encoder_step"


def rmsnorm(x, g):
    xf = x.astype(jnp.float32)
    y = xf * lax.rsqrt(jnp.mean(xf * xf, axis=-1, keepdims=True) + EPS)
    return (y * g.astype(jnp.float32)).astype(x.dtype)


def split_proj(p):
    offs = [int(o) for o in np.cumsum(SPLIT_SIZES)[:-1]]
    return jnp.split(p, offs, axis=-1)


def rel_bucket(rel):
    half = REL_BUCKETS // 2
    max_exact = half // 2
    ret = jnp.where(rel > 0, half, 0)
    n = jnp.abs(rel)
    nf = jnp.maximum(n, 1).astype(jnp.float32)
    large = max_exact + (jnp.log(nf / max_exact) / math.log(REL_MAX_DIST / max_exact)
                         * (half - max_exact)).astype(jnp.int32)
    large = jnp.minimum(large, half - 1)
    return ret + jnp.where(n < max_exact, n, large)


def rel_bias(q_pos, k_pos, table):
    b = rel_bucket(k_pos[None, :] - q_pos[:, None])
    return jnp.transpose(table[b], (2, 0, 1)).astype(jnp.float32)


def diff_weights(q, k, bias, mask, lam):
    s = jnp.einsum('bqhcd,bkhcd->bhcqk', q, k).astype(jnp.float32) * (DA_HEAD_DIM ** -0.5)
    s = s + bias[None, :, None]
    if mask is not None:
        s = jnp.where(mask, s, NEG_INF)
    p = jax.nn.softmax(s, axis=-1)
    return p[:, :, 0] - lam * p[:, :, 1]


def diff_attention_prompt(q, k, v, rel_table, lam):
    B, S = q.shape[0], q.shape[1]
    nb = S // QBLOCK
    kk = k.reshape(B, S, DA_HEADS, 2, DA_HEAD_DIM)
    k_pos = jnp.arange(S)
    qb = jnp.moveaxis(q.reshape(B, nb, QBLOCK, DA_HEADS, 2, DA_HEAD_DIM), 1, 0)

    def block(args):
        qi, bi = args
        q_pos = bi * QBLOCK + jnp.arange(QBLOCK)
        mask = (k_pos[None, :] // CHUNK) <= (q_pos[:, None] // CHUNK)
        a = diff_weights(qi, kk, rel_bias(q_pos, k_pos, rel_table), mask, lam)
        return jnp.einsum('bhqk,bkhv->bqhv', a.astype(v.dtype), v)

    o = lax.map(block, (qb, jnp.arange(nb)))
    return jnp.moveaxis(o, 0, 1).reshape(B, S, DA_HEADS, DA_VDIM)


def diff_attention_step(q, k_all, v_all, rel_table, lam, past_len):
    B, L = q.shape[0], q.shape[1]
    K = k_all.shape[1]
    kk = k_all.reshape(B, K, DA_HEADS, 2, DA_HEAD_DIM)
    q_pos = past_len + jnp.arange(L)
    k_pos = jnp.arange(K)
    a = diff_weights(q, kk, rel_bias(q_pos, k_pos, rel_table), None, lam)
    return jnp.einsum('bhqk,bkhv->bqhv', a.astype(v_all.dtype), v_all)


def causal_conv(x, prev, w, b):
    L = x.shape[1]
    xp = jnp.concatenate([prev, x], axis=1)
    y = b
    for i in range(ML_CONV):
        y = y + xp[:, i:i + L] * w[i]
    return y, xp[:, xp.shape[1] - (ML_CONV - 1):]


def mlstm_chunk(state, inp):
    C, n, m = state
    q, k, v, ig, lf = inp
    L = q.shape[2]
    F = jnp.cumsum(lf, axis=-1)
    causal = jnp.tril(jnp.ones((L, L), dtype=bool))
    D = jnp.where(causal, F[..., :, None] - F[..., None, :] + ig[..., None, :], NEG_INF)
    inter = F + m[..., None]
    m_t = jnp.maximum(inter, jnp.max(D, axis=-1))
    Sw = jnp.einsum('bhtd,bhsd->bhts', q, k) * jnp.exp(D - m_t[..., None])
    iw = jnp.exp(inter - m_t)
    num = jnp.einsum('bhts,bhsv->bhtv', Sw, v) + iw[..., None] * jnp.einsum('bhtd,bhdv->bhtv', q, C)
    den = jnp.sum(Sw, axis=-1) + iw * jnp.einsum('bhtd,bhd->bht', q, n)
    h = num / jnp.maximum(jnp.abs(den), jnp.exp(-m_t))[..., None]
    FL = F[..., -1]
    tail = FL[..., None] - F + ig
    m_new = jnp.maximum(FL + m, jnp.max(tail, axis=-1))
    wc = jnp.exp(FL + m - m_new)
    ws = jnp.exp(tail - m_new[..., None])
    C_new = wc[..., None, None] * C + jnp.einsum('bhs,bhsd,bhsv->bhdv', ws, k, v)
    n_new = wc[..., None] * n + jnp.einsum('bhs,bhsd->bhd', ws, k)
    return (C_new, n_new, m_new), h


def peer(x2d, w_q, sub_keys, u_tab, v_tab):
    T = x2d.shape[0]
    Tp = ((T + PEER_BLOCK - 1) // PEER_BLOCK) * PEER_BLOCK
    xb = jnp.pad(x2d, ((0, Tp - T), (0, 0))).reshape(Tp // PEER_BLOCK, PEER_BLOCK, D_MODEL)
    keys = sub_keys.astype(jnp.float32)

    def block(xi):
        q = (xi @ w_q).astype(jnp.float32).reshape(PEER_BLOCK, PEER_HEADS, PEER_QDIM)
        q = q * lax.rsqrt(jnp.mean(q * q, axis=-1, keepdims=True) + EPS)
        q = q.reshape(PEER_BLOCK, PEER_HEADS, 2, PEER_HALF)
        s = jnp.einsum('thcd,hckd->thck', q, keys)
        v1, i1 = lax.top_k(s[:, :, 0], PEER_TOPK)
        v2, i2 = lax.top_k(s[:, :, 1], PEER_TOPK)
        cand = (v1[..., :, None] + v2[..., None, :]).reshape(PEER_BLOCK, PEER_HEADS, PEER_TOPK * PEER_TOPK)
        vs, ci = lax.top_k(cand, PEER_TOPK)
        e = (jnp.take_along_axis(i1, ci // PEER_TOPK, axis=-1) * PEER_KEYS
             + jnp.take_along_axis(i2, ci % PEER_TOPK, axis=-1))
        g = jax.nn.softmax(vs, axis=-1)
        act = jax.nn.gelu(jnp.einsum('thkd,td->thk', u_tab[e], xi).astype(jnp.float32), approximate=False)
        return jnp.einsum('thk,thkd->td', (g * act).astype(v_tab.dtype), v_tab[e])

    y = lax.map(block, xb).reshape(Tp, D_MODEL)
    return y[:T]


def layer(x, past, l, rel_table, norm1_g, w_in, da_lambda, da_subln_g, ml_conv_w, ml_conv_b,
          ml_wq, ml_wk, ml_gate_b, ml_norm_g, ml_skip, cm_norm_g, cm_ws, cm_b, w_out,
          norm2_g, peer_wq, peer_keys, peer_u, peer_v):
    B, L = x.shape[0], x.shape[1]
    f32 = jnp.float32
    lam_init = 0.8 - 0.6 * math.exp(-0.3 * l)
    xn = rmsnorm(x, norm1_g)
    qa, ka, va, mc, mv, mo, mi, mf, cu, cv = split_proj(xn @ w_in)

    q = qa.reshape(B, L, DA_HEADS, 2, DA_HEAD_DIM)
    k_rows = ka.reshape(B, L, DA_HEADS, 2 * DA_HEAD_DIM)
    v_rows = va.reshape(B, L, DA_HEADS, DA_VDIM)
    lp = da_lambda.astype(f32)
    lam = jnp.exp(jnp.sum(lp[0] * lp[1])) - jnp.exp(jnp.sum(lp[2] * lp[3])) + lam_init
    if past is None:
        o_da = diff_attention_prompt(q, k_rows, v_rows, rel_table, lam)
    else:
        past_k, past_v = past[0], past[1]
        k_all = jnp.concatenate([past_k, k_rows], axis=1)
        v_all = jnp.concatenate([past_v, v_rows], axis=1)
        o_da = diff_attention_step(q, k_all, v_all, rel_table, lam, past_k.shape[1])
    o_da = (rmsnorm(o_da, da_subln_g) * (1.0 - lam_init)).reshape(B, L, DA_WIDTH)

    prev = jnp.zeros((B, ML_CONV - 1, ML_WIDTH), x.dtype) if past is None else past[5]
    cc, conv_new = causal_conv(mc, prev, ml_conv_w, ml_conv_b)
    cc = jax.nn.silu(cc)
    cch = cc.reshape(B, L, ML_HEADS, ML_HEAD_DIM)
    qm = jnp.einsum('blhd,hde->bhle', cch, ml_wq).astype(f32)
    km = jnp.einsum('blhd,hde->bhle', cch, ml_wk).astype(f32) * (ML_HEAD_DIM ** -0.5)
    vm = jnp.transpose(mv.reshape(B, L, ML_HEADS, ML_HEAD_DIM), (0, 2, 1, 3)).astype(f32)
    ig = jnp.transpose((mi + ml_gate_b[0]).astype(f32), (0, 2, 1))
    lf = jax.nn.log_sigmoid(jnp.transpose((mf + ml_gate_b[1]).astype(f32), (0, 2, 1)))
    if past is None:
        nc = L // CHUNK
        state0 = (jnp.zeros((B, ML_HEADS, ML_HEAD_DIM, ML_HEAD_DIM), f32),
                  jnp.zeros((B, ML_HEADS, ML_HEAD_DIM), f32),
                  jnp.zeros((B, ML_HEADS), f32))

        def to_chunks(t):
            return jnp.moveaxis(t.reshape(t.shape[:2] + (nc, CHUNK) + t.shape[3:]), 2, 0)

        (C_new, n_new, m_new), hs = lax.scan(
            mlstm_chunk, state0,
            (to_chunks(qm), to_chunks(km), to_chunks(vm), to_chunks(ig), to_chunks(lf)))
        h = jnp.moveaxis(hs, 0, 2).reshape(B, ML_HEADS, L, ML_HEAD_DIM)
    else:
        (C_new, n_new, m_new), h = mlstm_chunk(
            (past[2].astype(f32), past[3].astype(f32), past[4].astype(f32)), (qm, km, vm, ig, lf))
    h = jnp.transpose(h, (0, 2, 1, 3))
    hn = rmsnorm(h, ml_norm_g.reshape(ML_HEADS, ML_HEAD_DIM)).reshape(B, L, ML_WIDTH).astype(x.dtype)
    o_ml = (hn + ml_skip * cc) * jax.nn.sigmoid(mo)

    u = jax.nn.gelu(cu, approximate=False)
    vcm = rmsnorm(jax.nn.gelu(cv, approximate=False), cm_norm_g)
    ws = cm_ws * jnp.tril(jnp.ones((CM_CHUNK, CM_CHUNK), cm_ws.dtype))
    vg = vcm.reshape(B, L, CM_GROUPS, CM_GROUP_DIM)
    if past is None:
        nchunk = L // CM_CHUNK
        vgc = vg.reshape(B, nchunk, CM_CHUNK, CM_GROUPS, CM_GROUP_DIM)
        mixed = jnp.einsum('gts,bnsgd->bntgd', ws, vgc) + cm_b.T[None, None, :, :, None]
        mixed = mixed.reshape(B, L, CM_WIDTH)
    else:
        mixed = jnp.einsum('gts,bsgd->btgd', ws[:, :L, :L], vg) + cm_b[:, :L].T[None, :, :, None]
        mixed = mixed.reshape(B, L, CM_WIDTH)
    o_cm = u * mixed

    x = x + jnp.concatenate([o_da, o_ml, o_cm], axis=-1) @ w_out
    xn2 = rmsnorm(x, norm2_g).reshape(B * L, D_MODEL)
    x = x + peer(xn2, peer_wq, peer_keys, peer_u, peer_v).reshape(B, L, D_MODEL)
    if past is None:
        return x, (k_rows, v_rows, C_new, n_new, m_new, conv_new)
    return x, (k_rows, v_rows, C_new, n_new, m_new, conv_new, vcm)


def setup_inputs(seed: int = 0) -> dict:
    key = jax.random.key(seed)
    ks = jax.random.split(key, 32)
    nrm = jax.random.normal
    f32 = jnp.float32
    fb = jnp.linspace(3.0, 6.0, ML_HEADS, dtype=f32)
    gate_b = jnp.stack([0.1 * nrm(ks[12], (DEPTH, ML_HEADS), f32),
                        fb[None, :] + 0.1 * nrm(ks[13], (DEPTH, ML_HEADS), f32)], axis=1)
    return {
        "x_prompt": nrm(ks[0], (BATCH, SEQ, D_MODEL), f32),
        "x_sample": nrm(ks[1], (DEC_BATCH, DEC_SEQ, D_MODEL), f32),
        "cache_k": nrm(ks[2], (DEPTH, DEC_BATCH, PAST_LEN, DA_HEADS, 2 * DA_HEAD_DIM), f32),
        "cache_v": nrm(ks[3], (DEPTH, DEC_BATCH, PAST_LEN, DA_HEADS, DA_VDIM), f32),
        "state_mlstm_c": 0.3 * nrm(ks[4], (DEPTH, DEC_BATCH, ML_HEADS, ML_HEAD_DIM, ML_HEAD_DIM), f32),
        "state_mlstm_n": 0.3 * nrm(ks[5], (DEPTH, DEC_BATCH, ML_HEADS, ML_HEAD_DIM), f32),
        "state_mlstm_m": 0.5 * nrm(ks[6], (DEPTH, DEC_BATCH, ML_HEADS), f32),
        "state_mlstm_conv": nrm(ks[7], (DEPTH, DEC_BATCH, ML_CONV - 1, ML_WIDTH), f32),
        "norm1_g": 1.0 + 0.1 * nrm(ks[8], (DEPTH, D_MODEL), f32),
        "w_in": nrm(ks[9], (DEPTH, D_MODEL, IN_WIDTH), f32) * D_MODEL ** -0.5,
        "da_lambda": 0.1 * nrm(ks[10], (DEPTH, 4, DA_HEAD_DIM), f32),
        "da_subln_g": 1.0 + 0.1 * nrm(ks[11], (DEPTH, DA_VDIM), f32),
        "rel_bias_table": 0.5 * nrm(ks[14], (REL_BUCKETS, DA_HEADS), f32),
        "ml_conv_w": nrm(ks[15], (DEPTH, ML_CONV, ML_WIDTH), f32) * ML_CONV ** -0.5,
        "ml_conv_b": 0.05 * nrm(ks[16], (DEPTH, ML_WIDTH), f32),
        "ml_wq": nrm(ks[17], (DEPTH, ML_HEADS, ML_HEAD_DIM, ML_HEAD_DIM), f32) * ML_HEAD_DIM ** -0.5,
        "ml_wk": nrm(ks[18], (DEPTH, ML_HEADS, ML_HEAD_DIM, ML_HEAD_DIM), f32) * ML_HEAD_DIM ** -0.5,
        "ml_gate_b": gate_b,
        "ml_norm_g": 1.0 + 0.1 * nrm(ks[19], (DEPTH, ML_WIDTH), f32),
        "ml_skip": 1.0 + 0.1 * nrm(ks[20], (DEPTH, ML_WIDTH), f32),
        "cm_norm_g": 1.0 + 0.1 * nrm(ks[21], (DEPTH, CM_WIDTH), f32),
        "cm_ws": nrm(ks[22], (DEPTH, CM_GROUPS, CM_CHUNK, CM_CHUNK), f32) * CM_CHUNK ** -0.5,
        "cm_b": 1.0 + 0.1 * nrm(ks[23], (DEPTH, CM_GROUPS, CM_CHUNK), f32),
        "w_out": nrm(ks[24], (DEPTH, MIX_WIDTH, D_MODEL), f32) * MIX_WIDTH ** -0.5,
        "norm2_g": 1.0 + 0.1 * nrm(ks[25], (DEPTH, D_MODEL), f32),
        "peer_wq": nrm(ks[26], (DEPTH, D_MODEL, PEER_HEADS * PEER_QDIM), f32) * D_MODEL ** -0.5,
        "peer_keys": nrm(ks[27], (DEPTH, PEER_HEADS, 2, PEER_KEYS, PEER_HALF), f32) * PEER_HALF ** -0.5,
        "peer_u": nrm(ks[28], (DEPTH, PEER_EXPERTS, D_MODEL), f32) * D_MODEL ** -0.5,
        "peer_v": 0.1 * nrm(ks[29], (DEPTH, PEER_EXPERTS, D_MODEL), f32),
        "final_g": 1.0 + 0.1 * nrm(ks[30], (D_MODEL,), f32),
    }


def reference(x_prompt, x_sample, cache_k, cache_v, state_mlstm_c, state_mlstm_n, state_mlstm_m,
              state_mlstm_conv, norm1_g, w_in, da_lambda, da_subln_g, rel_bias_table, ml_conv_w,
              ml_conv_b, ml_wq, ml_wk, ml_gate_b, ml_norm_g, ml_skip, cm_norm_g, cm_ws, cm_b,
              w_out, norm2_g, peer_wq, peer_keys, peer_u, peer_v, final_g):
    hp, hs = x_prompt, x_sample
    st_p, st_s = [], []
    for l in range(DEPTH):
        lw = (norm1_g[l], w_in[l], da_lambda[l], da_subln_g[l], ml_conv_w[l], ml_conv_b[l],
              ml_wq[l], ml_wk[l], ml_gate_b[l], ml_norm_g[l], ml_skip[l], cm_norm_g[l], cm_ws[l],
              cm_b[l], w_out[l], norm2_g[l], peer_wq[l], peer_keys[l], peer_u[l], peer_v[l])
        hp, sp = layer(hp, None, l, rel_bias_table, *lw)
        past = (cache_k[l], cache_v[l], state_mlstm_c[l], state_mlstm_n[l], state_mlstm_m[l],
                state_mlstm_conv[l])
        hs, ss = layer(hs, past, l, rel_bias_table, *lw)
        st_p.append(sp)
        st_s.append(ss)
    y_prompt = rmsnorm(hp, final_g)
    y_sample = rmsnorm(hs, final_g)
    new_k_prompt = jnp.stack([s[0] for s in st_p])
    new_v_prompt = jnp.stack([s[1] for s in st_p])
    new_c_prompt = jnp.stack([s[2] for s in st_p])
    new_n_prompt = jnp.stack([s[3] for s in st_p])
    new_m_prompt = jnp.stack([s[4] for s in st_p])
    new_conv_prompt = jnp.stack([s[5] for s in st_p])
    new_k_sample = jnp.stack([s[0] for s in st_s])
    new_v_sample = jnp.stack([s[1] for s in st_s])
    new_c_sample = jnp.stack([s[2] for s in st_s])
    new_n_sample = jnp.stack([s[3] for s in st_s])
    new_m_sample = jnp.stack([s[4] for s in st_s])
    new_conv_sample = jnp.stack([s[5] for s in st_s])
    new_cmv_sample = jnp.stack([s[6] for s in st_s])
    return (y_prompt, y_sample, new_k_prompt, new_v_prompt, new_c_prompt, new_n_prompt, new_m_prompt,
            new_conv_prompt, new_k_sample, new_v_sample, new_c_sample, new_n_sample, new_m_sample,
            new_conv_sample, new_cmv_sample)
```

```python
import math
import os
from contextlib import ExitStack

import numpy as np
import concourse.bass as bass
import concourse.mybir as mybir
from concourse.bass_utils import run_bass_kernel_spmd

F32 = mybir.dt.float32
BF16 = mybir.dt.bfloat16
U32 = mybir.dt.uint32
AF = mybir.ActivationFunctionType
ALU = mybir.AluOpType
AX = mybir.AxisListType

D = 1024
INW = 2824
EPS = 1e-6
NEGM = -30000.0


class FW:
    def __init__(self, nc, es, ndma=6):
        self.nc = nc
        self.eng = {"pe": nc.tensor, "dve": nc.vector, "act": nc.scalar, "pool": nc.gpsimd, "sp": nc.sync}
        self.sem = {}
        self.cnt = {}
        for k in ["pe", "dve", "act", "pool"]:
            self.sem[k] = es.enter_context(nc.semaphore("s_" + k))
            self.cnt[k] = 0
        self.dq = {}
        for q in ["sp", "pool"]:
            sems = [es.enter_context(nc.semaphore("d_%s%d" % (q, i))) for i in range(ndma)]
            self.dq[q] = {"sems": sems, "n": 0}
        self.waited = {}
        self.lastw = {}
        self.reads = {}
        self.ninst = 0

    def _ev_wait(self, e, ev):
        sid, val, s, prod = ev
        key = (e, sid)
        if self.waited.get(key, 0) >= val:
            return
        self.waited[key] = val
        self.eng[e].wait_ge(s, val)

    def _deps(self, e, reads, writes):
        evs = []
        for r in reads:
            if r in self.lastw:
                evs.append(self.lastw[r])
        for w in writes:
            if w in self.lastw:
                evs.append(self.lastw[w])
            evs.extend(self.reads.get(w, []))
        for ev in evs:
            if ev[3] == e and e == "pe":
                continue
            self._ev_wait(e, ev)

    def _record(self, ev, reads, writes):
        for r in reads:
            lst = self.reads.setdefault(r, [])
            lst[:] = [x for x in lst if x[0] != ev[0]]
            lst.append(ev)
        for w in writes:
            self.lastw[w] = ev
            self.reads[w] = []

    def op(self, e, fn, reads=(), writes=()):
        writes = list(writes) + [r for r in reads if r.startswith("bk")]
        reads = [r for r in reads if not r.startswith("bk")]
        self._deps(e, reads, writes)
        self.cnt[e] += 1
        self.ninst += 1
        ins = fn()
        ins.then_inc(self.sem[e], 1)
        ev = (id(self.sem[e]), self.cnt[e], self.sem[e], e)
        self._record(ev, reads, writes)
        return ev

    def dma(self, q, out, in_, reads=(), writes=(), **kw):
        d = self.dq[q]
        n = d["n"]
        K = len(d["sems"])
        s = d["sems"][n % K]
        prev = 16 * (n // K)
        if prev > 0:
            self._ev_wait(q, (id(s), prev, s, "dma"))
        self._deps(q, reads, writes)
        d["n"] += 1
        self.ninst += 1
        ins = self.eng[q].dma_start(out=out, in_=in_, **kw)
        ins.then_inc(s, 16)
        ev = (id(s), prev + 16, s, "dma")
        self._record(ev, reads, writes)
        return ev

    def all_events(self):
        evs = []
        for k in ["pe", "dve", "act", "pool"]:
            if self.cnt[k] > 0:
                evs.append((id(self.sem[k]), self.cnt[k], self.sem[k], k))
        for q, d in self.dq.items():
            K = len(d["sems"])
            for i, s in enumerate(d["sems"]):
                n_i = (d["n"] - i + K - 1) // K
                if n_i > 0:
                    evs.append((id(s), 16 * n_i, s, "dma"))
        return evs

    def barrier(self):
        evs = self.all_events()
        for e in ["pe", "dve", "act", "pool", "sp"]:
            for ev in evs:
                self._ev_wait(e, ev)
        self.lastw = {}
        self.reads = {}

    def finish(self):
        evs = [ev for ev in self.all_events() if ev[3] == "dma"]
        for ev in evs:
            self._ev_wait("sp", ev)


class FWP:
    SHARED = {"wq", "keysT", "g2T", "io16", "ident_b", "ident_f", "X", "XN2T", "ABG", "kl", "io16i", "nhalf"}

    def __init__(self, fw, suf):
        self.fw, self.suf = fw, suf

    def _r(self, names):
        return [n if (n.startswith("bk") or n in self.SHARED) else n + self.suf for n in names]

    def op(self, e, fn, reads=(), writes=()):
        return self.fw.op(e, fn, reads=self._r(reads), writes=self._r(writes))

    def dma(self, q, out, in_, reads=(), writes=(), **kw):
        return self.fw.dma(q, out, in_, reads=self._r(reads), writes=self._r(writes), **kw)


def rel_bucket_np(rel):
    import jax
    import jax.numpy as jnp
    with jax.default_device(jax.devices("cpu")[0]):
        rel = jnp.asarray(rel, dtype=jnp.int32)
        half = 16
        max_exact = 8
        ret = jnp.where(rel > 0, half, 0)
        n = jnp.abs(rel)
        nf = jnp.maximum(n, 1).astype(jnp.float32)
        large = max_exact + (jnp.log(nf / max_exact) / math.log(128 / max_exact) * (half - max_exact)).astype(jnp.int32)
        large = jnp.minimum(large, half - 1)
        return np.asarray(ret + jnp.where(n < max_exact, n, large))


class _Stop(Exception):
    pass


def build(NTP, do_peer=True, depth=2):
    try:
        return _build(NTP, do_peer, depth)
    except _Stop as e:
        return e.args[0]


def _build(NTP, do_peer=True, depth=2):
    STOP = int(os.environ.get("KSTOP", "0"))
    T = NTP * 128
    NT = NTP + 1
    nc = bass.Bass("TRN2", target_bir_lowering=False)
    es = ExitStack()
    fw = FW(nc, es)

    def stop_here(stage):
        if STOP == stage:
            fw.barrier()
            fw.finish()
            print("STOP at stage", stage, "instructions:", fw.ninst)
            raise _Stop(nc)

    def din(name, shape, dt=F32):
        return nc.dram_tensor(name, list(shape), dt, kind="ExternalInput").ap()

    def dout(name, shape, dt=F32):
        return nc.dram_tensor(name, list(shape), dt, kind="ExternalOutput").ap()

    def dscr(name, shape, dt=F32):
        return nc.dram_tensor(name, list(shape), dt, kind="Internal").ap()

    xp = din("xp", [T, D]); xs = din("xs", [64, D])
    ck = din("ck", [2, 1024, 512]); cv = din("cv", [2, 1024, 512])
    s_c = din("s_c", [2, 4, 64, 64]); s_n = din("s_n", [2, 4, 64]); s_m = din("s_m", [2, 4])
    s_conv = din("s_conv", [2, 3, 256])
    norm1_g = din("norm1_g", [2, D]); w_in = din("w_in", [2, D, INW])
    da_lambda = din("da_lambda", [2, 4, 64]); da_subln_g = din("da_subln_g", [2, 128])
    rel_tab = din("rel_tab", [32, 4])
    ml_conv_w = din("ml_conv_w", [2, 4, 256]); ml_conv_b = din("ml_conv_b", [2, 256])
    ml_wq = din("ml_wq", [2, 4, 64, 64]); ml_wk = din("ml_wk", [2, 4, 64, 64])
    ml_gate_b = din("ml_gate_b", [2, 8]); ml_norm_g = din("ml_norm_g", [2, 256]); ml_skip = din("ml_skip", [2, 256])
    cm_norm_g = din("cm_norm_g", [2, 256]); cm_ws = din("cm_ws", [2, 4, 128, 128]); cm_b = din("cm_b", [2, 4, 128])
    w_out = din("w_out", [2, D, D]); norm2_g = din("norm2_g", [2, D])
    peer_wq = din("peer_wq", [2, D, 2048]); peer_keys = din("peer_keys", [2, 16, 128, 128])
    peer_u = din("peer_u", [2, 16384, D]); peer_v = din("peer_v", [2, 16384, D])
    final_g = din("final_g", [D])
    ohrel = din("ohrel", [32, 383])

    y_p = dout("y_p", [T, D]); y_s = dout("y_s", [64, D])
    nk_p = dout("nk_p", [2, T, 512]); nv_p = dout("nv_p", [2, T, 512])
    nc_p = dout("nc_p", [2, 4, 64, 64]); nn_p = dout("nn_p", [2, 4, 64]); nm_p = dout("nm_p", [2, 4])
    nconv_p = dout("nconv_p", [2, 3, 256])
    nk_s = dout("nk_s", [2, 64, 512]); nv_s = dout("nv_s", [2, 64, 512])
    nc_s = dout("nc_s", [2, 4, 64, 64]); nn_s = dout("nn_s", [2, 4, 64]); nm_s = dout("nm_s", [2, 4])
    nconv_s = dout("nconv_s", [2, 3, 256]); ncmv_s = dout("ncmv_s", [2, 64, 256])

    X = dscr("Xres", [NT * 128, D])
    A_d = dscr("A_d", [4, 383])
    UTs = dscr("UTs", [128, 128, 1024], BF16)
    Vs = dscr("Vs", [128, 128, 1024], BF16)
    XN2T = dscr("XN2T", [128, 8, NT * 128], BF16)
    ABG = dscr("ABG", [128, 3, NT * 128], F32)

    es.enter_context(nc.allow_non_contiguous_dma(reason="small parameter layouts"))
    es.enter_context(nc.allow_low_precision(reason="bf16 matmul operands"))

    V, S, G, PE = nc.vector, nc.scalar, nc.gpsimd, nc.tensor

    banks = [es.enter_context(nc.psum_tensor("bk%d" % i, [128, 512], F32)) for i in range(8)]
    BK = ["bk%d" % i for i in range(8)]

    def sbg(name, shape, dt):
        return es.enter_context(nc.sbuf_tensor(name, list(shape), dt))

    ident_f = sbg("ident_f", [128, 128], F32)
    ident_b = sbg("ident_b", [128, 128], BF16)
    triu_f = sbg("triu_f", [128, 128], F32)
    ones_f = sbg("ones_f", [128, 128], F32)
    biasT = sbg("biasT", [128, 4, 2, 128], F32)
    Jm = sbg("Jm", [128, 128], F32)
    Rv = sbg("Rv", [128, 8, 128], F32)
    fw.op("pool", lambda: G.memset(ident_f[:], 0.0), writes=["ident_f"])
    fw.op("pool", lambda: G.affine_select(out=ident_f[:], in_=ident_f[:], pattern=[[-1, 128]], compare_op=ALU.not_equal,
                                          fill=1.0, base=0, channel_multiplier=1), reads=["ident_f"], writes=["ident_f"])
    fw.op("dve", lambda: V.tensor_copy(out=ident_b[:], in_=ident_f[:]), reads=["ident_f"], writes=["ident_b"])
    fw.op("pool", lambda: G.memset(ones_f[:], 1.0), writes=["ones_f"])
    fw.op("pool", lambda: G.affine_select(out=triu_f[:], in_=ones_f[:], pattern=[[1, 128]], compare_op=ALU.is_ge,
                                          fill=0.0, base=0, channel_multiplier=-1), reads=["ones_f"], writes=["triu_f"])

    stop_here(10)
    with nc.sbuf_tensor("tab", [32, 4], F32) as tab, nc.sbuf_tensor("t15", [32, 4], F32) as t15, \
            nc.sbuf_tensor("ohs", [32, 383], F32) as ohs, nc.sbuf_tensor("Asb", [4, 383], F32) as Asb:
        fw.dma("sp", tab[:], rel_tab[:, :], writes=["tab"])
        fw.dma("sp", t15[:], rel_tab[15:16, :].to_broadcast([32, 4]), writes=["t15"])
        fw.dma("sp", ohs[:], ohrel[:, :], writes=["ohs"])
        fw.op("dve", lambda: V.tensor_tensor(out=tab[:], in0=tab[:], in1=t15[:], op=ALU.subtract), reads=["tab", "t15"], writes=["tab"])
        fw.op("pe", lambda: PE.matmul(banks[0][0:4, 0:383], lhsT=tab[:], rhs=ohs[:], start=True, stop=True),
              reads=["tab", "ohs"], writes=[BK[0]])
        fw.op("dve", lambda: V.tensor_copy(out=Asb[:], in_=banks[0][0:4, 0:383]), reads=[BK[0]], writes=["Asb"])
        fw.dma("sp", A_d[:, :], Asb[:], reads=["Asb"], writes=["A_d"])
        stop_here(11)
        fw.op("pool", lambda: G.memset(Jm[:], 0.0), writes=["Jm"])
        fw.op("pool", lambda: G.affine_select(out=Jm[:], in_=Jm[:], pattern=[[1, 128]], compare_op=ALU.not_equal,
                                              fill=1.0, base=-127, channel_multiplier=1), reads=["Jm"], writes=["Jm"])
        for h in range(4):
            src_prev = bass.AP(tensor=A_d.tensor, offset=h * 383 + 128, ap=[[1, 128], [1, 128]])
            src_diag = bass.AP(tensor=A_d.tensor, offset=h * 383 + 0, ap=[[1, 128], [1, 128]])
            fw.dma("sp", Rv[:, h * 2 + 0, :], src_prev, reads=["A_d"], writes=["Rv"])
            fw.dma("sp", Rv[:, h * 2 + 1, :], src_diag, reads=["A_d"], writes=["Rv"])
        stop_here(12)
        for half in range(2):
            for j in range(4):
                fw.op("pe", lambda: PE.matmul(banks[half][:, j * 128:(j + 1) * 128], lhsT=Jm[:], rhs=Rv[:, half * 4 + j, :], start=True, stop=True),
                      reads=["Jm", "Rv"], writes=[BK[half]])
            fw.op("dve", lambda: V.tensor_copy(out=biasT[:, half * 2:half * 2 + 2, :, :],
                                               in_=banks[half][:].rearrange("p (h t q) -> p h t q", h=2, t=2)), reads=[BK[half]], writes=["biasT"])
        for h in range(4):
            fw.op("pool", lambda: G.memset(biasT[64:128, h, 1, 0:64], NEGM), reads=["biasT"], writes=["biasT"])
        fw.barrier()
    stop_here(1)

    for L in range(depth):
        lam_init = 0.8 - 0.6 * math.exp(-0.3 * L)
        last = (L == depth - 1)
        with ExitStack() as pa:
            def sb(name, shape, dt):
                return pa.enter_context(nc.sbuf_tensor("%s_L%d" % (name, L), list(shape), dt))

            Win = sb("Win", [128, 8, INW], BF16)
            Wout = sb("Wout", [128, 8, D], BF16)
            g1T = sb("g1T", [128, 8], F32)
            convw = sb("convw", [128, 2, 4], F32)
            convb = sb("convb", [128, 2], F32)
            wq_bd = sb("wq_bd", [128, 2, 128], BF16)
            wk_bd = sb("wk_bd", [128, 2, 128], BF16)
            gateb = sb("gateb", [128, 8], F32)
            mlg = sb("mlg", [128, 256], F32)
            mlskip = sb("mlskip", [128, 256], F32)
            cmg = sb("cmg", [128, 256], F32)
            subg = sb("subg", [128, 128], F32)
            cmbT = sb("cmbT", [128, 4], F32)
            wsTm = sb("wsTm", [128, 4, 128], BF16)
            lamb = sb("lamb", [128, 4, 64], F32)
            lam2 = sb("lam2", [128, 2], F32)
            nlam = sb("nlam", [128, 1], F32)
            Cst = sb("Cst", [128, 2, 65], F32)
            Cbf = sb("Cbf", [128, 2, 65], BF16)
            mst = sb("mst", [4, 1], F32)
            mcT = sb("mcT", [128, 2, 131], F32)
            xt = sb("xt", [128, D], F32)
            junk = sb("junk", [128, D], F32)
            ss = sb("ss", [128, 8], F32)
            xn = sb("xn", [128, D], BF16)
            xnT = sb("xnT", [128, 8, 128], BF16)
            QTt = sb("QTt", [128, 4, 128], BF16)
            kvo = sb("kvo", [128, 2, 512], F32)
            mlb = sb("mlb", [128, 512], F32)
            sgo = sb("sgo", [128, 256], F32)
            gts = sb("gts", [128, 8], F32)
            ucm = sb("ucm", [128, 256], F32)
            gvc = sb("gvc", [128, 256], F32)
            vcm = sb("vcm", [128, 256], F32)
            vcb = sb("vcb", [128, 256], BF16)
            mixtok = sb("mixtok", [128, D], BF16)
            mixT = sb("mixT", [128, 8, 128], BF16)
            acc = sb("acc", [128, 2, 128], F32)
            ccT = sb("ccT", [128, 2, 128], F32)
            ccTb = sb("ccTb", [128, 2, 128], BF16)
            qmT = sb("qmT", [128, 2, 128], BF16)
            kmT = sb("kmT", [128, 2, 128], BF16)
            kmt = sb("kmt", [128, 256], BF16)
            cct = sb("cct", [128, 256], F32)
            Ff = sb("Ff", [128, 4], F32)
            wv = sb("wv", [128, 4], F32)
            av = sb("av", [128, 4], F32)
            eFL = sb("eFL", [128, 2], F32)
            VW = sb("VW", [128, 4, 65], BF16)
            Sm = sb("Sm", [128, 4, 128], BF16)
            ndt = sb("ndt", [128, 4, 65], F32)
            hh = sb("hh", [128, 4, 64], F32)
            sm4 = sb("sm4", [128, 8], F32)
            tmpa = sb("tmpa", [128, 256], F32)
            Pb = [sb("Pb%d" % i, [128, 4, 128], BF16) for i in range(2)]
            tmpS = sb("tmpS", [128, 128], F32)
            osb = sb("osb", [128, 128], F32)
            rr = sb("rr", [128, 4], F32)
            x1 = sb("x1", [128, D], F32)
            rowT = sb("rowT", [4, 128], F32)
            m4 = sb("m4", [4, 4], F32)
            m4d = sb("m4d", [4, 4], F32)

            for c0_ in range(0, INW, 1412):
                fw.dma("pool", Win[:, :, c0_:c0_ + 1412], w_in[L][:, c0_:c0_ + 1412].rearrange("(c p) n -> p c n", p=128), writes=["Win"])
            fw.dma("pool", Wout[:], w_out[L].rearrange("(c p) n -> p c n", p=128), writes=["Wout"])
            fw.dma("sp", g1T[:], norm1_g[L].rearrange("(c p) -> p c", p=128), writes=["g1T"])
            for a_ in range(2):
                fw.dma("sp", convw[:, a_, :], ml_conv_w[L][:, a_ * 128:(a_ + 1) * 128].rearrange("i p -> p i"), writes=["convw"])
            fw.dma("sp", convb[:], ml_conv_b[L].rearrange("(hh p) -> p hh", p=128), writes=["convb"])
            fw.op("pool", lambda: G.memset(wq_bd[:], 0.0), writes=["wq_bd"])
            fw.op("pool", lambda: G.memset(wk_bd[:], 0.0), writes=["wk_bd"])
            for h in range(4):
                g_, pb = h // 2, 64 * (h % 2)
                fw.dma("pool", wq_bd[pb:pb + 64, g_, pb:pb + 64], ml_wq[L, h], reads=["wq_bd"], writes=["wq_bd"])
                fw.dma("pool", wk_bd[pb:pb + 64, g_, pb:pb + 64], ml_wk[L, h], reads=["wk_bd"], writes=["wk_bd"])
            fw.dma("sp", gateb[:], ml_gate_b[L:L + 1, :].to_broadcast([128, 8]), writes=["gateb"])
            fw.dma("sp", mlg[:], ml_norm_g[L:L + 1, :].to_broadcast([128, 256]), writes=["mlg"])
            fw.dma("sp", mlskip[:], ml_skip[L:L + 1, :].to_broadcast([128, 256]), writes=["mlskip"])
            fw.dma("sp", cmg[:], cm_norm_g[L:L + 1, :].to_broadcast([128, 256]), writes=["cmg"])
            fw.dma("sp", subg[:], da_subln_g[L:L + 1, :].to_broadcast([128, 128]), writes=["subg"])
            fw.op("dve", lambda: V.tensor_scalar(out=subg[:], in0=subg[:], scalar1=float(1.0 - lam_init), scalar2=None, op0=ALU.mult),
                  reads=["subg"], writes=["subg"])
            fw.dma("sp", cmbT[:], cm_b[L].rearrange("g t -> t g"), writes=["cmbT"])
            fw.dma("sp", junk[:, 0:512], cm_ws[L].rearrange("g t s -> t g s"), writes=["junk"])
            for g_ in range(4):
                fw.op("pe", lambda: PE.transpose(out=banks[0][:, g_ * 128:(g_ + 1) * 128], in_=junk[:, g_ * 128:(g_ + 1) * 128], identity=ident_f[:]),
                      reads=["junk", "ident_f"], writes=[BK[0]])
            fw.op("dve", lambda: V.tensor_tensor(out=wsTm[:], in0=banks[0][:].rearrange("p (g t) -> p g t", g=4),
                                                 in1=triu_f[:].unsqueeze(1).to_broadcast([128, 4, 128]), op=ALU.mult),
                  reads=[BK[0], "triu_f"], writes=["wsTm"])
            fw.dma("sp", lamb[:], da_lambda[L:L + 1].to_broadcast([128, 4, 64]), writes=["lamb"])
            lv = lamb[:].rearrange("p (a b) d -> p a b d", b=2)
            fw.op("dve", lambda: V.tensor_tensor(out=junk[:, 0:128].rearrange("p (a d) -> p a d", a=2), in0=lv[:, :, 0, :], in1=lv[:, :, 1, :], op=ALU.mult),
                  reads=["lamb"], writes=["junk"])
            fw.op("dve", lambda: V.tensor_reduce(out=lam2[:], in_=junk[:, 0:128].rearrange("p (a d) -> p a d", a=2), axis=AX.X, op=ALU.add),
                  reads=["junk"], writes=["lam2"])
            fw.op("act", lambda: S.activation(out=lam2[:], in_=lam2[:], func=AF.Exp), reads=["lam2"], writes=["lam2"])
            fw.op("dve", lambda: V.scalar_tensor_tensor(out=nlam[:], in0=lam2[:, 1:2], scalar=float(-lam_init), in1=lam2[:, 0:1],
                                                        op0=ALU.add, op1=ALU.subtract), reads=["lam2"], writes=["nlam"])
            stop_here(2)

            def sample_cache_setup(KT_s, V1_s):
                ckb = junk[:].bitcast(BF16).rearrange("p (k n) -> p k n", k=4)
                for half in range(2):
                    fw.dma("pool", ckb, ck[L, half * 512:(half + 1) * 512, :].rearrange("(kt p) n -> p kt n", p=128), reads=["junk"], writes=["junk"])
                    for k4 in range(4):
                        kt = half * 4 + k4
                        for h in range(4):
                            fw.op("pe", lambda: PE.transpose(out=banks[0][:].bitcast(BF16)[:, h * 128:(h + 1) * 128], in_=ckb[:, k4, h * 128:(h + 1) * 128],
                                                             identity=ident_b[:]), reads=["junk", "ident_b"], writes=[BK[0]])
                        fw.op("act", lambda: S.copy(out=KT_s[:, :, kt * 128:(kt + 1) * 128],
                                                    in_=banks[0][:].bitcast(BF16)[:, 0:512].rearrange("p (h k) -> p h k", h=4)),
                              reads=[BK[0]], writes=["KT_s"])
                fw.op("pool", lambda: G.memset(V1_s[:], 1.0), writes=["V1_s"])
                for kt in range(8):
                    fw.dma("pool", V1_s[:, kt, :, 0:128], cv[L, kt * 128:(kt + 1) * 128, :].rearrange("p (h v) -> p h v", h=4), reads=["V1_s"], writes=["V1_s"])

            def tile_body(ti, KT_all, V1_all, KT_s, V1_s):
                    samp = (ti == NTP)
                    rows = 64 if samp else 128
                    r0 = ti * 128
                    if L == 0:
                        if samp:
                            fw.op("pool", lambda: G.memset(xt[64:128, :], 0.0), writes=["xt"])
                            fw.dma("sp", xt[0:64, :], xs[:, :], reads=["xt"], writes=["xt"])
                        else:
                            fw.dma("sp", xt[:], xp[r0:r0 + 128, :], writes=["xt"])
                    else:
                        fw.dma("sp", xt[:], X[r0:r0 + 128, :], reads=["X"], writes=["xt"])
                    fw.op("act", lambda: S.activation(out=junk[:], in_=xt[:], func=AF.Square, accum_out=ss[:, 0:1]), reads=["xt"], writes=["junk", "ss"])
                    fw.op("dve", lambda: V.tensor_scalar(out=ss[:, 0:1], in0=ss[:, 0:1], scalar1=1.0 / D, scalar2=EPS, op0=ALU.mult, op1=ALU.add), reads=["ss"], writes=["ss"])
                    fw.op("act", lambda: S.activation(out=ss[:, 0:1], in_=ss[:, 0:1], func=AF.Sqrt), reads=["ss"], writes=["ss"])
                    fw.op("dve", lambda: V.reciprocal(out=ss[:, 0:1], in_=ss[:, 0:1]), reads=["ss"], writes=["ss"])
                    fw.op("dve", lambda: V.tensor_scalar(out=xn[:], in0=xt[:], scalar1=ss[:, 0:1], scalar2=None, op0=ALU.mult), reads=["ss", "xt"], writes=["xn"])
                    b0b = banks[0][:].bitcast(BF16)
                    for c in range(8):
                        fw.op("pe", lambda: PE.transpose(out=b0b[:, c * 128:(c + 1) * 128], in_=xn[:, c * 128:(c + 1) * 128], identity=ident_b[:]),
                              reads=["xn", "ident_b"], writes=[BK[0]])
                    fw.op("dve", lambda: V.tensor_tensor(out=xnT[:], in0=b0b.rearrange("p (c t) -> p c t", c=8),
                                                         in1=g1T[:].unsqueeze(2).to_broadcast([128, 8, 128]), op=ALU.mult),
                          reads=[BK[0], "g1T"], writes=["xnT"])
                    for h in range(4):
                        for c in range(8):
                            fw.op("pe", lambda: PE.matmul(banks[1][:, h * 128:(h + 1) * 128], lhsT=Win[:, c, h * 128:(h + 1) * 128], rhs=xnT[:, c, :],
                                                          start=(c == 0), stop=(c == 7)), reads=["Win", "xnT"], writes=[BK[1]])
                    for h in range(4):
                        for c in range(8):
                            fw.op("pe", lambda: PE.matmul(banks[2][:, h * 128:(h + 1) * 128], lhsT=Win[:, c, 512 + h * 128:512 + (h + 1) * 128], rhs=xnT[:, c, :],
                                                          start=(c == 0), stop=(c == 7)), reads=["Win", "xnT"], writes=[BK[2]])
                    for hh_ in range(2):
                        for c in range(8):
                            fw.op("pe", lambda: PE.matmul(banks[5][:, hh_ * 128:(hh_ + 1) * 128], lhsT=Win[:, c, 1536 + hh_ * 128:1536 + (hh_ + 1) * 128], rhs=xnT[:, c, :],
                                                          start=(c == 0), stop=(c == 7)), reads=["Win", "xnT"], writes=[BK[5]])
                    fw.op("act", lambda: S.copy(out=QTt[:], in_=banks[1][:].rearrange("p (h t) -> p h t", h=4)), reads=[BK[1]], writes=["QTt"])
                    KTd = KT_s[:, :, 1024:1152] if samp else KT_all[:, :, r0:r0 + 128]
                    ktn = "KT_s" if samp else "KT_all"
                    fw.op("act", lambda: S.copy(out=KTd, in_=banks[2][:].rearrange("p (h t) -> p h t", h=4)), reads=[BK[2]], writes=[ktn])
                    if ti == 0:
                        fw.op("pool", lambda: G.memset(mcT[:, :, 0:3], 0.0), writes=["mcT"])
                    elif samp:
                        for a_ in range(2):
                            fw.dma("sp", mcT[:, a_, 0:3], s_conv[L][:, a_ * 128:(a_ + 1) * 128].rearrange("i p -> p i"), reads=["mcT"], writes=["mcT"])
                    else:
                        fw.op("dve", lambda: V.tensor_copy(out=mcT[:, :, 0:3], in_=mcT[:, :, 128:131]), reads=["mcT"], writes=["mcT"])
                    fw.op("dve", lambda: V.tensor_copy(out=mcT[:, :, 3:131], in_=banks[5][:, 0:256].rearrange("p (a t) -> p a t", a=2)),
                          reads=[BK[5], "mcT"], writes=["mcT"])
                    stop_here(30)
                    def tokproj(bk, c0, n):
                        for c in range(8):
                            fw.op("pe", lambda: PE.matmul(banks[bk][:, 0:n], lhsT=xnT[:, c, :], rhs=Win[:, c, c0:c0 + n], start=(c == 0), stop=(c == 7)),
                                  reads=["Win", "xnT"], writes=[BK[bk]])
                    tokproj(3, 512, 512)
                    fw.op("dve", lambda: V.tensor_copy(out=kvo[:, 0, :], in_=banks[3][:]), reads=[BK[3]], writes=["kvo0"])
                    if samp:
                        fw.dma("sp", nk_s[L], kvo[0:64, 0, :], reads=["kvo0"], writes=["o_nk"])
                    else:
                        fw.dma("sp", nk_p[L, r0:r0 + 128, :], kvo[:, 0, :], reads=["kvo0"], writes=["o_nk"])
                    stop_here(301)
                    tokproj(4, 1024, 512)
                    fw.op("dve", lambda: V.tensor_copy(out=kvo[:, 1, :], in_=banks[4][:]), reads=[BK[4]], writes=["kvo1"])
                    V1d = V1_s[:, 8, :, 0:128] if samp else V1_all[:, ti, :, 0:128]
                    v1n = "V1_s" if samp else "V1_all"
                    fw.op("act", lambda: S.copy(out=V1d, in_=banks[4][:].rearrange("p (h v) -> p h v", h=4)), reads=[BK[4]], writes=[v1n])
                    if samp:
                        fw.dma("sp", nv_s[L], kvo[0:64, 1, :], reads=["kvo1"], writes=["o_nv"])
                    else:
                        fw.dma("sp", nv_p[L, r0:r0 + 128, :], kvo[:, 1, :], reads=["kvo1"], writes=["o_nv"])
                    stop_here(302)
                    tokproj(3, 1536, 512)
                    fw.op("dve", lambda: V.tensor_copy(out=mlb[:], in_=banks[3][:]), reads=[BK[3]], writes=["mlb"])
                    if samp:
                        fw.dma("sp", nconv_s[L], mlb[61:64, 0:256], reads=["mlb"], writes=["o_conv"])
                    elif ti == NTP - 1:
                        fw.dma("sp", nconv_p[L], mlb[125:128, 0:256], reads=["mlb"], writes=["o_conv"])
                    stop_here(303)
                    tokproj(4, 2048, 264)
                    fw.op("act", lambda: S.activation(out=sgo[:], in_=banks[4][:, 0:256], func=AF.Sigmoid), reads=[BK[4]], writes=["sgo"])
                    fw.op("dve", lambda: V.tensor_tensor(out=gts[:], in0=banks[4][:, 256:264], in1=gateb[:], op=ALU.add), reads=[BK[4], "gateb"], writes=["gts"])
                    stop_here(304)
                    tokproj(3, 2312, 512)
                    fw.op("act", lambda: S.activation(out=ucm[:], in_=banks[3][:, 0:256], func=AF.Gelu), reads=[BK[3]], writes=["ucm"])
                    fw.op("act", lambda: S.activation(out=gvc[:], in_=banks[3][:, 256:512], func=AF.Gelu), reads=[BK[3]], writes=["gvc"])
                    stop_here(31)
                    fw.op("dve", lambda: V.tensor_tensor(out=tmpa[:], in0=gvc[:], in1=gvc[:], op=ALU.mult), reads=["gvc"], writes=["tmpa"])
                    fw.op("dve", lambda: V.tensor_reduce(out=ss[:, 1:2], in_=tmpa[:], axis=AX.X, op=ALU.add), reads=["tmpa"], writes=["ss1"])
                    fw.op("dve", lambda: V.tensor_scalar(out=ss[:, 1:2], in0=ss[:, 1:2], scalar1=1.0 / 256, scalar2=EPS, op0=ALU.mult, op1=ALU.add), reads=["ss1"], writes=["ss1"])
                    fw.op("act", lambda: S.activation(out=ss[:, 1:2], in_=ss[:, 1:2], func=AF.Sqrt), reads=["ss1"], writes=["ss1"])
                    fw.op("dve", lambda: V.reciprocal(out=ss[:, 1:2], in_=ss[:, 1:2]), reads=["ss1"], writes=["ss1"])
                    fw.op("dve", lambda: V.scalar_tensor_tensor(out=vcm[:], in0=gvc[:], scalar=ss[:, 1:2], in1=cmg[:], op0=ALU.mult, op1=ALU.mult),
                          reads=["gvc", "ss1", "cmg"], writes=["vcm"])
                    fw.op("dve", lambda: V.tensor_copy(out=vcb[:], in_=vcm[:]), reads=["vcm"], writes=["vcb"])
                    if samp:
                        fw.dma("sp", ncmv_s[L], vcm[0:64, :], reads=["vcm"], writes=["o_cmv"])
                    for g_ in range(4):
                        fw.op("pe", lambda: PE.matmul(banks[5][:, g_ * 64:(g_ + 1) * 64], lhsT=wsTm[:, g_, :], rhs=vcb[:, g_ * 64:(g_ + 1) * 64], start=True, stop=True),
                              reads=["wsTm", "vcb"], writes=[BK[5]])
                    fw.op("dve", lambda: V.tensor_tensor(out=tmpa[:].rearrange("p (g d) -> p g d", g=4), in0=banks[5][:, 0:256].rearrange("p (g d) -> p g d", g=4),
                                                         in1=cmbT[:].unsqueeze(2).to_broadcast([128, 4, 64]), op=ALU.add), reads=[BK[5], "cmbT"], writes=["tmpa"])
                    fw.op("dve", lambda: V.tensor_tensor(out=mixtok[:, 768:1024], in0=tmpa[:], in1=ucm[:], op=ALU.mult), reads=["tmpa", "ucm"], writes=["mixtok"])
                    stop_here(32)
                    fw.op("dve", lambda: V.tensor_tensor(out=acc[:], in0=mcT[:, :, 0:128], in1=convw[:, :, 0:1].to_broadcast([128, 2, 128]), op=ALU.mult),
                          reads=["mcT", "convw"], writes=["acc"])
                    for i_ in range(1, 4):
                        fw.op("dve", lambda: V.tensor_tensor(out=ccT[:], in0=mcT[:, :, i_:i_ + 128], in1=convw[:, :, i_:i_ + 1].to_broadcast([128, 2, 128]), op=ALU.mult),
                              reads=["mcT", "convw"], writes=["ccT"])
                        fw.op("dve", lambda: V.tensor_tensor(out=acc[:], in0=acc[:], in1=ccT[:], op=ALU.add), reads=["acc", "ccT"], writes=["acc"])
                    for a_ in range(2):
                        fw.op("act", lambda: S.activation(out=ccT[:, a_, :], in_=acc[:, a_, :], func=AF.Silu, bias=convb[:, a_:a_ + 1]), reads=["acc", "convb"], writes=["ccT"])
                    fw.op("dve", lambda: V.tensor_copy(out=ccTb[:], in_=ccT[:]), reads=["ccT"], writes=["ccTb"])
                    for g_ in range(2):
                        fw.op("pe", lambda: PE.matmul(banks[5][:, g_ * 128:(g_ + 1) * 128], lhsT=wq_bd[:, g_, :], rhs=ccTb[:, g_, :], start=True, stop=True),
                              reads=["wq_bd", "ccTb"], writes=[BK[5]])
                        fw.op("pe", lambda: PE.matmul(banks[5][:, 256 + g_ * 128:256 + (g_ + 1) * 128], lhsT=wk_bd[:, g_, :], rhs=ccTb[:, g_, :], start=True, stop=True),
                              reads=["wk_bd", "ccTb"], writes=[BK[5]])
                        fw.op("pe", lambda: PE.matmul(banks[3][:, g_ * 128:(g_ + 1) * 128], lhsT=ccTb[:, g_, :], rhs=wk_bd[:, g_, :], start=True, stop=True),
                              reads=["wk_bd", "ccTb"], writes=[BK[3]])
                        fw.op("pe", lambda: PE.transpose(out=banks[3][:, 256 + g_ * 128:256 + (g_ + 1) * 128], in_=ccT[:, g_, :], identity=ident_f[:]),
                              reads=["ccT", "ident_f"], writes=[BK[3]])
                    fw.op("act", lambda: S.copy(out=qmT[:], in_=banks[5][:, 0:256].rearrange("p (g t) -> p g t", g=2)), reads=[BK[5]], writes=["qmT"])
                    fw.op("act", lambda: S.mul(out=kmT[:], in_=banks[5][:, 256:512].rearrange("p (g t) -> p g t", g=2), mul=0.125), reads=[BK[5]], writes=["kmT"])
                    fw.op("act", lambda: S.mul(out=kmt[:], in_=banks[3][:, 0:256], mul=0.125), reads=[BK[3]], writes=["kmt"])
                    fw.op("dve", lambda: V.tensor_copy(out=cct[:], in_=banks[3][:, 256:512]), reads=[BK[3]], writes=["cct"])
                    stop_here(33)
                    fw.op("act", lambda: S.activation(out=sm4[:, 0:4], in_=gts[:, 4:8], func=AF.Exp, scale=-1.0), reads=["gts"], writes=["sm4"])
                    fw.op("act", lambda: S.activation(out=sm4[:, 0:4], in_=sm4[:, 0:4], func=AF.Ln, bias=1.0), reads=["sm4"], writes=["sm4"])
                    fw.op("dve", lambda: V.tensor_scalar(out=sm4[:, 0:4], in0=sm4[:, 0:4], scalar1=-1.0, scalar2=None, op0=ALU.mult), reads=["sm4"], writes=["sm4"])
                    fw.op("pe", lambda: PE.matmul(banks[4][:, 0:4], lhsT=triu_f[:], rhs=sm4[:, 0:4], start=True, stop=True), reads=["triu_f", "sm4"], writes=[BK[4]])
                    fw.op("pe", lambda: PE.matmul(banks[4][:, 8:12], lhsT=ones_f[0:rows, :], rhs=sm4[0:rows, 0:4], start=True, stop=True), reads=["ones_f", "sm4"], writes=[BK[4]])
                    fw.op("dve", lambda: V.tensor_copy(out=Ff[:], in_=banks[4][:, 0:4]), reads=[BK[4]], writes=["Ff"])
                    fw.op("act", lambda: S.activation(out=av[:], in_=Ff[:], func=AF.Exp), reads=["Ff"], writes=["av"])
                    fw.op("dve", lambda: V.tensor_tensor(out=sm4[:, 4:8], in0=gts[:, 0:4], in1=Ff[:], op=ALU.subtract), reads=["gts", "Ff", "sm4"], writes=["sm4"])
                    fw.op("act", lambda: S.activation(out=wv[:], in_=sm4[:, 4:8], func=AF.Exp), reads=["sm4"], writes=["wv"])
                    flv = banks[4][:, 8:12].rearrange("p (g a) -> p g a", a=2)
                    fw.op("act", lambda: S.activation(out=eFL[0:64, :], in_=flv[0:64, :, 0], func=AF.Exp), reads=[BK[4]], writes=["eFL"])
                    fw.op("act", lambda: S.activation(out=eFL[64:128, :], in_=flv[64:128, :, 1], func=AF.Exp), reads=[BK[4]], writes=["eFL"])
                    fw.op("pe", lambda: PE.transpose(out=banks[4][0:4, 128:256], in_=sm4[:, 4:8], identity=ident_f[:]), reads=["sm4", "ident_f"], writes=[BK[4]])
                    fw.op("dve", lambda: V.tensor_copy(out=rowT[:], in_=banks[4][0:4, 128:256]), reads=[BK[4]], writes=["rowT"])
                    fw.op("dve", lambda: V.tensor_reduce(out=m4[:, 0:1], in_=rowT[:, 0:rows], axis=AX.X, op=ALU.max), reads=["rowT"], writes=["m4"])
                    fw.op("pe", lambda: PE.matmul(banks[4][0:4, 16:17], lhsT=sm4[0:rows, 0:4], rhs=ones_f[0:rows, 0:1], start=True, stop=True), reads=["sm4", "ones_f"], writes=[BK[4]])
                    if ti == 0:
                        fw.op("pool", lambda: G.memset(mst[:], 0.0), reads=["mst"], writes=["mst"])
                        fw.op("pool", lambda: G.memset(Cst[:], 0.0), reads=["Cst"], writes=["Cst"])
                        fw.op("pool", lambda: G.memset(Cbf[:], 0.0), reads=["Cbf"], writes=["Cbf"])
                    elif samp:
                        fw.dma("sp", mst[:], s_m[L].rearrange("(h o) -> h o", o=1), reads=["mst"], writes=["mst"])
                        for h in range(4):
                            g_, pb = h // 2, 64 * (h % 2)
                            fw.dma("sp", Cst[pb:pb + 64, g_, 0:64], s_c[L, h], reads=["Cst"], writes=["Cst"])
                            fw.dma("sp", Cst[pb:pb + 64, g_, 64:65], s_n[L, h].rearrange("(e o) -> e o", o=1), reads=["Cst"], writes=["Cst"])
                        bcast_heads(fw, nc, mst, m4d, banks, BK, ones_f, ident_f, tmpa, scale=1.0)
                        fw.op("dve", lambda: V.tensor_tensor(out=Cst[:], in0=Cst[:], in1=tmpa[:, 0:2].unsqueeze(2).to_broadcast([128, 2, 65]), op=ALU.mult),
                              reads=["Cst", "tmpa"], writes=["Cst"])
                        fw.op("dve", lambda: V.tensor_copy(out=Cbf[:], in_=Cst[:]), reads=["Cst"], writes=["Cbf"])
                    fw.op("dve", lambda: V.tensor_tensor(out=mst[:], in0=mst[:], in1=m4[:, 0:1], op=ALU.max), reads=["mst", "m4"], writes=["mst"])
                    fw.op("dve", lambda: V.tensor_tensor(out=mst[:], in0=mst[:], in1=banks[4][0:4, 16:17], op=ALU.add), reads=["mst", BK[4]], writes=["mst"])
                    stop_here(34)
                    fw.op("dve", lambda: V.tensor_tensor(out=VW[:, :, 0:64], in0=mlb[:, 256:512].rearrange("p (h v) -> p h v", h=4),
                                                         in1=wv[:].unsqueeze(2).to_broadcast([128, 4, 64]), op=ALU.mult), reads=["mlb", "wv"], writes=["VW"])
                    fw.op("dve", lambda: V.tensor_copy(out=VW[:, :, 64:65], in_=wv[:].unsqueeze(2)), reads=["wv", "VW"], writes=["VW"])
                    if samp:
                        fw.op("pool", lambda: G.memset(VW[64:128, :, :], 0.0), reads=["VW"], writes=["VW"])
                    for h in (0, 2, 1, 3):
                        g_, pb = h // 2, 64 * (h % 2)
                        bk_ = 6 if h % 2 == 0 else 3
                        fw.op("pe", lambda: PE.matmul(banks[bk_][:, g_ * 128:(g_ + 1) * 128], lhsT=kmT[pb:pb + 64, g_, :], rhs=qmT[pb:pb + 64, g_, :], start=True, stop=True),
                              reads=["kmT", "qmT"], writes=[BK[bk_]])
                    Smv = Sm[:].rearrange("p (g a) t -> p g a t", a=2)
                    for a_, bk_ in ((0, 6), (1, 3)):
                        fw.op("dve", lambda: V.tensor_tensor(out=Smv[:, :, a_, :], in0=banks[bk_][:, 0:256].rearrange("p (g t) -> p g t", g=2),
                                                             in1=triu_f[:].unsqueeze(1).to_broadcast([128, 2, 128]), op=ALU.mult), reads=[BK[bk_], "triu_f"], writes=["Sm"])
                    for h in range(4):
                        g_, pb = h // 2, 64 * (h % 2)
                        fw.op("pe", lambda: PE.matmul(banks[7][:, h * 65:(h + 1) * 65], lhsT=Sm[:, h, :], rhs=VW[:, h, :], start=True, stop=False),
                              reads=["Sm", "VW"], writes=[BK[7]])
                        fw.op("pe", lambda: PE.matmul(banks[7][:, h * 65:(h + 1) * 65], lhsT=qmT[pb:pb + 64, g_, :], rhs=Cbf[pb:pb + 64, g_, :], start=False, stop=True),
                              reads=["qmT", "Cbf"], writes=[BK[7]])
                        fw.op("pe", lambda: PE.matmul(banks[4][pb:pb + 64, 256 + g_ * 65:256 + (g_ + 1) * 65], lhsT=kmt[:, h * 64:(h + 1) * 64], rhs=VW[:, h, :], start=True, stop=True),
                              reads=["kmt", "VW"], writes=[BK[4]])
                    fw.op("dve", lambda: V.tensor_tensor(out=ndt[:], in0=banks[7][:, 0:260].rearrange("p (h v) -> p h v", h=4),
                                                         in1=av[:].unsqueeze(2).to_broadcast([128, 4, 65]), op=ALU.mult), reads=[BK[7], "av"], writes=["ndt"])
                    fw.op("dve", lambda: V.tensor_tensor(out=Cst[:], in0=Cst[:], in1=banks[4][:, 256:386].rearrange("p (g v) -> p g v", g=2), op=ALU.add),
                          reads=["Cst", BK[4]], writes=["Cst"])
                    fw.op("dve", lambda: V.tensor_tensor(out=Cst[:], in0=Cst[:], in1=eFL[:].unsqueeze(2).to_broadcast([128, 2, 65]), op=ALU.mult),
                          reads=["Cst", "eFL"], writes=["Cst"])
                    fw.op("dve", lambda: V.tensor_copy(out=Cbf[:], in_=Cst[:]), reads=["Cst"], writes=["Cbf"])
                    stop_here(35)
                    fw.op("dve", lambda: V.scalar_tensor_tensor(out=rr[:], in0=ndt[:, :, 64], scalar=-1.0, in1=ndt[:, :, 64], op0=ALU.mult, op1=ALU.max), reads=["ndt"], writes=["rr"])
                    fw.op("dve", lambda: V.tensor_scalar(out=rr[:], in0=rr[:], scalar1=1.0, scalar2=None, op0=ALU.max), reads=["rr"], writes=["rr"])
                    fw.op("dve", lambda: V.reciprocal(out=rr[:], in_=rr[:]), reads=["rr"], writes=["rr"])
                    fw.op("dve", lambda: V.tensor_tensor(out=hh[:], in0=ndt[:, :, 0:64], in1=rr[:].unsqueeze(2).to_broadcast([128, 4, 64]), op=ALU.mult), reads=["ndt", "rr"], writes=["hh"])
                    fw.op("dve", lambda: V.tensor_tensor(out=tmpa[:].rearrange("p (h v) -> p h v", h=4), in0=hh[:], in1=hh[:], op=ALU.mult), reads=["hh"], writes=["tmpa"])
                    fw.op("dve", lambda: V.tensor_reduce(out=rr[:], in_=tmpa[:].rearrange("p (h v) -> p h v", h=4), axis=AX.X, op=ALU.add), reads=["tmpa"], writes=["rr"])
                    fw.op("dve", lambda: V.tensor_scalar(out=rr[:], in0=rr[:], scalar1=1.0 / 64, scalar2=EPS, op0=ALU.mult, op1=ALU.add), reads=["rr"], writes=["rr"])
                    fw.op("act", lambda: S.activation(out=rr[:], in_=rr[:], func=AF.Sqrt), reads=["rr"], writes=["rr"])
                    fw.op("dve", lambda: V.reciprocal(out=rr[:], in_=rr[:]), reads=["rr"], writes=["rr"])
                    fw.op("dve", lambda: V.tensor_tensor(out=hh[:], in0=hh[:], in1=rr[:].unsqueeze(2).to_broadcast([128, 4, 64]), op=ALU.mult), reads=["hh", "rr"], writes=["hh"])
                    fw.op("dve", lambda: V.tensor_tensor(out=tmpa[:], in0=hh[:].rearrange("p h v -> p (h v)"), in1=mlg[:], op=ALU.mult), reads=["hh", "mlg"], writes=["tmpa"])
                    fw.op("dve", lambda: V.tensor_tensor(out=cct[:], in0=cct[:], in1=mlskip[:], op=ALU.mult), reads=["cct", "mlskip"], writes=["cct"])
                    fw.op("dve", lambda: V.tensor_tensor(out=tmpa[:], in0=tmpa[:], in1=cct[:], op=ALU.add), reads=["tmpa", "cct"], writes=["tmpa"])
                    fw.op("dve", lambda: V.tensor_tensor(out=mixtok[:, 512:768], in0=tmpa[:], in1=sgo[:], op=ALU.mult), reads=["tmpa", "sgo"], writes=["mixtok"])
                    if samp:
                        emit_state(fw, nc, L, Cst, mst, m4d, nc_s, nn_s, nm_s, banks, BK, ones_f, ident_f, tmpa)
                    elif ti == NTP - 1:
                        emit_state(fw, nc, L, Cst, mst, m4d, nc_p, nn_p, nm_p, banks, BK, ones_f, ident_f, tmpa)
                    stop_here(36)
                    KTb = KT_s if samp else KT_all
                    V1b = V1_s if samp else V1_all
                    nk = 9 if samp else ti + 1
                    pcount = 0
                    for h in range(4):
                        for c in range(2):
                            pb = 64 * c
                            Ob = banks[1 + c]
                            On = BK[1 + c]
                            kts = list(range(nk))
                            far, near = kts[:-2], kts[-2:]
                            first = True
                            for g0 in range(0, len(far), 4):
                                grp = far[g0:g0 + 4]
                                sbk = 6 + (pcount % 2)
                                pbuf = Pb[pcount % 2]
                                pbn = "Pb%d" % (pcount % 2)
                                pcount += 1
                                for j, kt in enumerate(grp):
                                    fw.op("pe", lambda: PE.matmul(banks[sbk][:, j * 128:(j + 1) * 128], lhsT=KTb[pb:pb + 64, h, kt * 128:(kt + 1) * 128],
                                                                  rhs=QTt[pb:pb + 64, h, :], start=True, stop=True), reads=[ktn, "QTt"], writes=[BK[sbk]])
                                n_ = len(grp)
                                fw.op("act", lambda: S.activation(out=pbuf[:, 0:n_, :], in_=banks[sbk][:, 0:n_ * 128].rearrange("p (j q) -> p j q", j=n_),
                                                                  func=AF.Exp, scale=0.125), reads=[BK[sbk]], writes=[pbn])
                                for j, kt in enumerate(grp):
                                    fw.op("pe", lambda: PE.matmul(Ob[:, 0:129], lhsT=pbuf[:, j, :], rhs=V1b[:, kt, h, 0:129], start=first, stop=False),
                                          reads=[pbn, v1n], writes=[On])
                                    first = False
                            for kt in near:
                                typ = 1 if kt == kts[-1] else 0
                                sbk = 6 + (pcount % 2)
                                pbuf = Pb[pcount % 2]
                                pbn = "Pb%d" % (pcount % 2)
                                pcount += 1
                                fw.op("pe", lambda: PE.matmul(banks[sbk][:, 0:128], lhsT=KTb[pb:pb + 64, h, kt * 128:(kt + 1) * 128],
                                                              rhs=QTt[pb:pb + 64, h, :], start=True, stop=True), reads=[ktn, "QTt"], writes=[BK[sbk]])
                                fw.op("dve", lambda: V.scalar_tensor_tensor(out=tmpS[:], in0=banks[sbk][:, 0:128], scalar=0.125, in1=biasT[:, h, typ, :],
                                                                            op0=ALU.mult, op1=ALU.add), reads=[BK[sbk], "biasT"], writes=["tmpS"])
                                fw.op("act", lambda: S.activation(out=pbuf[:, 0, :], in_=tmpS[:], func=AF.Exp), reads=["tmpS"], writes=[pbn])
                                fw.op("pe", lambda: PE.matmul(Ob[:, 0:129], lhsT=pbuf[:, 0, :], rhs=V1b[:, kt, h, 0:129], start=first, stop=(kt == kts[-1])),
                                      reads=[pbn, v1n], writes=[On])
                                first = False
                        fw.op("dve", lambda: V.reciprocal(out=rr[:, 0:1], in_=banks[1][:, 128:129]), reads=[BK[1]], writes=["rr"])
                        fw.op("dve", lambda: V.reciprocal(out=rr[:, 1:2], in_=banks[2][:, 128:129]), reads=[BK[2], "rr"], writes=["rr"])
                        fw.op("dve", lambda: V.tensor_tensor(out=rr[:, 1:2], in0=rr[:, 1:2], in1=nlam[:], op=ALU.mult), reads=["rr", "nlam"], writes=["rr"])
                        fw.op("dve", lambda: V.tensor_scalar(out=osb[:], in0=banks[1][:, 0:128], scalar1=rr[:, 0:1], scalar2=None, op0=ALU.mult), reads=[BK[1], "rr"], writes=["osb"])
                        fw.op("dve", lambda: V.scalar_tensor_tensor(out=osb[:], in0=banks[2][:, 0:128], scalar=rr[:, 1:2], in1=osb[:], op0=ALU.mult, op1=ALU.add),
                              reads=[BK[2], "rr", "osb"], writes=["osb"])
                        fw.op("act", lambda: S.activation(out=tmpS[:], in_=osb[:], func=AF.Square, accum_out=rr[:, 2:3]), reads=["osb"], writes=["tmpS", "rr"])
                        fw.op("dve", lambda: V.tensor_scalar(out=rr[:, 2:3], in0=rr[:, 2:3], scalar1=1.0 / 128, scalar2=EPS, op0=ALU.mult, op1=ALU.add), reads=["rr"], writes=["rr"])
                        fw.op("act", lambda: S.activation(out=rr[:, 2:3], in_=rr[:, 2:3], func=AF.Sqrt), reads=["rr"], writes=["rr"])
                        fw.op("dve", lambda: V.reciprocal(out=rr[:, 2:3], in_=rr[:, 2:3]), reads=["rr"], writes=["rr"])
                        fw.op("dve", lambda: V.scalar_tensor_tensor(out=mixtok[:, h * 128:(h + 1) * 128], in0=osb[:], scalar=rr[:, 2:3], in1=subg[:], op0=ALU.mult, op1=ALU.mult),
                              reads=["osb", "rr", "subg"], writes=["mixtok"])
                    stop_here(37)
                    for c in range(8):
                        fw.op("pe", lambda: PE.transpose(out=b0b[:, c * 128:(c + 1) * 128], in_=mixtok[:, c * 128:(c + 1) * 128], identity=ident_b[:]),
                              reads=["mixtok", "ident_b"], writes=[BK[0]])
                    fw.op("act", lambda: S.copy(out=mixT[:], in_=b0b.rearrange("p (c t) -> p c t", c=8)), reads=[BK[0]], writes=["mixT"])
                    for hf in range(2):
                        for c in range(8):
                            fw.op("pe", lambda: PE.matmul(banks[3 + hf][:], lhsT=mixT[:, c, :], rhs=Wout[:, c, hf * 512:(hf + 1) * 512], start=(c == 0), stop=(c == 7)),
                                  reads=["mixT", "Wout"], writes=[BK[3 + hf]])
                        fw.op("dve", lambda: V.tensor_tensor(out=x1[:, hf * 512:(hf + 1) * 512], in0=banks[3 + hf][:], in1=xt[:, hf * 512:(hf + 1) * 512], op=ALU.add),
                              reads=[BK[3 + hf], "xt"], writes=["x1"])
                    fw.dma("sp", X[r0:r0 + 128, :], x1[:], reads=["x1"], writes=["X"])
            with ExitStack() as scs:
                KT_s = scs.enter_context(nc.sbuf_tensor("KT_s%d" % L, [128, 4, 9 * 128], BF16))
                V1_s = scs.enter_context(nc.sbuf_tensor("V1_s%d" % L, [128, 9, 4, 130], BF16))
                sample_cache_setup(KT_s, V1_s)
                stop_here(3)
                tile_body(NTP, None, None, KT_s, V1_s)
                fw.barrier()
                stop_here(4)
            with ExitStack() as scp:
                KT_all = scp.enter_context(nc.sbuf_tensor("KT_all%d" % L, [128, 4, T], BF16))
                V1_all = scp.enter_context(nc.sbuf_tensor("V1_all%d" % L, [128, NTP, 4, 130], BF16))
                fw.op("pool", lambda: G.memset(V1_all[:], 1.0), writes=["V1_all"])
                for ti in range(NTP):
                    tile_body(ti, KT_all, V1_all, None, None)
            fw.barrier()

        peer_phase(fw, nc, es, L, NT, NTP, last, do_peer, X, UTs, banks, BK, ident_f, ident_b,
                   norm2_g, peer_wq, peer_keys, peer_u, peer_v, final_g, y_p, y_s, Vs, XN2T, ABG, stop_here)
        fw.barrier()

    fw.finish()
    print("instructions:", fw.ninst)
    return nc


def bcast_heads(fw, nc, col4, m4, banks, BK, ones_f, ident_f, tmpa, scale):
    V, S, PE = nc.vector, nc.scalar, nc.tensor
    fw.op("dve", lambda: V.tensor_scalar(out=m4[:, :], in0=ident_f[0:4, 0:4], scalar1=col4[:, 0:1], scalar2=None, op0=ALU.mult),
          reads=["ident_f", "mst", "m4d"], writes=["m4d"])
    fw.op("pe", lambda: PE.matmul(banks[4][:, 32:36], lhsT=ones_f[0:4, :], rhs=m4[:, :], start=True, stop=True), reads=["ones_f", "m4d"], writes=[BK[4]])
    v = banks[4][:, 32:36].rearrange("p (g a) -> p g a", a=2)
    fw.op("act", lambda: S.activation(out=tmpa[0:64, 0:2], in_=v[0:64, :, 0], func=AF.Exp, scale=scale), reads=[BK[4], "tmpa"], writes=["tmpa"])
    fw.op("act", lambda: S.activation(out=tmpa[64:128, 0:2], in_=v[64:128, :, 1], func=AF.Exp, scale=scale), reads=[BK[4], "tmpa"], writes=["tmpa"])


def emit_state(fw, nc, L, Cst, mst, m4, o_c, o_n, o_m, banks, BK, ones_f, ident_f, tmpa):
    V = nc.vector
    bcast_heads(fw, nc, mst, m4, banks, BK, ones_f, ident_f, tmpa, scale=-1.0)
    cs = tmpa[:, 4:134].rearrange("p (g v) -> p g v", g=2)
    fw.op("dve", lambda: V.tensor_tensor(out=cs, in0=Cst[:], in1=tmpa[:, 0:2].unsqueeze(2).to_broadcast([128, 2, 65]), op=ALU.mult),
          reads=["Cst", "tmpa"], writes=["tmpa"])
    for h in range(4):
        g_, pb = h // 2, 64 * (h % 2)
        fw.dma("sp", o_c[L, h], cs[pb:pb + 64, g_, 0:64], reads=["tmpa"], writes=["o_c"])
        fw.dma("sp", o_n[L, h].rearrange("(e o) -> e o", o=1), cs[pb:pb + 64, g_, 64:65], reads=["tmpa"], writes=["o_n"])
    fw.dma("sp", o_m[L].rearrange("(h o) -> h o", o=1), mst[:], reads=["mst"], writes=["o_m"])


def peer_phase(fw, nc, es, L, NT, NTP, last, do_peer, X, UTs, banks, BK, ident_f, ident_b,
               norm2_g, peer_wq, peer_keys, peer_u, peer_v, final_g, y_p, y_s, Vs, XN2T, ABG, stop_here):
    V, S, G, PE = nc.vector, nc.scalar, nc.gpsimd, nc.tensor
    I32 = mybir.dt.int32
    TT = NT * 128
    if not do_peer:
        with ExitStack() as pp:
            def sb(name, shape, dt):
                return pp.enter_context(nc.sbuf_tensor("%s_P%d" % (name, L), list(shape), dt))
            x1 = [sb("px1_%d" % i, [128, D], F32) for i in range(2)]
            junk = sb("pjunk", [128, D], F32)
            ss = sb("pss", [128, 16], F32)
            fg = sb("fg", [128, D], F32)
            if last:
                fw.dma("sp", fg[:], final_g.rearrange("(o n) -> o n", o=1).to_broadcast([128, D]), writes=["fg"])
                for ti in range(NT):
                    xx = x1[ti % 2]
                    xn_ = "px1_%d" % (ti % 2)
                    fw.dma("sp", xx[:], X[ti * 128:(ti + 1) * 128, :], reads=["X"], writes=[xn_])
                    final_norm(fw, nc, xx, xn_, junk, ss, fg, ti, NTP, y_p, y_s)
        return

    u_v = peer_u[L].rearrange("(i1 i2) d -> i2 i1 d", i2=128)
    v_v = peer_v[L].rearrange("(i1 i2) d -> i2 i1 d", i2=128)
    with ExitStack() as pp:
        def sb(name, shape, dt):
            return pp.enter_context(nc.sbuf_tensor("%s_Q%d" % (name, L), list(shape), dt))
        ul = [sb("ul%d" % i, [128, D], BF16) for i in range(2)]
        us = [sb("us%d" % i, [128, D], BF16) for i in range(2)]
        for i2 in range(128):
            b_ = i2 % 2
            fw.dma("pool", Vs[i2], v_v[i2], writes=["Vs%d" % i2])
            fw.dma("pool", ul[b_][:], u_v[i2], writes=["ul%d" % b_])
            bkb = banks[b_][:].bitcast(BF16)
            for c in range(8):
                fw.op("pe", lambda: PE.transpose(out=bkb[:, c * 128:(c + 1) * 128], in_=ul[b_][:, c * 128:(c + 1) * 128], identity=ident_b[:]),
                      reads=["ul%d" % b_, "ident_b"], writes=[BK[b_]])
            if b_ == 0:
                fw.op("act", lambda: S.copy(out=us[b_][:], in_=bkb), reads=[BK[b_]], writes=["us%d" % b_])
            else:
                fw.op("dve", lambda: V.tensor_copy(out=us[b_][:], in_=bkb), reads=[BK[b_]], writes=["us%d" % b_])
            fw.dma("sp", UTs[i2], us[b_][:], reads=["us%d" % b_], writes=["UTs%d" % i2])
        fw.barrier()
    stop_here(50)

    with ExitStack() as pp:
        def sb(name, shape, dt):
            return pp.enter_context(nc.sbuf_tensor("%s_R%d" % (name, L), list(shape), dt))
        wq = sb("wq", [128, 8, 2048], BF16)
        g2T = sb("g2T", [128, 8], F32)
        keysT = sb("keysT", [128, 16, 128], BF16)
        kl = sb("kl", [128, 16, 128], BF16)
        io16i = sb("io16i", [128, 16], I32)
        io16 = sb("io16", [128, 16], F32)
        x1t_2 = [sb("x1t_%d" % i_, [128, D], F32) for i_ in range(2)]
        junk_2 = [sb("junk_%d" % i_, [128, D], F32) for i_ in range(2)]
        ss_2 = [sb("ss_%d" % i_, [128, 16], F32) for i_ in range(2)]
        rq_2 = [sb("rq_%d" % i_, [128, 8], F32) for i_ in range(2)]
        xh_2 = [sb("xh_%d" % i_, [128, D], BF16) for i_ in range(2)]
        xT_2 = [sb("xT_%d" % i_, [128, 8, 128], BF16) for i_ in range(2)]
        qn_2 = [sb("qn_%d" % i_, [128, 2048], BF16) for i_ in range(2)]
        qnT_2 = [sb("qnT_%d" % i_, [128, 16, 128], BF16) for i_ in range(2)]
        scr_2 = [sb("scr_%d" % i_, [128, 2048], F32) for i_ in range(2)]
        s2_2 = [sb("s2_%d" % i_, [128, 2048], F32) for i_ in range(2)]
        m16_2 = [sb("m16_%d" % i_, [128, 16, 16], F32) for i_ in range(2)]
        i16_2 = [sb("i16_%d" % i_, [128, 16, 16], U32) for i_ in range(2)]
        i16f_2 = [sb("i16f_%d" % i_, [128, 16, 16], F32) for i_ in range(2)]
        vs_2 = [sb("vs_%d" % i_, [128, 8, 16], F32) for i_ in range(2)]
        ci_2 = [sb("ci_%d" % i_, [128, 8, 16], U32) for i_ in range(2)]
        hi_2 = [sb("hi_%d" % i_, [128, 8, 16], U32) for i_ in range(2)]
        lo_2 = [sb("lo_%d" % i_, [128, 8, 16], U32) for i_ in range(2)]
        hif_2 = [sb("hif_%d" % i_, [128, 8, 16], F32) for i_ in range(2)]
        lof_2 = [sb("lof_%d" % i_, [128, 8, 16], F32) for i_ in range(2)]
        abg_2 = [sb("abg_%d" % i_, [128, 3, 128], F32) for i_ in range(2)]
        abgT_2 = [sb("abgT_%d" % i_, [128, 3, 128], F32) for i_ in range(2)]
        Zs_2 = [sb("Zs_%d" % i_, [128, 8], F32) for i_ in range(2)]
        for hf in range(2):
            fw.dma("pool", wq[:, :, hf * 1024:(hf + 1) * 1024], peer_wq[L][:, hf * 1024:(hf + 1) * 1024].rearrange("(c p) n -> p c n", p=128), writes=["wq"])
        fw.dma("sp", g2T[:], norm2_g[L].rearrange("(c p) -> p c", p=128), writes=["g2T"])
        fw.dma("pool", kl[:], peer_keys[L].rearrange("hc k d -> k hc d"), writes=["kl"])
        for half in range(2):
            bkb = banks[half][:].bitcast(BF16)
            for j in range(8):
                hc = half * 8 + j
                fw.op("pe", lambda: PE.transpose(out=bkb[:, j * 128:(j + 1) * 128], in_=kl[:, hc, :], identity=ident_b[:]), reads=["kl", "ident_b"], writes=[BK[half]])
            fw.op("act", lambda: S.copy(out=keysT[:, half * 8:(half + 1) * 8, :], in_=bkb.rearrange("p (j k) -> p j k", j=8)), reads=[BK[half]], writes=["keysT"])
        fw.op("pool", lambda: G.iota(out=io16i[:], pattern=[[1, 16]], base=0, channel_multiplier=0), writes=["io16i"])
        fw.op("dve", lambda: V.tensor_copy(out=io16[:], in_=io16i[:]), reads=["io16i"], writes=["io16"])

        def front_a(ti):
            r0 = ti * 128
            pr_ = ti % 2
            x1t = x1t_2[pr_]
            junk = junk_2[pr_]
            ss = ss_2[pr_]
            rq = rq_2[pr_]
            xh = xh_2[pr_]
            xT = xT_2[pr_]
            qn = qn_2[pr_]
            qnT = qnT_2[pr_]
            scr = scr_2[pr_]
            s2 = s2_2[pr_]
            m16 = m16_2[pr_]
            i16 = i16_2[pr_]
            i16f = i16f_2[pr_]
            vs = vs_2[pr_]
            ci = ci_2[pr_]
            hi = hi_2[pr_]
            lo = lo_2[pr_]
            hif = hif_2[pr_]
            lof = lof_2[pr_]
            abg = abg_2[pr_]
            abgT = abgT_2[pr_]
            Zs = Zs_2[pr_]
            fwp = FWP(fw, "_p%d" % pr_)
            fwp.dma("sp", x1t[:], X[r0:r0 + 128, :], reads=["X"], writes=["x1t"])
            fwp.op("act", lambda: S.activation(out=junk[:], in_=x1t[:], func=AF.Square, accum_out=ss[:, 0:1]), reads=["x1t"], writes=["junk", "ss"])
            fwp.op("dve", lambda: V.tensor_scalar(out=ss[:, 0:1], in0=ss[:, 0:1], scalar1=1.0 / D, scalar2=EPS, op0=ALU.mult, op1=ALU.add), reads=["ss"], writes=["ss"])
            fwp.op("act", lambda: S.activation(out=ss[:, 0:1], in_=ss[:, 0:1], func=AF.Sqrt), reads=["ss"], writes=["ss"])
            fwp.op("dve", lambda: V.reciprocal(out=ss[:, 0:1], in_=ss[:, 0:1]), reads=["ss"], writes=["ss"])
            fwp.op("act", lambda: S.activation(out=xh[:], in_=x1t[:], func=AF.Copy, scale=ss[:, 0:1]), reads=["ss", "x1t"], writes=["xh"])
            b0b = banks[0][:].bitcast(BF16)
            for c in range(8):
                fwp.op("pe", lambda: PE.transpose(out=b0b[:, c * 128:(c + 1) * 128], in_=xh[:, c * 128:(c + 1) * 128], identity=ident_b[:]),
                       reads=["xh", "ident_b"], writes=[BK[0]])
            for c in range(8):
                fwp.op("act", lambda: S.activation(out=xT[:, c, :], in_=b0b[:, c * 128:(c + 1) * 128], func=AF.Copy, scale=g2T[:, c:c + 1]),
                       reads=[BK[0], "g2T"], writes=["xT"])
            fwp.dma("sp", XN2T[:, :, r0:r0 + 128], xT[:], reads=["xT"], writes=["XN2T"])
            for blk in range(4):
                for c in range(8):
                    fwp.op("pe", lambda: PE.matmul(banks[1 + blk][:], lhsT=xT[:, c, :], rhs=wq[:, c, blk * 512:(blk + 1) * 512], start=(c == 0), stop=(c == 7)),
                           reads=["xT", "wq"], writes=[BK[1 + blk]])
            for h in range(8):
                fwp.op("act", lambda: S.activation(out=junk[:, 0:256], in_=banks[1 + h // 2][:, (h % 2) * 256:(h % 2 + 1) * 256], func=AF.Square, accum_out=rq[:, h:h + 1]),
                       reads=[BK[1 + h // 2]], writes=["junk", "rq%d" % h])

        def front_b(ti):
            r0 = ti * 128
            pr_ = ti % 2
            x1t = x1t_2[pr_]
            junk = junk_2[pr_]
            ss = ss_2[pr_]
            rq = rq_2[pr_]
            xh = xh_2[pr_]
            xT = xT_2[pr_]
            qn = qn_2[pr_]
            qnT = qnT_2[pr_]
            scr = scr_2[pr_]
            s2 = s2_2[pr_]
            m16 = m16_2[pr_]
            i16 = i16_2[pr_]
            i16f = i16f_2[pr_]
            vs = vs_2[pr_]
            ci = ci_2[pr_]
            hi = hi_2[pr_]
            lo = lo_2[pr_]
            hif = hif_2[pr_]
            lof = lof_2[pr_]
            abg = abg_2[pr_]
            abgT = abgT_2[pr_]
            Zs = Zs_2[pr_]
            fwp = FWP(fw, "_p%d" % pr_)
            rqr = ["rq%d" % h for h in range(8)]
            fwp.op("dve", lambda: V.tensor_scalar(out=rq[:], in0=rq[:], scalar1=1.0 / 256, scalar2=EPS, op0=ALU.mult, op1=ALU.add), reads=rqr, writes=["rq"] + rqr)
            fwp.op("act", lambda: S.activation(out=rq[:], in_=rq[:], func=AF.Sqrt), reads=["rq"], writes=["rq"])
            fwp.op("dve", lambda: V.reciprocal(out=rq[:], in_=rq[:]), reads=["rq"], writes=["rq"])
            for h in range(8):
                fwp.op("act", lambda: S.activation(out=qn[:, h * 256:(h + 1) * 256], in_=banks[1 + h // 2][:, (h % 2) * 256:(h % 2 + 1) * 256], func=AF.Copy, scale=rq[:, h:h + 1]),
                       reads=[BK[1 + h // 2], "rq"], writes=["qn"])
            for half in range(2):
                bkb = banks[5 + half][:].bitcast(BF16)
                for j in range(8):
                    hc = half * 8 + j
                    fwp.op("pe", lambda: PE.transpose(out=bkb[:, j * 128:(j + 1) * 128], in_=qn[:, hc * 128:(hc + 1) * 128], identity=ident_b[:]),
                           reads=["qn", "ident_b"], writes=[BK[5 + half]])
                fwp.op("act", lambda: S.copy(out=qnT[:, half * 8:(half + 1) * 8, :], in_=bkb.rearrange("p (j t) -> p j t", j=8)), reads=[BK[5 + half]], writes=["qnT"])
            for hc in range(16):
                fwp.op("pe", lambda: PE.matmul(banks[1 + hc // 4][:, (hc % 4) * 128:(hc % 4 + 1) * 128], lhsT=qnT[:, hc, :], rhs=keysT[:, hc, :], start=True, stop=True),
                       reads=["qnT", "keysT"], writes=[BK[1 + hc // 4]])
            for b_ in range(4):
                fwp.op("act", lambda: S.copy(out=scr[:, b_ * 512:(b_ + 1) * 512], in_=banks[1 + b_][:]), reads=[BK[1 + b_]], writes=["scr"])

        def back_a(ti):
            r0 = ti * 128
            pr_ = ti % 2
            x1t = x1t_2[pr_]
            junk = junk_2[pr_]
            ss = ss_2[pr_]
            rq = rq_2[pr_]
            xh = xh_2[pr_]
            xT = xT_2[pr_]
            qn = qn_2[pr_]
            qnT = qnT_2[pr_]
            scr = scr_2[pr_]
            s2 = s2_2[pr_]
            m16 = m16_2[pr_]
            i16 = i16_2[pr_]
            i16f = i16f_2[pr_]
            vs = vs_2[pr_]
            ci = ci_2[pr_]
            hi = hi_2[pr_]
            lo = lo_2[pr_]
            hif = hif_2[pr_]
            lof = lof_2[pr_]
            abg = abg_2[pr_]
            abgT = abgT_2[pr_]
            Zs = Zs_2[pr_]
            fwp = FWP(fw, "_p%d" % pr_)

            def top16_batch(items, n, tag, src_res):
                K_ = len(items)
                s2v = s2[:].rearrange("p (k n) -> p k n", k=K_)
                for k, (src2d, mv, iv) in enumerate(items):
                    fwp.op("dve", lambda: V.max(out=mv[:, 0:8], in_=src2d), reads=[src_res], writes=["%sm%d" % (tag, k)])
                for k, (src2d, mv, iv) in enumerate(items):
                    fwp.op("dve", lambda: V.match_replace(out=s2v[:, k, :], in_to_replace=mv[:, 0:8], in_values=src2d, imm_value=-1e30),
                          reads=[src_res, "%sm%d" % (tag, k)], writes=["s2_%d" % k])
                for k, (src2d, mv, iv) in enumerate(items):
                    fwp.op("dve", lambda: V.max(out=mv[:, 8:16], in_=s2v[:, k, :]), reads=["s2_%d" % k], writes=["%sn%d" % (tag, k)])
                for k, (src2d, mv, iv) in enumerate(items):
                    fwp.op("dve", lambda: V.max_index(out=iv[:, 0:8], in_max=mv[:, 0:8], in_values=src2d), reads=[src_res, "%sm%d" % (tag, k)], writes=["%si%d" % (tag, k)])
                for k, (src2d, mv, iv) in enumerate(items):
                    fwp.op("dve", lambda: V.max_index(out=iv[:, 8:16], in_max=mv[:, 8:16], in_values=src2d), reads=[src_res, "%sn%d" % (tag, k)], writes=["%sj%d" % (tag, k)])
                return (["%sm%d" % (tag, k) for k in range(K_)] + ["%sn%d" % (tag, k) for k in range(K_)],
                        ["%si%d" % (tag, k) for k in range(K_)] + ["%sj%d" % (tag, k) for k in range(K_)])

            mres1, ires1 = top16_batch([(scr[:, hc * 128:(hc + 1) * 128], m16[:, hc, :], i16[:, hc, :]) for hc in range(16)], 128, "a", "scr")
            fwp.op("dve", lambda: V.tensor_copy(out=i16f[:], in_=i16[:]), reads=ires1, writes=["i16f"])
            m16v = m16[:].rearrange("p (h c) k -> p h c k", c=2)
            i16v = i16f[:].rearrange("p (h c) k -> p h c k", c=2)
            cand = scr[:].rearrange("p (h a b) -> p h a b", h=8, a=16)
            fwp.op("dve", lambda: V.tensor_tensor(out=cand, in0=m16v[:, :, 0, :].unsqueeze(3).to_broadcast([128, 8, 16, 16]),
                                                 in1=m16v[:, :, 1, :].unsqueeze(2).to_broadcast([128, 8, 16, 16]), op=ALU.add),
                  reads=mres1 + ["scr"], writes=["scr"])
            return mres1, ires1

        def back_b(ti):
            r0 = ti * 128
            pr_ = ti % 2
            x1t = x1t_2[pr_]
            junk = junk_2[pr_]
            ss = ss_2[pr_]
            rq = rq_2[pr_]
            xh = xh_2[pr_]
            xT = xT_2[pr_]
            qn = qn_2[pr_]
            qnT = qnT_2[pr_]
            scr = scr_2[pr_]
            s2 = s2_2[pr_]
            m16 = m16_2[pr_]
            i16 = i16_2[pr_]
            i16f = i16f_2[pr_]
            vs = vs_2[pr_]
            ci = ci_2[pr_]
            hi = hi_2[pr_]
            lo = lo_2[pr_]
            hif = hif_2[pr_]
            lof = lof_2[pr_]
            abg = abg_2[pr_]
            abgT = abgT_2[pr_]
            Zs = Zs_2[pr_]
            fwp = FWP(fw, "_p%d" % pr_)
            i16v = i16f[:].rearrange("p (h c) k -> p h c k", c=2)

            def top16_batch(items, n, tag, src_res):
                K_ = len(items)
                s2v = s2[:].rearrange("p (k n) -> p k n", k=K_)
                for k, (src2d, mv, iv) in enumerate(items):
                    fwp.op("dve", lambda: V.max(out=mv[:, 0:8], in_=src2d), reads=[src_res], writes=["%sm%d" % (tag, k)])
                for k, (src2d, mv, iv) in enumerate(items):
                    fwp.op("dve", lambda: V.match_replace(out=s2v[:, k, :], in_to_replace=mv[:, 0:8], in_values=src2d, imm_value=-1e30),
                          reads=[src_res, "%sm%d" % (tag, k)], writes=["s2_%d" % k])
                for k, (src2d, mv, iv) in enumerate(items):
                    fwp.op("dve", lambda: V.max(out=mv[:, 8:16], in_=s2v[:, k, :]), reads=["s2_%d" % k], writes=["%sn%d" % (tag, k)])
                for k, (src2d, mv, iv) in enumerate(items):
                    fwp.op("dve", lambda: V.max_index(out=iv[:, 0:8], in_max=mv[:, 0:8], in_values=src2d), reads=[src_res, "%sm%d" % (tag, k)], writes=["%si%d" % (tag, k)])
                for k, (src2d, mv, iv) in enumerate(items):
                    fwp.op("dve", lambda: V.max_index(out=iv[:, 8:16], in_max=mv[:, 8:16], in_values=src2d), reads=[src_res, "%sn%d" % (tag, k)], writes=["%sj%d" % (tag, k)])
                return (["%sm%d" % (tag, k) for k in range(K_)] + ["%sn%d" % (tag, k) for k in range(K_)],
                        ["%si%d" % (tag, k) for k in range(K_)] + ["%sj%d" % (tag, k) for k in range(K_)])

            mres2, ires2 = top16_batch([(scr[:, h * 256:(h + 1) * 256], vs[:, h, :], ci[:, h, :]) for h in range(8)], 256, "b", "scr")
            gt_ = abg[:, 2, :].rearrange("p (h k) -> p h k", h=8)
            fwp.op("dve", lambda: V.tensor_tensor(out=gt_, in0=vs[:], in1=vs[:, :, 0:1].to_broadcast([128, 8, 16]), op=ALU.subtract), reads=mres2, writes=["abg"])
            fwp.op("act", lambda: S.activation(out=abg[:, 2, :], in_=abg[:, 2, :], func=AF.Exp), reads=["abg"], writes=["abg"])
            fwp.op("dve", lambda: V.tensor_reduce(out=Zs[:], in_=gt_, axis=AX.X, op=ALU.add), reads=["abg"], writes=["Zs"])
            fwp.op("dve", lambda: V.reciprocal(out=Zs[:], in_=Zs[:]), reads=["Zs"], writes=["Zs"])
            fwp.op("dve", lambda: V.tensor_tensor(out=gt_, in0=gt_, in1=Zs[:].unsqueeze(2).to_broadcast([128, 8, 16]), op=ALU.mult), reads=["abg", "Zs"], writes=["abg"])
            fwp.op("dve", lambda: V.tensor_single_scalar(out=hi[:], in_=ci[:], scalar=4, op=ALU.logical_shift_right), reads=ires2, writes=["hi"])
            fwp.op("dve", lambda: V.tensor_single_scalar(out=lo[:], in_=ci[:], scalar=15, op=ALU.bitwise_and), reads=ires2, writes=["lo"])
            fwp.op("dve", lambda: V.tensor_copy(out=hif[:], in_=hi[:]), reads=["hi"], writes=["hif"])
            fwp.op("dve", lambda: V.tensor_copy(out=lof[:], in_=lo[:]), reads=["lo"], writes=["lof"])
            eq = scr[:].rearrange("p (h k a) -> p h k a", h=8, k=16)
            io_b = io16[:].unsqueeze(1).unsqueeze(1).to_broadcast([128, 8, 16, 16])
            for which, srcf in ((0, hif), (1, lof)):
                fwp.op("dve", lambda: V.tensor_tensor(out=eq, in0=srcf[:].unsqueeze(3).to_broadcast([128, 8, 16, 16]), in1=io_b, op=ALU.is_equal),
                      reads=["hif", "lof", "io16", "scr"], writes=["scr"])
                fwp.op("dve", lambda: V.tensor_tensor(out=eq, in0=eq, in1=i16v[:, :, which, :].unsqueeze(2).to_broadcast([128, 8, 16, 16]), op=ALU.mult),
                      reads=["scr", "i16f"], writes=["scr"])
                fwp.op("dve", lambda: V.tensor_reduce(out=abg[:, which, :].rearrange("p (h k) -> p h k", h=8), in_=eq, axis=AX.X, op=ALU.add), reads=["scr"], writes=["abg"])
            for w_ in range(3):
                fwp.op("pe", lambda: PE.transpose(out=banks[7][:, w_ * 128:(w_ + 1) * 128], in_=abg[:, w_, :], identity=ident_f[:]), reads=["abg", "ident_f"], writes=[BK[7]])
            fwp.op("dve", lambda: V.tensor_copy(out=abgT[:], in_=banks[7][:, 0:384].rearrange("p (w t) -> p w t", w=3)), reads=[BK[7]], writes=["abgT"])
            fwp.dma("sp", ABG[:, :, r0:r0 + 128], abgT[:], reads=["abgT"], writes=["ABG"])

        front_a(0)
        front_b(0)
        for ti in range(NT):
            if ti + 1 < NT:
                front_a(ti + 1)
            back_a(ti)
            if ti + 1 < NT:
                front_b(ti + 1)
            back_b(ti)
        fw.barrier()
    stop_here(51)

    NTG = 3
    with ExitStack() as pp:
        def sb(name, shape, dt):
            return pp.enter_context(nc.sbuf_tensor("%s_S%d" % (name, L), list(shape), dt))
        Gt = sb("Gt", [128, NTG, 128, 128], BF16)
        W1_2 = [sb("W1_%d" % i_, [128, 32, 128], BF16) for i_ in range(2)]
        W2_2 = [sb("W2_%d" % i_, [128, 32, 128], BF16) for i_ in range(2)]
        wblk = 0
        xg = sb("xg", [128, 8, NTG * 128], BF16)
        ab = sb("ab", [128, 3, NTG * 128], F32)
        x1g = [sb("x1g%d" % j, [128, D], F32) for j in range(2)]
        ut = [sb("ut%d" % i, [128, D], BF16) for i in range(3)]
        vc = [sb("vc%d" % i, [128, D], BF16) for i in range(3)]
        gl = [sb("gl%d" % i, [128, NTG * 128], F32) for i in range(2)]
        Ab = [sb("Ab%d" % i, [128, NTG * 128], BF16) for i in range(2)]
        io128i = sb("io128i", [128, 128], I32)
        io128 = sb("io128", [128, 128], F32)
        junk = sb("pjunk", [128, D], F32)
        ss = sb("pss", [128, 16], F32)
        fg = sb("fg", [128, D], F32)
        fw.op("pool", lambda: G.iota(out=io128i[:], pattern=[[1, 128]], base=0, channel_multiplier=0), writes=["io128i"])
        fw.op("dve", lambda: V.tensor_copy(out=io128[:], in_=io128i[:]), reads=["io128i"], writes=["io128"])
        if last:
            fw.dma("sp", fg[:], final_g.rearrange("(o n) -> o n", o=1).to_broadcast([128, D]), writes=["fg"])
        gcount = 0
        hcount = 0
        for j0 in range(0, NT, NTG):
            n = min(NTG, NT - j0)
            N = n * 128
            c0 = j0 * 128
            fw.dma("sp", xg[:, :, 0:N], XN2T[:, :, c0:c0 + N], reads=["XN2T"], writes=["xg"])
            fw.dma("sp", ab[:, :, 0:N], ABG[:, :, c0:c0 + N], reads=["ABG"], writes=["ab"])
            for j in range(n):
                for qq in range(4):
                    t0 = j * 128 + qq * 32
                    wp_ = wblk % 2
                    wblk += 1
                    W1, W2 = W1_2[wp_], W2_2[wp_]
                    w1n, w2n = "W1_%d" % wp_, "W2_%d" % wp_
                    io_b = io128[:].unsqueeze(1).to_broadcast([128, 32, 128])
                    fw.op("dve", lambda: V.tensor_tensor(out=W1[:], in0=ab[:, 0, t0:t0 + 32].unsqueeze(2).to_broadcast([128, 32, 128]), in1=io_b, op=ALU.is_equal),
                          reads=["ab", "io128"], writes=[w1n])
                    fw.op("dve", lambda: V.tensor_tensor(out=W1[:], in0=W1[:], in1=ab[:, 2, t0:t0 + 32].unsqueeze(2).to_broadcast([128, 32, 128]), op=ALU.mult),
                          reads=["ab", w1n], writes=[w1n])
                    fw.op("dve", lambda: V.tensor_tensor(out=W2[:], in0=ab[:, 1, t0:t0 + 32].unsqueeze(2).to_broadcast([128, 32, 128]), in1=io_b, op=ALU.is_equal),
                          reads=["ab", "io128"], writes=[w2n])
                    for q4 in range(8):
                        bk_ = 6 + (gcount % 2)
                        gcount += 1
                        for u_ in range(4):
                            tt = q4 * 4 + u_
                            fw.op("pe", lambda: PE.matmul(banks[bk_][:, u_ * 128:(u_ + 1) * 128], lhsT=W1[:, tt, :], rhs=W2[:, tt, :], start=True, stop=True),
                                  reads=[w1n, w2n], writes=[BK[bk_]])
                        dst = Gt[:, j, qq * 32 + q4 * 4:qq * 32 + q4 * 4 + 4, :]
                        fw.op("act", lambda: S.copy(out=dst, in_=banks[bk_][:].rearrange("p (u i) -> p u i", u=4)), reads=[BK[bk_]], writes=["Gt"])
            for i2 in range(128):
                b_ = i2 % 2
                d_ = i2 % 3
                fw.dma("sp", ut[d_][:], UTs[i2], reads=["UTs%d" % i2], writes=["ut%d" % d_])
                fw.dma("sp", vc[d_][:], Vs[i2], reads=["Vs%d" % i2], writes=["vc%d" % d_])
                hb = 6 + (hcount % 2)
                hcount += 1
                for c in range(8):
                    fw.op("pe", lambda: PE.matmul(banks[hb][:, 0:N], lhsT=ut[d_][:, c * 128:(c + 1) * 128], rhs=xg[:, c, 0:N], start=(c == 0), stop=(c == 7)),
                          reads=["ut%d" % d_, "xg"], writes=[BK[hb]])
                fw.op("act", lambda: S.activation(out=gl[b_][:, 0:N], in_=banks[hb][:, 0:N], func=AF.Gelu), reads=[BK[hb]], writes=["gl%d" % b_])
                fw.op("dve", lambda: V.tensor_tensor(out=Ab[b_][:, 0:N].rearrange("p (j t) -> p j t", j=n), in0=gl[b_][:, 0:N].rearrange("p (j t) -> p j t", j=n),
                                                     in1=Gt[:, 0:n, :, i2], op=ALU.mult), reads=["gl%d" % b_, "Gt"], writes=["Ab%d" % b_])
                for j in range(n):
                    for hf in range(2):
                        fw.op("pe", lambda: PE.matmul(banks[2 * j + hf][:], lhsT=Ab[b_][:, j * 128:(j + 1) * 128], rhs=vc[d_][:, hf * 512:(hf + 1) * 512],
                                                      start=(i2 == 0), stop=(i2 == 127)), reads=["Ab%d" % b_, "vc%d" % d_], writes=[BK[2 * j + hf]])
            for j in range(n):
                ti = j0 + j
                xb = x1g[ti % 2]
                xbn = "x1g%d" % (ti % 2)
                fw.dma("sp", xb[:], X[ti * 128:(ti + 1) * 128, :], reads=["X"], writes=[xbn])
                for hf in range(2):
                    fw.op("dve", lambda: V.tensor_tensor(out=xb[:, hf * 512:(hf + 1) * 512], in0=banks[2 * j + hf][:], in1=xb[:, hf * 512:(hf + 1) * 512], op=ALU.add),
                          reads=[BK[2 * j + hf], xbn], writes=[xbn])
                if last:
                    final_norm(fw, nc, xb, xbn, junk, ss, fg, ti, NTP, y_p, y_s)
                else:
                    fw.dma("sp", X[ti * 128:(ti + 1) * 128, :], xb[:], reads=[xbn], writes=["X"])
        fw.barrier()


def final_norm(fw, nc, xx, xn_, junk, ss, fg, ti, NTP, y_p, y_s):
    V, S = nc.vector, nc.scalar
    fw.op("act", lambda: S.activation(out=junk[:], in_=xx[:], func=AF.Square, accum_out=ss[:, 0:1]), reads=[xn_], writes=["pjunk", "pss"])
    fw.op("dve", lambda: V.tensor_scalar(out=ss[:, 0:1], in0=ss[:, 0:1], scalar1=1.0 / D, scalar2=EPS, op0=ALU.mult, op1=ALU.add), reads=["pss"], writes=["pss"])
    fw.op("act", lambda: S.activation(out=ss[:, 0:1], in_=ss[:, 0:1], func=AF.Sqrt), reads=["pss"], writes=["pss"])
    fw.op("dve", lambda: V.reciprocal(out=ss[:, 0:1], in_=ss[:, 0:1]), reads=["pss"], writes=["pss"])
    fw.op("dve", lambda: V.scalar_tensor_tensor(out=junk[:], in0=xx[:], scalar=ss[:, 0:1], in1=fg[:], op0=ALU.mult, op1=ALU.mult),
          reads=[xn_, "pss", "fg"], writes=["pjunk"])
    if ti == NTP:
        fw.dma("sp", y_s[:, :], junk[0:64, :], reads=["pjunk"], writes=["o_y"])
    else:
        fw.dma("sp", y_p[ti * 128:(ti + 1) * 128, :], junk[:], reads=["pjunk"], writes=["o_y"])


_CACHE = {}


def make_in_maps(inp, n_cores, nb_prompt):
    f = lambda a: np.ascontiguousarray(np.asarray(a, dtype=np.float32))
    rel = 127 - np.arange(383)
    bk = rel_bucket_np(rel)
    oh = np.zeros((32, 383), np.float32)
    oh[bk, np.arange(383)] = 1.0
    shared = {
        "norm1_g": f(inp["norm1_g"]), "w_in": f(inp["w_in"]), "da_lambda": f(inp["da_lambda"]),
        "da_subln_g": f(inp["da_subln_g"]), "rel_tab": f(inp["rel_bias_table"]),
        "ml_conv_w": f(inp["ml_conv_w"]), "ml_conv_b": f(inp["ml_conv_b"]), "ml_wq": f(inp["ml_wq"]), "ml_wk": f(inp["ml_wk"]),
        "ml_gate_b": f(inp["ml_gate_b"]).reshape(2, 8), "ml_norm_g": f(inp["ml_norm_g"]), "ml_skip": f(inp["ml_skip"]),
        "cm_norm_g": f(inp["cm_norm_g"]), "cm_ws": f(inp["cm_ws"]), "cm_b": f(inp["cm_b"]),
        "w_out": f(inp["w_out"]), "norm2_g": f(inp["norm2_g"]), "peer_wq": f(inp["peer_wq"]),
        "peer_keys": f(inp["peer_keys"]).reshape(2, 16, 128, 128), "peer_u": f(inp["peer_u"]), "peer_v": f(inp["peer_v"]),
        "final_g": f(inp["final_g"]), "ohrel": oh,
    }
    maps = []
    for c in range(n_cores):
        m = dict(shared)
        m["xp"] = f(inp["x_prompt"][c % nb_prompt])
        m["xs"] = f(inp["x_sample"][c])
        m["ck"] = f(inp["cache_k"][:, c]).reshape(2, 1024, 512)
        m["cv"] = f(inp["cache_v"][:, c]).reshape(2, 1024, 512)
        m["s_c"] = f(inp["state_mlstm_c"][:, c]); m["s_n"] = f(inp["state_mlstm_n"][:, c])
        m["s_m"] = f(inp["state_mlstm_m"][:, c]); m["s_conv"] = f(inp["state_mlstm_conv"][:, c])
        maps.append(m)
    return maps


def run(inp, n_cores=8, do_peer=True):
    xpr = np.asarray(inp["x_prompt"])
    B, SEQ = xpr.shape[0], xpr.shape[1]
    NTP = SEQ // 128
    key = (NTP, do_peer)
    if key not in _CACHE:
        _CACHE[key] = build(NTP, do_peer=do_peer)
    nc = _CACHE[key]
    maps = make_in_maps(inp, n_cores, B)
    res = run_bass_kernel_spmd(nc, maps, core_ids=list(range(n_cores)), trace=bool(os.environ.get("KTRACE")))
    if os.environ.get("KTRACE"):
        print("EXEC_TIME_NS", res.exec_time_ns)
    R = res.results
    DB = n_cores
    st = lambda name, cores: np.stack([np.asarray(R[c][name]) for c in cores], axis=0)
    pc = list(range(B))
    sc = list(range(DB))
    y_prompt = st("y_p", pc)
    y_sample = st("y_s", sc)
    mv = lambda a: np.ascontiguousarray(np.moveaxis(a, 0, 1))
    out = (
        y_prompt, y_sample,
        mv(st("nk_p", pc)).reshape(2, B, SEQ, 4, 128), mv(st("nv_p", pc)).reshape(2, B, SEQ, 4, 128),
        mv(st("nc_p", pc)), mv(st("nn_p", pc)), mv(st("nm_p", pc)), mv(st("nconv_p", pc)),
        mv(st("nk_s", sc)).reshape(2, DB, 64, 4, 128), mv(st("nv_s", sc)).reshape(2, DB, 64, 4, 128),
        mv(st("nc_s", sc)), mv(st("nn_s", sc)), mv(st("nm_s", sc)), mv(st("nconv_s", sc)), mv(st("ncmv_s", sc)),
    )
    return tuple(np.asarray(o, dtype=np.float32) for o in out)


def kernel(**inputs):
    return run(inputs, n_cores=8, do_peer=True)
```

```python
import math
import os
from contextlib import ExitStack

import numpy as np
import concourse.bass as bass
import concourse.mybir as mybir
from concourse.bass_utils import run_bass_kernel_spmd

F32 = mybir.dt.float32
BF16 = mybir.dt.bfloat16
U32 = mybir.dt.uint32
AF = mybir.ActivationFunctionType
ALU = mybir.AluOpType
AX = mybir.AxisListType

D = 1024
INW = 2824
EPS = 1e-6
NEGM = -30000.0


class FW:
    def __init__(self, nc, es, ndma=6):
        self.nc = nc
        self.eng = {"pe": nc.tensor, "dve": nc.vector, "act": nc.scalar, "pool": nc.gpsimd, "sp": nc.sync}
        self.sem = {}
        self.cnt = {}
        for k in ["pe", "dve", "act", "pool"]:
            self.sem[k] = es.enter_context(nc.semaphore("s_" + k))
            self.cnt[k] = 0
        self.dq = {}
        for q in ["sp", "pool"]:
            sems = [es.enter_context(nc.semaphore("d_%s%d" % (q, i))) for i in range(ndma)]
            self.dq[q] = {"sems": sems, "n": 0}
        self.waited = {}
        self.lastw = {}
        self.reads = {}
        self.ninst = 0

    def _ev_wait(self, e, ev):
        sid, val, s, prod = ev
        key = (e, sid)
        if self.waited.get(key, 0) >= val:
            return
        self.waited[key] = val
        self.eng[e].wait_ge(s, val)

    def _deps(self, e, reads, writes):
        evs = []
        for r in reads:
            if r in self.lastw:
                evs.append(self.lastw[r])
        for w in writes:
            if w in self.lastw:
                evs.append(self.lastw[w])
            evs.extend(self.reads.get(w, []))
        for ev in evs:
            if ev[3] == e and e == "pe":
                continue
            self._ev_wait(e, ev)

    def _record(self, ev, reads, writes):
        for r in reads:
            lst = self.reads.setdefault(r, [])
            lst[:] = [x for x in lst if x[0] != ev[0]]
            lst.append(ev)
        for w in writes:
            self.lastw[w] = ev
            self.reads[w] = []

    def op(self, e, fn, reads=(), writes=()):
        writes = list(writes) + [r for r in reads if r.startswith("bk")]
        reads = [r for r in reads if not r.startswith("bk")]
        self._deps(e, reads, writes)
        self.cnt[e] += 1
        self.ninst += 1
        ins = fn()
        ins.then_inc(self.sem[e], 1)
        ev = (id(self.sem[e]), self.cnt[e], self.sem[e], e)
        self._record(ev, reads, writes)
        return ev

    def dma(self, q, out, in_, reads=(), writes=(), **kw):
        d = self.dq[q]
        n = d["n"]
        K = len(d["sems"])
        s = d["sems"][n % K]
        prev = 16 * (n // K)
        if prev > 0:
            self._ev_wait(q, (id(s), prev, s, "dma"))
        self._deps(q, reads, writes)
        d["n"] += 1
        self.ninst += 1
        ins = self.eng[q].dma_start(out=out, in_=in_, **kw)
        ins.then_inc(s, 16)
        ev = (id(s), prev + 16, s, "dma")
        self._record(ev, reads, writes)
        return ev

    def all_events(self):
        evs = []
        for k in ["pe", "dve", "act", "pool"]:
            if self.cnt[k] > 0:
                evs.append((id(self.sem[k]), self.cnt[k], self.sem[k], k))
        for q, d in self.dq.items():
            K = len(d["sems"])
            for i, s in enumerate(d["sems"]):
                n_i = (d["n"] - i + K - 1) // K
                if n_i > 0:
                    evs.append((id(s), 16 * n_i, s, "dma"))
        return evs

    def barrier(self):
        evs = self.all_events()
        for e in ["pe", "dve", "act", "pool", "sp"]:
            for ev in evs:
                self._ev_wait(e, ev)
        self.lastw = {}
        self.reads = {}

    def finish(self):
        evs = [ev for ev in self.all_events() if ev[3] == "dma"]
        for ev in evs:
            self._ev_wait("sp", ev)


class FWP:
    SHARED = {"wq", "keysT", "g2T", "io16", "ident_b", "ident_f", "X", "XN2T", "ABG", "kl", "io16i", "nhalf"}

    def __init__(self, fw, suf):
        self.fw, self.suf = fw, suf

    def _r(self, names):
        return [n if (n.startswith("bk") or n in self.SHARED) else n + self.suf for n in names]

    def op(self, e, fn, reads=(), writes=()):
        return self.fw.op(e, fn, reads=self._r(reads), writes=self._r(writes))

    def dma(self, q, out, in_, reads=(), writes=(), **kw):
        return self.fw.dma(q, out, in_, reads=self._r(reads), writes=self._r(writes), **kw)


def rel_bucket_np(rel):
    import jax
    import jax.numpy as jnp
    with jax.default_device(jax.devices("cpu")[0]):
        rel = jnp.asarray(rel, dtype=jnp.int32)
        half = 16
        max_exact = 8
        ret = jnp.where(rel > 0, half, 0)
        n = jnp.abs(rel)
        nf = jnp.maximum(n, 1).astype(jnp.float32)
        large = max_exact + (jnp.log(nf / max_exact) / math.log(128 / max_exact) * (half - max_exact)).astype(jnp.int32)
        large = jnp.minimum(large, half - 1)
        return np.asarray(ret + jnp.where(n < max_exact, n, large))


class _Stop(Exception):
    pass


def build(NTP, do_peer=True, depth=2):
    try:
        return _build(NTP, do_peer, depth)
    except _Stop as e:
        return e.args[0]


def _build(NTP, do_peer=True, depth=2):
    STOP = int(os.environ.get("KSTOP", "0"))
    T = NTP * 128
    NT = NTP + 1
    nc = bass.Bass("TRN2", target_bir_lowering=False)
    es = ExitStack()
    fw = FW(nc, es)

    def stop_here(stage):
        if STOP == stage:
            fw.barrier()
            fw.finish()
            print("STOP at stage", stage, "instructions:", fw.ninst)
            raise _Stop(nc)

    def din(name, shape, dt=F32):
        return nc.dram_tensor(name, list(shape), dt, kind="ExternalInput").ap()

    def dout(name, shape, dt=F32):
        return nc.dram_tensor(name, list(shape), dt, kind="ExternalOutput").ap()

    def dscr(name, shape, dt=F32):
        return nc.dram_tensor(name, list(shape), dt, kind="Internal").ap()

    xp = din("xp", [T, D]); xs = din("xs", [64, D])
    ck = din("ck", [2, 1024, 512]); cv = din("cv", [2, 1024, 512])
    s_c = din("s_c", [2, 4, 64, 64]); s_n = din("s_n", [2, 4, 64]); s_m = din("s_m", [2, 4])
    s_conv = din("s_conv", [2, 3, 256])
    norm1_g = din("norm1_g", [2, D]); w_in = din("w_in", [2, D, INW])
    da_lambda = din("da_lambda", [2, 4, 64]); da_subln_g = din("da_subln_g", [2, 128])
    rel_tab = din("rel_tab", [32, 4])
    ml_conv_w = din("ml_conv_w", [2, 4, 256]); ml_conv_b = din("ml_conv_b", [2, 256])
    ml_wq = din("ml_wq", [2, 4, 64, 64]); ml_wk = din("ml_wk", [2, 4, 64, 64])
    ml_gate_b = din("ml_gate_b", [2, 8]); ml_norm_g = din("ml_norm_g", [2, 256]); ml_skip = din("ml_skip", [2, 256])
    cm_norm_g = din("cm_norm_g", [2, 256]); cm_ws = din("cm_ws", [2, 4, 128, 128]); cm_b = din("cm_b", [2, 4, 128])
    w_out = din("w_out", [2, D, D]); norm2_g = din("norm2_g", [2, D])
    peer_wq = din("peer_wq", [2, D, 2048]); peer_keys = din("peer_keys", [2, 16, 128, 128])
    peer_u = din("peer_u", [2, 16384, D]); peer_v = din("peer_v", [2, 16384, D])
    final_g = din("final_g", [D])
    ohrel = din("ohrel", [32, 383])

    y_p = dout("y_p", [T, D]); y_s = dout("y_s", [64, D])
    nk_p = dout("nk_p", [2, T, 512]); nv_p = dout("nv_p", [2, T, 512])
    nc_p = dout("nc_p", [2, 4, 64, 64]); nn_p = dout("nn_p", [2, 4, 64]); nm_p = dout("nm_p", [2, 4])
    nconv_p = dout("nconv_p", [2, 3, 256])
    nk_s = dout("nk_s", [2, 64, 512]); nv_s = dout("nv_s", [2, 64, 512])
    nc_s = dout("nc_s", [2, 4, 64, 64]); nn_s = dout("nn_s", [2, 4, 64]); nm_s = dout("nm_s", [2, 4])
    nconv_s = dout("nconv_s", [2, 3, 256]); ncmv_s = dout("ncmv_s", [2, 64, 256])

    X = dscr("Xres", [NT * 128, D])
    A_d = dscr("A_d", [4, 383])
    UTs = dscr("UTs", [128, 128, 1024], BF16)
    Vs = dscr("Vs", [128, 128, 1024], BF16)
    XN2T = dscr("XN2T", [128, 8, NT * 128], BF16)
    ABG = dscr("ABG", [128, 3, NT * 128], F32)

    es.enter_context(nc.allow_non_contiguous_dma(reason="small parameter layouts"))
    es.enter_context(nc.allow_low_precision(reason="bf16 matmul operands"))

    V, S, G, PE = nc.vector, nc.scalar, nc.gpsimd, nc.tensor

    banks = [es.enter_context(nc.psum_tensor("bk%d" % i, [128, 512], F32)) for i in range(8)]
    BK = ["bk%d" % i for i in range(8)]

    def sbg(name, shape, dt):
        return es.enter_context(nc.sbuf_tensor(name, list(shape), dt))

    ident_f = sbg("ident_f", [128, 128], F32)
    ident_b = sbg("ident_b", [128, 128], BF16)
    triu_f = sbg("triu_f", [128, 128], F32)
    ones_f = sbg("ones_f", [128, 128], F32)
    biasT = sbg("biasT", [128, 4, 2, 128], F32)
    Jm = sbg("Jm", [128, 128], F32)
    Rv = sbg("Rv", [128, 8, 128], F32)
    fw.op("pool", lambda: G.memset(ident_f[:], 0.0), writes=["ident_f"])
    fw.op("pool", lambda: G.affine_select(out=ident_f[:], in_=ident_f[:], pattern=[[-1, 128]], compare_op=ALU.not_equal,
                                          fill=1.0, base=0, channel_multiplier=1), reads=["ident_f"], writes=["ident_f"])
    fw.op("dve", lambda: V.tensor_copy(out=ident_b[:], in_=ident_f[:]), reads=["ident_f"], writes=["ident_b"])
    fw.op("pool", lambda: G.memset(ones_f[:], 1.0), writes=["ones_f"])
    fw.op("pool", lambda: G.affine_select(out=triu_f[:], in_=ones_f[:], pattern=[[1, 128]], compare_op=ALU.is_ge,
                                          fill=0.0, base=0, channel_multiplier=-1), reads=["ones_f"], writes=["triu_f"])

    stop_here(10)
    with nc.sbuf_tensor("tab", [32, 4], F32) as tab, nc.sbuf_tensor("t15", [32, 4], F32) as t15, \
            nc.sbuf_tensor("ohs", [32, 383], F32) as ohs, nc.sbuf_tensor("Asb", [4, 383], F32) as Asb:
        fw.dma("sp", tab[:], rel_tab[:, :], writes=["tab"])
        fw.dma("sp", t15[:], rel_tab[15:16, :].to_broadcast([32, 4]), writes=["t15"])
        fw.dma("sp", ohs[:], ohrel[:, :], writes=["ohs"])
        fw.op("dve", lambda: V.tensor_tensor(out=tab[:], in0=tab[:], in1=t15[:], op=ALU.subtract), reads=["tab", "t15"], writes=["tab"])
        fw.op("pe", lambda: PE.matmul(banks[0][0:4, 0:383], lhsT=tab[:], rhs=ohs[:], start=True, stop=True),
              reads=["tab", "ohs"], writes=[BK[0]])
        fw.op("dve", lambda: V.tensor_copy(out=Asb[:], in_=banks[0][0:4, 0:383]), reads=[BK[0]], writes=["Asb"])
        fw.dma("sp", A_d[:, :], Asb[:], reads=["Asb"], writes=["A_d"])
        stop_here(11)
        fw.op("pool", lambda: G.memset(Jm[:], 0.0), writes=["Jm"])
        fw.op("pool", lambda: G.affine_select(out=Jm[:], in_=Jm[:], pattern=[[1, 128]], compare_op=ALU.not_equal,
                                              fill=1.0, base=-127, channel_multiplier=1), reads=["Jm"], writes=["Jm"])
        for h in range(4):
            src_prev = bass.AP(tensor=A_d.tensor, offset=h * 383 + 128, ap=[[1, 128], [1, 128]])
            src_diag = bass.AP(tensor=A_d.tensor, offset=h * 383 + 0, ap=[[1, 128], [1, 128]])
            fw.dma("sp", Rv[:, h * 2 + 0, :], src_prev, reads=["A_d"], writes=["Rv"])
            fw.dma("sp", Rv[:, h * 2 + 1, :], src_diag, reads=["A_d"], writes=["Rv"])
        stop_here(12)
        for half in range(2):
            for j in range(4):
                fw.op("pe", lambda: PE.matmul(banks[half][:, j * 128:(j + 1) * 128], lhsT=Jm[:], rhs=Rv[:, half * 4 + j, :], start=True, stop=True),
                      reads=["Jm", "Rv"], writes=[BK[half]])
            fw.op("dve", lambda: V.tensor_copy(out=biasT[:, half * 2:half * 2 + 2, :, :],
                                               in_=banks[half][:].rearrange("p (h t q) -> p h t q", h=2, t=2)), reads=[BK[half]], writes=["biasT"])
        for h in range(4):
            fw.op("pool", lambda: G.memset(biasT[64:128, h, 1, 0:64], NEGM), reads=["biasT"], writes=["biasT"])
        fw.barrier()
    stop_here(1)

    for L in range(depth):
        lam_init = 0.8 - 0.6 * math.exp(-0.3 * L)
        last = (L == depth - 1)
        with ExitStack() as pa:
            def sb(name, shape, dt):
                return pa.enter_context(nc.sbuf_tensor("%s_L%d" % (name, L), list(shape), dt))

            Win = sb("Win", [128, 8, INW], BF16)
            Wout = sb("Wout", [128, 8, D], BF16)
            g1T = sb("g1T", [128, 8], F32)
            convw = sb("convw", [128, 2, 4], F32)
            convb = sb("convb", [128, 2], F32)
            wq_bd = sb("wq_bd", [128, 2, 128], BF16)
            wk_bd = sb("wk_bd", [128, 2, 128], BF16)
            gateb = sb("gateb", [128, 8], F32)
            mlg = sb("mlg", [128, 256], F32)
            mlskip = sb("mlskip", [128, 256], F32)
            cmg = sb("cmg", [128, 256], F32)
            subg = sb("subg", [128, 128], F32)
            cmbT = sb("cmbT", [128, 4], F32)
            wsTm = sb("wsTm", [128, 4, 128], BF16)
            lamb = sb("lamb", [128, 4, 64], F32)
            lam2 = sb("lam2", [128, 2], F32)
            nlam = sb("nlam", [128, 1], F32)
            Cst = sb("Cst", [128, 2, 65], F32)
            Cbf = sb("Cbf", [128, 2, 65], BF16)
            mst = sb("mst", [4, 1], F32)
            mcT = sb("mcT", [128, 2, 131], F32)
            xt = sb("xt", [128, D], F32)
            junk = sb("junk", [128, D], F32)
            ss = sb("ss", [128, 8], F32)
            xn = sb("xn", [128, D], BF16)
            xnT = sb("xnT", [128, 8, 128], BF16)
            QTt = sb("QTt", [128, 4, 128], BF16)
            kvo = sb("kvo", [128, 2, 512], F32)
            mlb = sb("mlb", [128, 512], F32)
            sgo = sb("sgo", [128, 256], F32)
            gts = sb("gts", [128, 8], F32)
            ucm = sb("ucm", [128, 256], F32)
            gvc = sb("gvc", [128, 256], F32)
            vcm = sb("vcm", [128, 256], F32)
            vcb = sb("vcb", [128, 256], BF16)
            mixtok = sb("mixtok", [128, D], BF16)
            mixT = sb("mixT", [128, 8, 128], BF16)
            acc = sb("acc", [128, 2, 128], F32)
            ccT = sb("ccT", [128, 2, 128], F32)
            ccTb = sb("ccTb", [128, 2, 128], BF16)
            qmT = sb("qmT", [128, 2, 128], BF16)
            kmT = sb("kmT", [128, 2, 128], BF16)
            kmt = sb("kmt", [128, 256], BF16)
            cct = sb("cct", [128, 256], F32)
            Ff = sb("Ff", [128, 4], F32)
            wv = sb("wv", [128, 4], F32)
            av = sb("av", [128, 4], F32)
            eFL = sb("eFL", [128, 2], F32)
            VW = sb("VW", [128, 4, 65], BF16)
            Sm = sb("Sm", [128, 4, 128], BF16)
            ndt = sb("ndt", [128, 4, 65], F32)
            hh = sb("hh", [128, 4, 64], F32)
            sm4 = sb("sm4", [128, 8], F32)
            tmpa = sb("tmpa", [128, 256], F32)
            Pb = [sb("Pb%d" % i, [128, 4, 128], BF16) for i in range(2)]
            tmpS = sb("tmpS", [128, 128], F32)
            osb = sb("osb", [128, 128], F32)
            rr = sb("rr", [128, 4], F32)
            x1 = sb("x1", [128, D], F32)
            rowT = sb("rowT", [4, 128], F32)
            m4 = sb("m4", [4, 4], F32)
            m4d = sb("m4d", [4, 4], F32)

            for c0_ in range(0, INW, 1412):
                fw.dma("pool", Win[:, :, c0_:c0_ + 1412], w_in[L][:, c0_:c0_ + 1412].rearrange("(c p) n -> p c n", p=128), writes=["Win"])
            fw.dma("pool", Wout[:], w_out[L].rearrange("(c p) n -> p c n", p=128), writes=["Wout"])
            fw.dma("sp", g1T[:], norm1_g[L].rearrange("(c p) -> p c", p=128), writes=["g1T"])
            for a_ in range(2):
                fw.dma("sp", convw[:, a_, :], ml_conv_w[L][:, a_ * 128:(a_ + 1) * 128].rearrange("i p -> p i"), writes=["convw"])
            fw.dma("sp", convb[:], ml_conv_b[L].rearrange("(hh p) -> p hh", p=128), writes=["convb"])
            fw.op("pool", lambda: G.memset(wq_bd[:], 0.0), writes=["wq_bd"])
            fw.op("pool", lambda: G.memset(wk_bd[:], 0.0), writes=["wk_bd"])
            for h in range(4):
                g_, pb = h // 2, 64 * (h % 2)
                fw.dma("pool", wq_bd[pb:pb + 64, g_, pb:pb + 64], ml_wq[L, h], reads=["wq_bd"], writes=["wq_bd"])
                fw.dma("pool", wk_bd[pb:pb + 64, g_, pb:pb + 64], ml_wk[L, h], reads=["wk_bd"], writes=["wk_bd"])
            fw.dma("sp", gateb[:], ml_gate_b[L:L + 1, :].to_broadcast([128, 8]), writes=["gateb"])
            fw.dma("sp", mlg[:], ml_norm_g[L:L + 1, :].to_broadcast([128, 256]), writes=["mlg"])
            fw.dma("sp", mlskip[:], ml_skip[L:L + 1, :].to_broadcast([128, 256]), writes=["mlskip"])
            fw.dma("sp", cmg[:], cm_norm_g[L:L + 1, :].to_broadcast([128, 256]), writes=["cmg"])
            fw.dma("sp", subg[:], da_subln_g[L:L + 1, :].to_broadcast([128, 128]), writes=["subg"])
            fw.op("dve", lambda: V.tensor_scalar(out=subg[:], in0=subg[:], scalar1=float(1.0 - lam_init), scalar2=None, op0=ALU.mult),
                  reads=["subg"], writes=["subg"])
            fw.dma("sp", cmbT[:], cm_b[L].rearrange("g t -> t g"), writes=["cmbT"])
            fw.dma("sp", junk[:, 0:512], cm_ws[L].rearrange("g t s -> t g s"), writes=["junk"])
            for g_ in range(4):
                fw.op("pe", lambda: PE.transpose(out=banks[0][:, g_ * 128:(g_ + 1) * 128], in_=junk[:, g_ * 128:(g_ + 1) * 128], identity=ident_f[:]),
                      reads=["junk", "ident_f"], writes=[BK[0]])
            fw.op("dve", lambda: V.tensor_tensor(out=wsTm[:], in0=banks[0][:].rearrange("p (g t) -> p g t", g=4),
                                                 in1=triu_f[:].unsqueeze(1).to_broadcast([128, 4, 128]), op=ALU.mult),
                  reads=[BK[0], "triu_f"], writes=["wsTm"])
            fw.dma("sp", lamb[:], da_lambda[L:L + 1].to_broadcast([128, 4, 64]), writes=["lamb"])
            lv = lamb[:].rearrange("p (a b) d -> p a b d", b=2)
            fw.op("dve", lambda: V.tensor_tensor(out=junk[:, 0:128].rearrange("p (a d) -> p a d", a=2), in0=lv[:, :, 0, :], in1=lv[:, :, 1, :], op=ALU.mult),
                  reads=["lamb"], writes=["junk"])
            fw.op("dve", lambda: V.tensor_reduce(out=lam2[:], in_=junk[:, 0:128].rearrange("p (a d) -> p a d", a=2), axis=AX.X, op=ALU.add),
                  reads=["junk"], writes=["lam2"])
            fw.op("act", lambda: S.activation(out=lam2[:], in_=lam2[:], func=AF.Exp), reads=["lam2"], writes=["lam2"])
            fw.op("dve", lambda: V.scalar_tensor_tensor(out=nlam[:], in0=lam2[:, 1:2], scalar=float(-lam_init), in1=lam2[:, 0:1],
                                                        op0=ALU.add, op1=ALU.subtract), reads=["lam2"], writes=["nlam"])
            stop_here(2)

            def sample_cache_setup(KT_s, V1_s):
                ckb = junk[:].bitcast(BF16).rearrange("p (k n) -> p k n", k=4)
                for half in range(2):
                    fw.dma("pool", ckb, ck[L, half * 512:(half + 1) * 512, :].rearrange("(kt p) n -> p kt n", p=128), reads=["junk"], writes=["junk"])
                    for k4 in range(4):
                        kt = half * 4 + k4
                        for h in range(4):
                            fw.op("pe", lambda: PE.transpose(out=banks[0][:].bitcast(BF16)[:, h * 128:(h + 1) * 128], in_=ckb[:, k4, h * 128:(h + 1) * 128],
                                                             identity=ident_b[:]), reads=["junk", "ident_b"], writes=[BK[0]])
                        fw.op("act", lambda: S.copy(out=KT_s[:, :, kt * 128:(kt + 1) * 128],
                                                    in_=banks[0][:].bitcast(BF16)[:, 0:512].rearrange("p (h k) -> p h k", h=4)),
                              reads=[BK[0]], writes=["KT_s"])
                fw.op("pool", lambda: G.memset(V1_s[:], 1.0), writes=["V1_s"])
                for kt in range(8):
                    fw.dma("pool", V1_s[:, kt, :, 0:128], cv[L, kt * 128:(kt + 1) * 128, :].rearrange("p (h v) -> p h v", h=4), reads=["V1_s"], writes=["V1_s"])

            def tile_body(ti, KT_all, V1_all, KT_s, V1_s):
                    samp = (ti == NTP)
                    rows = 64 if samp else 128
                    r0 = ti * 128
                    if L == 0:
                        if samp:
                            fw.op("pool", lambda: G.memset(xt[64:128, :], 0.0), writes=["xt"])
                            fw.dma("sp", xt[0:64, :], xs[:, :], reads=["xt"], writes=["xt"])
                        else:
                            fw.dma("sp", xt[:], xp[r0:r0 + 128, :], writes=["xt"])
                    else:
                        fw.dma("sp", xt[:], X[r0:r0 + 128, :], reads=["X"], writes=["xt"])
                    fw.op("act", lambda: S.activation(out=junk[:], in_=xt[:], func=AF.Square, accum_out=ss[:, 0:1]), reads=["xt"], writes=["junk", "ss"])
                    fw.op("dve", lambda: V.tensor_scalar(out=ss[:, 0:1], in0=ss[:, 0:1], scalar1=1.0 / D, scalar2=EPS, op0=ALU.mult, op1=ALU.add), reads=["ss"], writes=["ss"])
                    fw.op("act", lambda: S.activation(out=ss[:, 0:1], in_=ss[:, 0:1], func=AF.Sqrt), reads=["ss"], writes=["ss"])
                    fw.op("dve", lambda: V.reciprocal(out=ss[:, 0:1], in_=ss[:, 0:1]), reads=["ss"], writes=["ss"])
                    fw.op("dve", lambda: V.tensor_scalar(out=xn[:], in0=xt[:], scalar1=ss[:, 0:1], scalar2=None, op0=ALU.mult), reads=["ss", "xt"], writes=["xn"])
                    b0b = banks[0][:].bitcast(BF16)
                    for c in range(8):
                        fw.op("pe", lambda: PE.transpose(out=b0b[:, c * 128:(c + 1) * 128], in_=xn[:, c * 128:(c + 1) * 128], identity=ident_b[:]),
                              reads=["xn", "ident_b"], writes=[BK[0]])
                    fw.op("dve", lambda: V.tensor_tensor(out=xnT[:], in0=b0b.rearrange("p (c t) -> p c t", c=8),
                                                         in1=g1T[:].unsqueeze(2).to_broadcast([128, 8, 128]), op=ALU.mult),
                          reads=[BK[0], "g1T"], writes=["xnT"])
                    for h in range(4):
                        for c in range(8):
                            fw.op("pe", lambda: PE.matmul(banks[1][:, h * 128:(h + 1) * 128], lhsT=Win[:, c, h * 128:(h + 1) * 128], rhs=xnT[:, c, :],
                                                          start=(c == 0), stop=(c == 7)), reads=["Win", "xnT"], writes=[BK[1]])
                    for h in range(4):
                        for c in range(8):
                            fw.op("pe", lambda: PE.matmul(banks[2][:, h * 128:(h + 1) * 128], lhsT=Win[:, c, 512 + h * 128:512 + (h + 1) * 128], rhs=xnT[:, c, :],
                                                          start=(c == 0), stop=(c == 7)), reads=["Win", "xnT"], writes=[BK[2]])
                    for hh_ in range(2):
                        for c in range(8):
                            fw.op("pe", lambda: PE.matmul(banks[5][:, hh_ * 128:(hh_ + 1) * 128], lhsT=Win[:, c, 1536 + hh_ * 128:1536 + (hh_ + 1) * 128], rhs=xnT[:, c, :],
                                                          start=(c == 0), stop=(c == 7)), reads=["Win", "xnT"], writes=[BK[5]])
                    fw.op("act", lambda: S.copy(out=QTt[:], in_=banks[1][:].rearrange("p (h t) -> p h t", h=4)), reads=[BK[1]], writes=["QTt"])
                    KTd = KT_s[:, :, 1024:1152] if samp else KT_all[:, :, r0:r0 + 128]
                    ktn = "KT_s" if samp else "KT_all"
                    fw.op("act", lambda: S.copy(out=KTd, in_=banks[2][:].rearrange("p (h t) -> p h t", h=4)), reads=[BK[2]], writes=[ktn])
                    if ti == 0:
                        fw.op("pool", lambda: G.memset(mcT[:, :, 0:3], 0.0), writes=["mcT"])
                    elif samp:
                        for a_ in range(2):
                            fw.dma("sp", mcT[:, a_, 0:3], s_conv[L][:, a_ * 128:(a_ + 1) * 128].rearrange("i p -> p i"), reads=["mcT"], writes=["mcT"])
                    else:
                        fw.op("dve", lambda: V.tensor_copy(out=mcT[:, :, 0:3], in_=mcT[:, :, 128:131]), reads=["mcT"], writes=["mcT"])
                    fw.op("dve", lambda: V.tensor_copy(out=mcT[:, :, 3:131], in_=banks[5][:, 0:256].rearrange("p (a t) -> p a t", a=2)),
                          reads=[BK[5], "mcT"], writes=["mcT"])
                    stop_here(30)
                    def tokproj(bk, c0, n):
                        for c in range(8):
                            fw.op("pe", lambda: PE.matmul(banks[bk][:, 0:n], lhsT=xnT[:, c, :], rhs=Win[:, c, c0:c0 + n], start=(c == 0), stop=(c == 7)),
                                  reads=["Win", "xnT"], writes=[BK[bk]])
                    tokproj(3, 512, 512)
                    fw.op("dve", lambda: V.tensor_copy(out=kvo[:, 0, :], in_=banks[3][:]), reads=[BK[3]], writes=["kvo0"])
                    if samp:
                        fw.dma("sp", nk_s[L], kvo[0:64, 0, :], reads=["kvo0"], writes=["o_nk"])
                    else:
                        fw.dma("sp", nk_p[L, r0:r0 + 128, :], kvo[:, 0, :], reads=["kvo0"], writes=["o_nk"])
                    stop_here(301)
                    tokproj(4, 1024, 512)
                    fw.op("dve", lambda: V.tensor_copy(out=kvo[:, 1, :], in_=banks[4][:]), reads=[BK[4]], writes=["kvo1"])
                    V1d = V1_s[:, 8, :, 0:128] if samp else V1_all[:, ti, :, 0:128]
                    v1n = "V1_s" if samp else "V1_all"
                    fw.op("act", lambda: S.copy(out=V1d, in_=banks[4][:].rearrange("p (h v) -> p h v", h=4)), reads=[BK[4]], writes=[v1n])
                    if samp:
                        fw.dma("sp", nv_s[L], kvo[0:64, 1, :], reads=["kvo1"], writes=["o_nv"])
                    else:
                        fw.dma("sp", nv_p[L, r0:r0 + 128, :], kvo[:, 1, :], reads=["kvo1"], writes=["o_nv"])
                    stop_here(302)
                    tokproj(3, 1536, 512)
                    fw.op("dve", lambda: V.tensor_copy(out=mlb[:], in_=banks[3][:]), reads=[BK[3]], writes=["mlb"])
                    if samp:
                        fw.dma("sp", nconv_s[L], mlb[61:64, 0:256], reads=["mlb"], writes=["o_conv"])
                    elif ti == NTP - 1:
                        fw.dma("sp", nconv_p[L], mlb[125:128, 0:256], reads=["mlb"], writes=["o_conv"])
                    stop_here(303)
                    tokproj(4, 2048, 264)
                    fw.op("act", lambda: S.activation(out=sgo[:], in_=banks[4][:, 0:256], func=AF.Sigmoid), reads=[BK[4]], writes=["sgo"])
                    fw.op("dve", lambda: V.tensor_tensor(out=gts[:], in0=banks[4][:, 256:264], in1=gateb[:], op=ALU.add), reads=[BK[4], "gateb"], writes=["gts"])
                    stop_here(304)
                    tokproj(3, 2312, 512)
                    fw.op("act", lambda: S.activation(out=ucm[:], in_=banks[3][:, 0:256], func=AF.Gelu), reads=[BK[3]], writes=["ucm"])
                    fw.op("act", lambda: S.activation(out=gvc[:], in_=banks[3][:, 256:512], func=AF.Gelu), reads=[BK[3]], writes=["gvc"])
                    stop_here(31)
                    fw.op("dve", lambda: V.tensor_tensor(out=tmpa[:], in0=gvc[:], in1=gvc[:], op=ALU.mult), reads=["gvc"], writes=["tmpa"])
                    fw.op("dve", lambda: V.tensor_reduce(out=ss[:, 1:2], in_=tmpa[:], axis=AX.X, op=ALU.add), reads=["tmpa"], writes=["ss1"])
                    fw.op("dve", lambda: V.tensor_scalar(out=ss[:, 1:2], in0=ss[:, 1:2], scalar1=1.0 / 256, scalar2=EPS, op0=ALU.mult, op1=ALU.add), reads=["ss1"], writes=["ss1"])
                    fw.op("act", lambda: S.activation(out=ss[:, 1:2], in_=ss[:, 1:2], func=AF.Sqrt), reads=["ss1"], writes=["ss1"])
                    fw.op("dve", lambda: V.reciprocal(out=ss[:, 1:2], in_=ss[:, 1:2]), reads=["ss1"], writes=["ss1"])
                    fw.op("dve", lambda: V.scalar_tensor_tensor(out=vcm[:], in0=gvc[:], scalar=ss[:, 1:2], in1=cmg[:], op0=ALU.mult, op1=ALU.mult),
                          reads=["gvc", "ss1", "cmg"], writes=["vcm"])
                    fw.op("dve", lambda: V.tensor_copy(out=vcb[:], in_=vcm[:]), reads=["vcm"], writes=["vcb"])
                    if samp:
                        fw.dma("sp", ncmv_s[L], vcm[0:64, :], reads=["vcm"], writes=["o_cmv"])
                    for g_ in range(4):
                        fw.op("pe", lambda: PE.matmul(banks[5][:, g_ * 64:(g_ + 1) * 64], lhsT=wsTm[:, g_, :], rhs=vcb[:, g_ * 64:(g_ + 1) * 64], start=True, stop=True),
                              reads=["wsTm", "vcb"], writes=[BK[5]])
                    fw.op("dve", lambda: V.tensor_tensor(out=tmpa[:].rearrange("p (g d) -> p g d", g=4), in0=banks[5][:, 0:256].rearrange("p (g d) -> p g d", g=4),
                                                         in1=cmbT[:].unsqueeze(2).to_broadcast([128, 4, 64]), op=ALU.add), reads=[BK[5], "cmbT"], writes=["tmpa"])
                    fw.op("dve", lambda: V.tensor_tensor(out=mixtok[:, 768:1024], in0=tmpa[:], in1=ucm[:], op=ALU.mult), reads=["tmpa", "ucm"], writes=["mixtok"])
                    stop_here(32)
                    fw.op("dve", lambda: V.tensor_tensor(out=acc[:], in0=mcT[:, :, 0:128], in1=convw[:, :, 0:1].to_broadcast([128, 2, 128]), op=ALU.mult),
                          reads=["mcT", "convw"], writes=["acc"])
                    for i_ in range(1, 4):
                        fw.op("dve", lambda: V.tensor_tensor(out=ccT[:], in0=mcT[:, :, i_:i_ + 128], in1=convw[:, :, i_:i_ + 1].to_broadcast([128, 2, 128]), op=ALU.mult),
                              reads=["mcT", "convw"], writes=["ccT"])
                        fw.op("dve", lambda: V.tensor_tensor(out=acc[:], in0=acc[:], in1=ccT[:], op=ALU.add), reads=["acc", "ccT"], writes=["acc"])
                    for a_ in range(2):
                        fw.op("act", lambda: S.activation(out=ccT[:, a_, :], in_=acc[:, a_, :], func=AF.Silu, bias=convb[:, a_:a_ + 1]), reads=["acc", "convb"], writes=["ccT"])
                    fw.op("dve", lambda: V.tensor_copy(out=ccTb[:], in_=ccT[:]), reads=["ccT"], writes=["ccTb"])
                    for g_ in range(2):
                        fw.op("pe", lambda: PE.matmul(banks[5][:, g_ * 128:(g_ + 1) * 128], lhsT=wq_bd[:, g_, :], rhs=ccTb[:, g_, :], start=True, stop=True),
                              reads=["wq_bd", "ccTb"], writes=[BK[5]])
                        fw.op("pe", lambda: PE.matmul(banks[5][:, 256 + g_ * 128:256 + (g_ + 1) * 128], lhsT=wk_bd[:, g_, :], rhs=ccTb[:, g_, :], start=True, stop=True),
                              reads=["wk_bd", "ccTb"], writes=[BK[5]])
                        fw.op("pe", lambda: PE.matmul(banks[3][:, g_ * 128:(g_ + 1) * 128], lhsT=ccTb[:, g_, :], rhs=wk_bd[:, g_, :], start=True, stop=True),
                              reads=["wk_bd", "ccTb"], writes=[BK[3]])
                        fw.op("pe", lambda: PE.transpose(out=banks[3][:, 256 + g_ * 128:256 + (g_ + 1) * 128], in_=ccT[:, g_, :], identity=ident_f[:]),
                              reads=["ccT", "ident_f"], writes=[BK[3]])
                    fw.op("act", lambda: S.copy(out=qmT[:], in_=banks[5][:, 0:256].rearrange("p (g t) -> p g t", g=2)), reads=[BK[5]], writes=["qmT"])
                    fw.op("act", lambda: S.mul(out=kmT[:], in_=banks[5][:, 256:512].rearrange("p (g t) -> p g t", g=2), mul=0.125), reads=[BK[5]], writes=["kmT"])
                    fw.op("act", lambda: S.mul(out=kmt[:], in_=banks[3][:, 0:256], mul=0.125), reads=[BK[3]], writes=["kmt"])
                    fw.op("dve", lambda: V.tensor_copy(out=cct[:], in_=banks[3][:, 256:512]), reads=[BK[3]], writes=["cct"])
                    stop_here(33)
                    fw.op("act", lambda: S.activation(out=sm4[:, 0:4], in_=gts[:, 4:8], func=AF.Exp, scale=-1.0), reads=["gts"], writes=["sm4"])
                    fw.op("act", lambda: S.activation(out=sm4[:, 0:4], in_=sm4[:, 0:4], func=AF.Ln, bias=1.0), reads=["sm4"], writes=["sm4"])
                    fw.op("dve", lambda: V.tensor_scalar(out=sm4[:, 0:4], in0=sm4[:, 0:4], scalar1=-1.0, scalar2=None, op0=ALU.mult), reads=["sm4"], writes=["sm4"])
                    fw.op("pe", lambda: PE.matmul(banks[4][:, 0:4], lhsT=triu_f[:], rhs=sm4[:, 0:4], start=True, stop=True), reads=["triu_f", "sm4"], writes=[BK[4]])
                    fw.op("pe", lambda: PE.matmul(banks[4][:, 8:12], lhsT=ones_f[0:rows, :], rhs=sm4[0:rows, 0:4], start=True, stop=True), reads=["ones_f", "sm4"], writes=[BK[4]])
                    fw.op("dve", lambda: V.tensor_copy(out=Ff[:], in_=banks[4][:, 0:4]), reads=[BK[4]], writes=["Ff"])
                    fw.op("act", lambda: S.activation(out=av[:], in_=Ff[:], func=AF.Exp), reads=["Ff"], writes=["av"])
                    fw.op("dve", lambda: V.tensor_tensor(out=sm4[:, 4:8], in0=gts[:, 0:4], in1=Ff[:], op=ALU.subtract), reads=["gts", "Ff", "sm4"], writes=["sm4"])
                    fw.op("act", lambda: S.activation(out=wv[:], in_=sm4[:, 4:8], func=AF.Exp), reads=["sm4"], writes=["wv"])
                    flv = banks[4][:, 8:12].rearrange("p (g a) -> p g a", a=2)
                    fw.op("act", lambda: S.activation(out=eFL[0:64, :], in_=flv[0:64, :, 0], func=AF.Exp), reads=[BK[4]], writes=["eFL"])
                    fw.op("act", lambda: S.activation(out=eFL[64:128, :], in_=flv[64:128, :, 1], func=AF.Exp), reads=[BK[4]], writes=["eFL"])
                    fw.op("pe", lambda: PE.transpose(out=banks[4][0:4, 128:256], in_=sm4[:, 4:8], identity=ident_f[:]), reads=["sm4", "ident_f"], writes=[BK[4]])
                    fw.op("dve", lambda: V.tensor_copy(out=rowT[:], in_=banks[4][0:4, 128:256]), reads=[BK[4]], writes=["rowT"])
                    fw.op("dve", lambda: V.tensor_reduce(out=m4[:, 0:1], in_=rowT[:, 0:rows], axis=AX.X, op=ALU.max), reads=["rowT"], writes=["m4"])
                    fw.op("pe", lambda: PE.matmul(banks[4][0:4, 16:17], lhsT=sm4[0:rows, 0:4], rhs=ones_f[0:rows, 0:1], start=True, stop=True), reads=["sm4", "ones_f"], writes=[BK[4]])
                    if ti == 0:
                        fw.op("pool", lambda: G.memset(mst[:], 0.0), reads=["mst"], writes=["mst"])
                        fw.op("pool", lambda: G.memset(Cst[:], 0.0), reads=["Cst"], writes=["Cst"])
                        fw.op("pool", lambda: G.memset(Cbf[:], 0.0), reads=["Cbf"], writes=["Cbf"])
                    elif samp:
                        fw.dma("sp", mst[:], s_m[L].rearrange("(h o) -> h o", o=1), reads=["mst"], writes=["mst"])
                        for h in range(4):
                            g_, pb = h // 2, 64 * (h % 2)
                            fw.dma("sp", Cst[pb:pb + 64, g_, 0:64], s_c[L, h], reads=["Cst"], writes=["Cst"])
                            fw.dma("sp", Cst[pb:pb + 64, g_, 64:65], s_n[L, h].rearrange("(e o) -> e o", o=1), reads=["Cst"], writes=["Cst"])
                        bcast_heads(fw, nc, mst, m4d, banks, BK, ones_f, ident_f, tmpa, scale=1.0)
                        fw.op("dve", lambda: V.tensor_tensor(out=Cst[:], in0=Cst[:], in1=tmpa[:, 0:2].unsqueeze(2).to_broadcast([128, 2, 65]), op=ALU.mult),
                              reads=["Cst", "tmpa"], writes=["Cst"])
                        fw.op("dve", lambda: V.tensor_copy(out=Cbf[:], in_=Cst[:]), reads=["Cst"], writes=["Cbf"])
                    fw.op("dve", lambda: V.tensor_tensor(out=mst[:], in0=mst[:], in1=m4[:, 0:1], op=ALU.max), reads=["mst", "m4"], writes=["mst"])
                    fw.op("dve", lambda: V.tensor_tensor(out=mst[:], in0=mst[:], in1=banks[4][0:4, 16:17], op=ALU.add), reads=["mst", BK[4]], writes=["mst"])
                    stop_here(34)
                    fw.op("dve", lambda: V.tensor_tensor(out=VW[:, :, 0:64], in0=mlb[:, 256:512].rearrange("p (h v) -> p h v", h=4),
                                                         in1=wv[:].unsqueeze(2).to_broadcast([128, 4, 64]), op=ALU.mult), reads=["mlb", "wv"], writes=["VW"])
                    fw.op("dve", lambda: V.tensor_copy(out=VW[:, :, 64:65], in_=wv[:].unsqueeze(2)), reads=["wv", "VW"], writes=["VW"])
                    if samp:
                        fw.op("pool", lambda: G.memset(VW[64:128, :, :], 0.0), reads=["VW"], writes=["VW"])
                    for h in (0, 2, 1, 3):
                        g_, pb = h // 2, 64 * (h % 2)
                        bk_ = 6 if h % 2 == 0 else 3
                        fw.op("pe", lambda: PE.matmul(banks[bk_][:, g_ * 128:(g_ + 1) * 128], lhsT=kmT[pb:pb + 64, g_, :], rhs=qmT[pb:pb + 64, g_, :], start=True, stop=True),
                              reads=["kmT", "qmT"], writes=[BK[bk_]])
                    Smv = Sm[:].rearrange("p (g a) t -> p g a t", a=2)
                    for a_, bk_ in ((0, 6), (1, 3)):
                        fw.op("dve", lambda: V.tensor_tensor(out=Smv[:, :, a_, :], in0=banks[bk_][:, 0:256].rearrange("p (g t) -> p g t", g=2),
                                                             in1=triu_f[:].unsqueeze(1).to_broadcast([128, 2, 128]), op=ALU.mult), reads=[BK[bk_], "triu_f"], writes=["Sm"])
                    for h in range(4):
                        g_, pb = h // 2, 64 * (h % 2)
                        fw.op("pe", lambda: PE.matmul(banks[7][:, h * 65:(h + 1) * 65], lhsT=Sm[:, h, :], rhs=VW[:, h, :], start=True, stop=False),
                              reads=["Sm", "VW"], writes=[BK[7]])
                        fw.op("pe", lambda: PE.matmul(banks[7][:, h * 65:(h + 1) * 65], lhsT=qmT[pb:pb + 64, g_, :], rhs=Cbf[pb:pb + 64, g_, :], start=False, stop=True),
                              reads=["qmT", "Cbf"], writes=[BK[7]])
                        fw.op("pe", lambda: PE.matmul(banks[4][pb:pb + 64, 256 + g_ * 65:256 + (g_ + 1) * 65], lhsT=kmt[:, h * 64:(h + 1) * 64], rhs=VW[:, h, :], start=True, stop=True),
                              reads=["kmt", "VW"], writes=[BK[4]])
                    fw.op("dve", lambda: V.tensor_tensor(out=ndt[:], in0=banks[7][:, 0:260].rearrange("p (h v) -> p h v", h=4),
                                                         in1=av[:].unsqueeze(2).to_broadcast([128, 4, 65]), op=ALU.mult), reads=[BK[7], "av"], writes=["ndt"])
                    fw.op("dve", lambda: V.tensor_tensor(out=Cst[:], in0=Cst[:], in1=banks[4][:, 256:386].rearrange("p (g v) -> p g v", g=2), op=ALU.add),
                          reads=["Cst", BK[4]], writes=["Cst"])
                    fw.op("dve", lambda: V.tensor_tensor(out=Cst[:], in0=Cst[:], in1=eFL[:].unsqueeze(2).to_broadcast([128, 2, 65]), op=ALU.mult),
                          reads=["Cst", "eFL"], writes=["Cst"])
                    fw.op("dve", lambda: V.tensor_copy(out=Cbf[:], in_=Cst[:]), reads=["Cst"], writes=["Cbf"])
                    stop_here(35)
                    fw.op("dve", lambda: V.scalar_tensor_tensor(out=rr[:], in0=ndt[:, :, 64], scalar=-1.0, in1=ndt[:, :, 64], op0=ALU.mult, op1=ALU.max), reads=["ndt"], writes=["rr"])
                    fw.op("dve", lambda: V.tensor_scalar(out=rr[:], in0=rr[:], scalar1=1.0, scalar2=None, op0=ALU.max), reads=["rr"], writes=["rr"])
                    fw.op("dve", lambda: V.reciprocal(out=rr[:], in_=rr[:]), reads=["rr"], writes=["rr"])
                    fw.op("dve", lambda: V.tensor_tensor(out=hh[:], in0=ndt[:, :, 0:64], in1=rr[:].unsqueeze(2).to_broadcast([128, 4, 64]), op=ALU.mult), reads=["ndt", "rr"], writes=["hh"])
                    fw.op("dve", lambda: V.tensor_tensor(out=tmpa[:].rearrange("p (h v) -> p h v", h=4), in0=hh[:], in1=hh[:], op=ALU.mult), reads=["hh"], writes=["tmpa"])
                    fw.op("dve", lambda: V.tensor_reduce(out=rr[:], in_=tmpa[:].rearrange("p (h v) -> p h v", h=4), axis=AX.X, op=ALU.add), reads=["tmpa"], writes=["rr"])
                    fw.op("dve", lambda: V.tensor_scalar(out=rr[:], in0=rr[:], scalar1=1.0 / 64, scalar2=EPS, op0=ALU.mult, op1=ALU.add), reads=["rr"], writes=["rr"])
                    fw.op("act", lambda: S.activation(out=rr[:], in_=rr[:], func=AF.Sqrt), reads=["rr"], writes=["rr"])
                    fw.op("dve", lambda: V.reciprocal(out=rr[:], in_=rr[:]), reads=["rr"], writes=["rr"])
                    fw.op("dve", lambda: V.tensor_tensor(out=hh[:], in0=hh[:], in1=rr[:].unsqueeze(2).to_broadcast([128, 4, 64]), op=ALU.mult), reads=["hh", "rr"], writes=["hh"])
                    fw.op("dve", lambda: V.tensor_tensor(out=tmpa[:], in0=hh[:].rearrange("p h v -> p (h v)"), in1=mlg[:], op=ALU.mult), reads=["hh", "mlg"], writes=["tmpa"])
                    fw.op("dve", lambda: V.tensor_tensor(out=cct[:], in0=cct[:], in1=mlskip[:], op=ALU.mult), reads=["cct", "mlskip"], writes=["cct"])
                    fw.op("dve", lambda: V.tensor_tensor(out=tmpa[:], in0=tmpa[:], in1=cct[:], op=ALU.add), reads=["tmpa", "cct"], writes=["tmpa"])
                    fw.op("dve", lambda: V.tensor_tensor(out=mixtok[:, 512:768], in0=tmpa[:], in1=sgo[:], op=ALU.mult), reads=["tmpa", "sgo"], writes=["mixtok"])
                    if samp:
                        emit_state(fw, nc, L, Cst, mst, m4d, nc_s, nn_s, nm_s, banks, BK, ones_f, ident_f, tmpa)
                    elif ti == NTP - 1:
                        emit_state(fw, nc, L, Cst, mst, m4d, nc_p, nn_p, nm_p, banks, BK, ones_f, ident_f, tmpa)
                    stop_here(36)
                    KTb = KT_s if samp else KT_all
                    V1b = V1_s if samp else V1_all
                    nk = 9 if samp else ti + 1
                    pcount = 0
                    for h in range(4):
                        for c in range(2):
                            pb = 64 * c
                            Ob = banks[1 + c]
                            On = BK[1 + c]
                            kts = list(range(nk))
                            far, near = kts[:-2], kts[-2:]
                            first = True
                            for g0 in range(0, len(far), 4):
                                grp = far[g0:g0 + 4]
                                sbk = 6 + (pcount % 2)
                                pbuf = Pb[pcount % 2]
                                pbn = "Pb%d" % (pcount % 2)
                                pcount += 1
                                for j, kt in enumerate(grp):
                                    fw.op("pe", lambda: PE.matmul(banks[sbk][:, j * 128:(j + 1) * 128], lhsT=KTb[pb:pb + 64, h, kt * 128:(kt + 1) * 128],
                                                                  rhs=QTt[pb:pb + 64, h, :], start=True, stop=True), reads=[ktn, "QTt"], writes=[BK[sbk]])
                                n_ = len(grp)
                                fw.op("act", lambda: S.activation(out=pbuf[:, 0:n_, :], in_=banks[sbk][:, 0:n_ * 128].rearrange("p (j q) -> p j q", j=n_),
                                                                  func=AF.Exp, scale=0.125), reads=[BK[sbk]], writes=[pbn])
                                for j, kt in enumerate(grp):
                                    fw.op("pe", lambda: PE.matmul(Ob[:, 0:129], lhsT=pbuf[:, j, :], rhs=V1b[:, kt, h, 0:129], start=first, stop=False),
                                          reads=[pbn, v1n], writes=[On])
                                    first = False
                            for kt in near:
                                typ = 1 if kt == kts[-1] else 0
                                sbk = 6 + (pcount % 2)
                                pbuf = Pb[pcount % 2]
                                pbn = "Pb%d" % (pcount % 2)
                                pcount += 1
                                fw.op("pe", lambda: PE.matmul(banks[sbk][:, 0:128], lhsT=KTb[pb:pb + 64, h, kt * 128:(kt + 1) * 128],
                                                              rhs=QTt[pb:pb + 64, h, :], start=True, stop=True), reads=[ktn, "QTt"], writes=[BK[sbk]])
                                fw.op("dve", lambda: V.scalar_tensor_tensor(out=tmpS[:], in0=banks[sbk][:, 0:128], scalar=0.125, in1=biasT[:, h, typ, :],
                                                                            op0=ALU.mult, op1=ALU.add), reads=[BK[sbk], "biasT"], writes=["tmpS"])
                                fw.op("act", lambda: S.activation(out=pbuf[:, 0, :], in_=tmpS[:], func=AF.Exp), reads=["tmpS"], writes=[pbn])
                                fw.op("pe", lambda: PE.matmul(Ob[:, 0:129], lhsT=pbuf[:, 0, :], rhs=V1b[:, kt, h, 0:129], start=first, stop=(kt == kts[-1])),
                                      reads=[pbn, v1n], writes=[On])
                                first = False
                        fw.op("dve", lambda: V.reciprocal(out=rr[:, 0:1], in_=banks[1][:, 128:129]), reads=[BK[1]], writes=["rr"])
                        fw.op("dve", lambda: V.reciprocal(out=rr[:, 1:2], in_=banks[2][:, 128:129]), reads=[BK[2], "rr"], writes=["rr"])
                        fw.op("dve", lambda: V.tensor_tensor(out=rr[:, 1:2], in0=rr[:, 1:2], in1=nlam[:], op=ALU.mult), reads=["rr", "nlam"], writes=["rr"])
                        fw.op("dve", lambda: V.tensor_scalar(out=osb[:], in0=banks[1][:, 0:128], scalar1=rr[:, 0:1], scalar2=None, op0=ALU.mult), reads=[BK[1], "rr"], writes=["osb"])
                        fw.op("dve", lambda: V.scalar_tensor_tensor(out=osb[:], in0=banks[2][:, 0:128], scalar=rr[:, 1:2], in1=osb[:], op0=ALU.mult, op1=ALU.add),
                              reads=[BK[2], "rr", "osb"], writes=["osb"])
                        fw.op("act", lambda: S.activation(out=tmpS[:], in_=osb[:], func=AF.Square, accum_out=rr[:, 2:3]), reads=["osb"], writes=["tmpS", "rr"])
                        fw.op("dve", lambda: V.tensor_scalar(out=rr[:, 2:3], in0=rr[:, 2:3], scalar1=1.0 / 128, scalar2=EPS, op0=ALU.mult, op1=ALU.add), reads=["rr"], writes=["rr"])
                        fw.op("act", lambda: S.activation(out=rr[:, 2:3], in_=rr[:, 2:3], func=AF.Sqrt), reads=["rr"], writes=["rr"])
                        fw.op("dve", lambda: V.reciprocal(out=rr[:, 2:3], in_=rr[:, 2:3]), reads=["rr"], writes=["rr"])
                        fw.op("dve", lambda: V.scalar_tensor_tensor(out=mixtok[:, h * 128:(h + 1) * 128], in0=osb[:], scalar=rr[:, 2:3], in1=subg[:], op0=ALU.mult, op1=ALU.mult),
                              reads=["osb", "rr", "subg"], writes=["mixtok"])
                    stop_here(37)
                    for c in range(8):
                        fw.op("pe", lambda: PE.transpose(out=b0b[:, c * 128:(c + 1) * 128], in_=mixtok[:, c * 128:(c + 1) * 128], identity=ident_b[:]),
                              reads=["mixtok", "ident_b"], writes=[BK[0]])
                    fw.op("act", lambda: S.copy(out=mixT[:], in_=b0b.rearrange("p (c t) -> p c t", c=8)), reads=[BK[0]], writes=["mixT"])
                    for hf in range(2):
                        for c in range(8):
                            fw.op("pe", lambda: PE.matmul(banks[3 + hf][:], lhsT=mixT[:, c, :], rhs=Wout[:, c, hf * 512:(hf + 1) * 512], start=(c == 0), stop=(c == 7)),
                                  reads=["mixT", "Wout"], writes=[BK[3 + hf]])
                        fw.op("dve", lambda: V.tensor_tensor(out=x1[:, hf * 512:(hf + 1) * 512], in0=banks[3 + hf][:], in1=xt[:, hf * 512:(hf + 1) * 512], op=ALU.add),
                              reads=[BK[3 + hf], "xt"], writes=["x1"])
                    fw.dma("sp", X[r0:r0 + 128, :], x1[:], reads=["x1"], writes=["X"])
            with ExitStack() as scs:
                KT_s = scs.enter_context(nc.sbuf_tensor("KT_s%d" % L, [128, 4, 9 * 128], BF16))
                V1_s = scs.enter_context(nc.sbuf_tensor("V1_s%d" % L, [128, 9, 4, 130], BF16))
                sample_cache_setup(KT_s, V1_s)
                stop_here(3)
                tile_body(NTP, None, None, KT_s, V1_s)
                fw.barrier()
                stop_here(4)
            with ExitStack() as scp:
                KT_all = scp.enter_context(nc.sbuf_tensor("KT_all%d" % L, [128, 4, T], BF16))
                V1_all = scp.enter_context(nc.sbuf_tensor("V1_all%d" % L, [128, NTP, 4, 130], BF16))
                fw.op("pool", lambda: G.memset(V1_all[:], 1.0), writes=["V1_all"])
                for ti in range(NTP):
                    tile_body(ti, KT_all, V1_all, None, None)
            fw.barrier()

        peer_phase(fw, nc, es, L, NT, NTP, last, do_peer, X, UTs, banks, BK, ident_f, ident_b,
                   norm2_g, peer_wq, peer_keys, peer_u, peer_v, final_g, y_p, y_s, Vs, XN2T, ABG, stop_here)
        fw.barrier()

    fw.finish()
    print("instructions:", fw.ninst)
    return nc


def bcast_heads(fw, nc, col4, m4, banks, BK, ones_f, ident_f, tmpa, scale):
    V, S, PE = nc.vector, nc.scalar, nc.tensor
    fw.op("dve", lambda: V.tensor_scalar(out=m4[:, :], in0=ident_f[0:4, 0:4], scalar1=col4[:, 0:1], scalar2=None, op0=ALU.mult),
          reads=["ident_f", "mst", "m4d"], writes=["m4d"])
    fw.op("pe", lambda: PE.matmul(banks[4][:, 32:36], lhsT=ones_f[0:4, :], rhs=m4[:, :], start=True, stop=True), reads=["ones_f", "m4d"], writes=[BK[4]])
    v = banks[4][:, 32:36].rearrange("p (g a) -> p g a", a=2)
    fw.op("act", lambda: S.activation(out=tmpa[0:64, 0:2], in_=v[0:64, :, 0], func=AF.Exp, scale=scale), reads=[BK[4], "tmpa"], writes=["tmpa"])
    fw.op("act", lambda: S.activation(out=tmpa[64:128, 0:2], in_=v[64:128, :, 1], func=AF.Exp, scale=scale), reads=[BK[4], "tmpa"], writes=["tmpa"])


def emit_state(fw, nc, L, Cst, mst, m4, o_c, o_n, o_m, banks, BK, ones_f, ident_f, tmpa):
    V = nc.vector
    bcast_heads(fw, nc, mst, m4, banks, BK, ones_f, ident_f, tmpa, scale=-1.0)
    cs = tmpa[:, 4:134].rearrange("p (g v) -> p g v", g=2)
    fw.op("dve", lambda: V.tensor_tensor(out=cs, in0=Cst[:], in1=tmpa[:, 0:2].unsqueeze(2).to_broadcast([128, 2, 65]), op=ALU.mult),
          reads=["Cst", "tmpa"], writes=["tmpa"])
    for h in range(4):
        g_, pb = h // 2, 64 * (h % 2)
        fw.dma("sp", o_c[L, h], cs[pb:pb + 64, g_, 0:64], reads=["tmpa"], writes=["o_c"])
        fw.dma("sp", o_n[L, h].rearrange("(e o) -> e o", o=1), cs[pb:pb + 64, g_, 64:65], reads=["tmpa"], writes=["o_n"])
    fw.dma("sp", o_m[L].rearrange("(h o) -> h o", o=1), mst[:], reads=["mst"], writes=["o_m"])


def peer_phase(fw, nc, es, L, NT, NTP, last, do_peer, X, UTs, banks, BK, ident_f, ident_b,
               norm2_g, peer_wq, peer_keys, peer_u, peer_v, final_g, y_p, y_s, Vs, XN2T, ABG, stop_here):
    V, S, G, PE = nc.vector, nc.scalar, nc.gpsimd, nc.tensor
    I32 = mybir.dt.int32
    TT = NT * 128
    if not do_peer:
        with ExitStack() as pp:
            def sb(name, shape, dt):
                return pp.enter_context(nc.sbuf_tensor("%s_P%d" % (name, L), list(shape), dt))
            x1 = [sb("px1_%d" % i, [128, D], F32) for i in range(2)]
            junk = sb("pjunk", [128, D], F32)
            ss = sb("pss", [128, 16], F32)
            fg = sb("fg", [128, D], F32)
            if last:
                fw.dma("sp", fg[:], final_g.rearrange("(o n) -> o n", o=1).to_broadcast([128, D]), writes=["fg"])
                for ti in range(NT):
                    xx = x1[ti % 2]
                    xn_ = "px1_%d" % (ti % 2)
                    fw.dma("sp", xx[:], X[ti * 128:(ti + 1) * 128, :], reads=["X"], writes=[xn_])
                    final_norm(fw, nc, xx, xn_, junk, ss, fg, ti, NTP, y_p, y_s)
        return

    u_v = peer_u[L].rearrange("(i1 i2) d -> i2 i1 d", i2=128)
    v_v = peer_v[L].rearrange("(i1 i2) d -> i2 i1 d", i2=128)
    with ExitStack() as pp:
        def sb(name, shape, dt):
            return pp.enter_context(nc.sbuf_tensor("%s_Q%d" % (name, L), list(shape), dt))
        ul = [sb("ul%d" % i, [128, D], BF16) for i in range(2)]
        us = [sb("us%d" % i, [128, D], BF16) for i in range(2)]
        for i2 in range(128):
            b_ = i2 % 2
            fw.dma("pool", Vs[i2], v_v[i2], writes=["Vs%d" % i2])
            fw.dma("pool", ul[b_][:], u_v[i2], writes=["ul%d" % b_])
            bkb = banks[b_][:].bitcast(BF16)
            for c in range(8):
                fw.op("pe", lambda: PE.transpose(out=bkb[:, c * 128:(c + 1) * 128], in_=ul[b_][:, c * 128:(c + 1) * 128], identity=ident_b[:]),
                      reads=["ul%d" % b_, "ident_b"], writes=[BK[b_]])
            if b_ == 0:
                fw.op("act", lambda: S.copy(out=us[b_][:], in_=bkb), reads=[BK[b_]], writes=["us%d" % b_])
            else:
                fw.op("dve", lambda: V.tensor_copy(out=us[b_][:], in_=bkb), reads=[BK[b_]], writes=["us%d" % b_])
            fw.dma("sp", UTs[i2], us[b_][:], reads=["us%d" % b_], writes=["UTs%d" % i2])
        fw.barrier()
    stop_here(50)

    with ExitStack() as pp:
        def sb(name, shape, dt):
            return pp.enter_context(nc.sbuf_tensor("%s_R%d" % (name, L), list(shape), dt))
        wq = sb("wq", [128, 8, 2048], BF16)
        g2T = sb("g2T", [128, 8], F32)
        keysT = sb("keysT", [128, 16, 128], BF16)
        kl = sb("kl", [128, 16, 128], BF16)
        io16i = sb("io16i", [128, 16], I32)
        io16 = sb("io16", [128, 16], F32)
        x1t_2 = [sb("x1t_%d" % i_, [128, D], F32) for i_ in range(2)]
        junk_2 = [sb("junk_%d" % i_, [128, D], F32) for i_ in range(2)]
        ss_2 = [sb("ss_%d" % i_, [128, 16], F32) for i_ in range(2)]
        rq_2 = [sb("rq_%d" % i_, [128, 8], F32) for i_ in range(2)]
        xh_2 = [sb("xh_%d" % i_, [128, D], BF16) for i_ in range(2)]
        xT_2 = [sb("xT_%d" % i_, [128, 8, 128], BF16) for i_ in range(2)]
        qn_2 = [sb("qn_%d" % i_, [128, 2048], BF16) for i_ in range(2)]
        qnT_2 = [sb("qnT_%d" % i_, [128, 16, 128], BF16) for i_ in range(2)]
        scr_2 = [sb("scr_%d" % i_, [128, 2048], F32) for i_ in range(2)]
        s2_2 = [sb("s2_%d" % i_, [128, 2048], F32) for i_ in range(2)]
        m16_2 = [sb("m16_%d" % i_, [128, 16, 16], F32) for i_ in range(2)]
        i16_2 = [sb("i16_%d" % i_, [128, 16, 16], U32) for i_ in range(2)]
        i16f_2 = [sb("i16f_%d" % i_, [128, 16, 16], F32) for i_ in range(2)]
        vs_2 = [sb("vs_%d" % i_, [128, 8, 16], F32) for i_ in range(2)]
        ci_2 = [sb("ci_%d" % i_, [128, 8, 16], U32) for i_ in range(2)]
        hi_2 = [sb("hi_%d" % i_, [128, 8, 16], U32) for i_ in range(2)]
        lo_2 = [sb("lo_%d" % i_, [128, 8, 16], U32) for i_ in range(2)]
        hif_2 = [sb("hif_%d" % i_, [128, 8, 16], F32) for i_ in range(2)]
        lof_2 = [sb("lof_%d" % i_, [128, 8, 16], F32) for i_ in range(2)]
        abg_2 = [sb("abg_%d" % i_, [128, 3, 128], F32) for i_ in range(2)]
        abgT_2 = [sb("abgT_%d" % i_, [128, 3, 128], F32) for i_ in range(2)]
        Zs_2 = [sb("Zs_%d" % i_, [128, 8], F32) for i_ in range(2)]
        for hf in range(2):
            fw.dma("pool", wq[:, :, hf * 1024:(hf + 1) * 1024], peer_wq[L][:, hf * 1024:(hf + 1) * 1024].rearrange("(c p) n -> p c n", p=128), writes=["wq"])
        fw.dma("sp", g2T[:], norm2_g[L].rearrange("(c p) -> p c", p=128), writes=["g2T"])
        fw.dma("pool", kl[:], peer_keys[L].rearrange("hc k d -> k hc d"), writes=["kl"])
        for half in range(2):
            bkb = banks[half][:].bitcast(BF16)
            for j in range(8):
                hc = half * 8 + j
                fw.op("pe", lambda: PE.transpose(out=bkb[:, j * 128:(j + 1) * 128], in_=kl[:, hc, :], identity=ident_b[:]), reads=["kl", "ident_b"], writes=[BK[half]])
            fw.op("act", lambda: S.copy(out=keysT[:, half * 8:(half + 1) * 8, :], in_=bkb.rearrange("p (j k) -> p j k", j=8)), reads=[BK[half]], writes=["keysT"])
        fw.op("pool", lambda: G.iota(out=io16i[:], pattern=[[1, 16]], base=0, channel_multiplier=0), writes=["io16i"])
        fw.op("dve", lambda: V.tensor_copy(out=io16[:], in_=io16i[:]), reads=["io16i"], writes=["io16"])

        def front_a(ti):
            r0 = ti * 128
            pr_ = ti % 2
            x1t = x1t_2[pr_]
            junk = junk_2[pr_]
            ss = ss_2[pr_]
            rq = rq_2[pr_]
            xh = xh_2[pr_]
            xT = xT_2[pr_]
            qn = qn_2[pr_]
            qnT = qnT_2[pr_]
            scr = scr_2[pr_]
            s2 = s2_2[pr_]
            m16 = m16_2[pr_]
            i16 = i16_2[pr_]
            i16f = i16f_2[pr_]
            vs = vs_2[pr_]
            ci = ci_2[pr_]
            hi = hi_2[pr_]
            lo = lo_2[pr_]
            hif = hif_2[pr_]
            lof = lof_2[pr_]
            abg = abg_2[pr_]
            abgT = abgT_2[pr_]
            Zs = Zs_2[pr_]
            fwp = FWP(fw, "_p%d" % pr_)
            fwp.dma("sp", x1t[:], X[r0:r0 + 128, :], reads=["X"], writes=["x1t"])
            fwp.op("act", lambda: S.activation(out=junk[:], in_=x1t[:], func=AF.Square, accum_out=ss[:, 0:1]), reads=["x1t"], writes=["junk", "ss"])
            fwp.op("dve", lambda: V.tensor_scalar(out=ss[:, 0:1], in0=ss[:, 0:1], scalar1=1.0 / D, scalar2=EPS, op0=ALU.mult, op1=ALU.add), reads=["ss"], writes=["ss"])
            fwp.op("act", lambda: S.activation(out=ss[:, 0:1], in_=ss[:, 0:1], func=AF.Sqrt), reads=["ss"], writes=["ss"])
            fwp.op("dve", lambda: V.reciprocal(out=ss[:, 0:1], in_=ss[:, 0:1]), reads=["ss"], writes=["ss"])
            fwp.op("act", lambda: S.activation(out=xh[:], in_=x1t[:], func=AF.Copy, scale=ss[:, 0:1]), reads=["ss", "x1t"], writes=["xh"])
            b0b = banks[0][:].bitcast(BF16)
            for c in range(8):
                fwp.op("pe", lambda: PE.transpose(out=b0b[:, c * 128:(c + 1) * 128], in_=xh[:, c * 128:(c + 1) * 128], identity=ident_b[:]),
                       reads=["xh", "ident_b"], writes=[BK[0]])
            for c in range(8):
                fwp.op("act", lambda: S.activation(out=xT[:, c, :], in_=b0b[:, c * 128:(c + 1) * 128], func=AF.Copy, scale=g2T[:, c:c + 1]),
                       reads=[BK[0], "g2T"], writes=["xT"])
            fwp.dma("sp", XN2T[:, :, r0:r0 + 128], xT[:], reads=["xT"], writes=["XN2T"])
            for blk in range(4):
                for c in range(8):
                    fwp.op("pe", lambda: PE.matmul(banks[1 + blk][:], lhsT=xT[:, c, :], rhs=wq[:, c, blk * 512:(blk + 1) * 512], start=(c == 0), stop=(c == 7)),
                           reads=["xT", "wq"], writes=[BK[1 + blk]])
            for h in range(8):
                fwp.op("act", lambda: S.activation(out=junk[:, 0:256], in_=banks[1 + h // 2][:, (h % 2) * 256:(h % 2 + 1) * 256], func=AF.Square, accum_out=rq[:, h:h + 1]),
                       reads=[BK[1 + h // 2]], writes=["junk", "rq%d" % h])

        def front_b(ti):
            r0 = ti * 128
            pr_ = ti % 2
            x1t = x1t_2[pr_]
            junk = junk_2[pr_]
            ss = ss_2[pr_]
            rq = rq_2[pr_]
            xh = xh_2[pr_]
            xT = xT_2[pr_]
            qn = qn_2[pr_]
            qnT = qnT_2[pr_]
            scr = scr_2[pr_]
            s2 = s2_2[pr_]
            m16 = m16_2[pr_]
            i16 = i16_2[pr_]
            i16f = i16f_2[pr_]
            vs = vs_2[pr_]
            ci = ci_2[pr_]
            hi = hi_2[pr_]
            lo = lo_2[pr_]
            hif = hif_2[pr_]
            lof = lof_2[pr_]
            abg = abg_2[pr_]
            abgT = abgT_2[pr_]
            Zs = Zs_2[pr_]
            fwp = FWP(fw, "_p%d" % pr_)
            rqr = ["rq%d" % h for h in range(8)]
            fwp.op("dve", lambda: V.tensor_scalar(out=rq[:], in0=rq[:], scalar1=1.0 / 256, scalar2=EPS, op0=ALU.mult, op1=ALU.add), reads=rqr, writes=["rq"] + rqr)
            fwp.op("act", lambda: S.activation(out=rq[:], in_=rq[:], func=AF.Sqrt), reads=["rq"], writes=["rq"])
            fwp.op("dve", lambda: V.reciprocal(out=rq[:], in_=rq[:]), reads=["rq"], writes=["rq"])
            for h in range(8):
                fwp.op("act", lambda: S.activation(out=qn[:, h * 256:(h + 1) * 256], in_=banks[1 + h // 2][:, (h % 2) * 256:(h % 2 + 1) * 256], func=AF.Copy, scale=rq[:, h:h + 1]),
                       reads=[BK[1 + h // 2], "rq"], writes=["qn"])
            for half in range(2):
                bkb = banks[5 + half][:].bitcast(BF16)
                for j in range(8):
                    hc = half * 8 + j
                    fwp.op("pe", lambda: PE.transpose(out=bkb[:, j * 128:(j + 1) * 128], in_=qn[:, hc * 128:(hc + 1) * 128], identity=ident_b[:]),
                           reads=["qn", "ident_b"], writes=[BK[5 + half]])
                fwp.op("act", lambda: S.copy(out=qnT[:, half * 8:(half + 1) * 8, :], in_=bkb.rearrange("p (j t) -> p j t", j=8)), reads=[BK[5 + half]], writes=["qnT"])
            for hc in range(16):
                fwp.op("pe", lambda: PE.matmul(banks[1 + hc // 4][:, (hc % 4) * 128:(hc % 4 + 1) * 128], lhsT=qnT[:, hc, :], rhs=keysT[:, hc, :], start=True, stop=True),
                       reads=["qnT", "keysT"], writes=[BK[1 + hc // 4]])
            for b_ in range(4):
                fwp.op("act", lambda: S.copy(out=scr[:, b_ * 512:(b_ + 1) * 512], in_=banks[1 + b_][:]), reads=[BK[1 + b_]], writes=["scr"])

        def back_a(ti):
            r0 = ti * 128
            pr_ = ti % 2
            x1t = x1t_2[pr_]
            junk = junk_2[pr_]
            ss = ss_2[pr_]
            rq = rq_2[pr_]
            xh = xh_2[pr_]
            xT = xT_2[pr_]
            qn = qn_2[pr_]
            qnT = qnT_2[pr_]
            scr = scr_2[pr_]
            s2 = s2_2[pr_]
            m16 = m16_2[pr_]
            i16 = i16_2[pr_]
            i16f = i16f_2[pr_]
            vs = vs_2[pr_]
            ci = ci_2[pr_]
            hi = hi_2[pr_]
            lo = lo_2[pr_]
            hif = hif_2[pr_]
            lof = lof_2[pr_]
            abg = abg_2[pr_]
            abgT = abgT_2[pr_]
            Zs = Zs_2[pr_]
            fwp = FWP(fw, "_p%d" % pr_)

            def top16_batch(items, n, tag, src_res):
                K_ = len(items)
                s2v = s2[:].rearrange("p (k n) -> p k n", k=K_)
                for k, (src2d, mv, iv) in enumerate(items):
                    fwp.op("dve", lambda: V.max(out=mv[:, 0:8], in_=src2d), reads=[src_res], writes=["%sm%d" % (tag, k)])
                for k, (src2d, mv, iv) in enumerate(items):
                    fwp.op("dve", lambda: V.match_replace(out=s2v[:, k, :], in_to_replace=mv[:, 0:8], in_values=src2d, imm_value=-1e30),
                          reads=[src_res, "%sm%d" % (tag, k)], writes=["s2_%d" % k])
                for k, (src2d, mv, iv) in enumerate(items):
                    fwp.op("dve", lambda: V.max(out=mv[:, 8:16], in_=s2v[:, k, :]), reads=["s2_%d" % k], writes=["%sn%d" % (tag, k)])
                for k, (src2d, mv, iv) in enumerate(items):
                    fwp.op("dve", lambda: V.max_index(out=iv[:, 0:8], in_max=mv[:, 0:8], in_values=src2d), reads=[src_res, "%sm%d" % (tag, k)], writes=["%si%d" % (tag, k)])
                for k, (src2d, mv, iv) in enumerate(items):
                    fwp.op("dve", lambda: V.max_index(out=iv[:, 8:16], in_max=mv[:, 8:16], in_values=src2d), reads=[src_res, "%sn%d" % (tag, k)], writes=["%sj%d" % (tag, k)])
                return (["%sm%d" % (tag, k) for k in range(K_)] + ["%sn%d" % (tag, k) for k in range(K_)],
                        ["%si%d" % (tag, k) for k in range(K_)] + ["%sj%d" % (tag, k) for k in range(K_)])

            mres1, ires1 = top16_batch([(scr[:, hc * 128:(hc + 1) * 128], m16[:, hc, :], i16[:, hc, :]) for hc in range(16)], 128, "a", "scr")
            fwp.op("dve", lambda: V.tensor_copy(out=i16f[:], in_=i16[:]), reads=ires1, writes=["i16f"])
            m16v = m16[:].rearrange("p (h c) k -> p h c k", c=2)
            i16v = i16f[:].rearrange("p (h c) k -> p h c k", c=2)
            cand = scr[:].rearrange("p (h a b) -> p h a b", h=8, a=16)
            fwp.op("dve", lambda: V.tensor_tensor(out=cand, in0=m16v[:, :, 0, :].unsqueeze(3).to_broadcast([128, 8, 16, 16]),
                                                 in1=m16v[:, :, 1, :].unsqueeze(2).to_broadcast([128, 8, 16, 16]), op=ALU.add),
                  reads=mres1 + ["scr"], writes=["scr"])
            return mres1, ires1

        def back_b(ti):
            r0 = ti * 128
            pr_ = ti % 2
            x1t = x1t_2[pr_]
            junk = junk_2[pr_]
            ss = ss_2[pr_]
            rq = rq_2[pr_]
            xh = xh_2[pr_]
            xT = xT_2[pr_]
            qn = qn_2[pr_]
            qnT = qnT_2[pr_]
            scr = scr_2[pr_]
            s2 = s2_2[pr_]
            m16 = m16_2[pr_]
            i16 = i16_2[pr_]
            i16f = i16f_2[pr_]
            vs = vs_2[pr_]
            ci = ci_2[pr_]
            hi = hi_2[pr_]
            lo = lo_2[pr_]
            hif = hif_2[pr_]
            lof = lof_2[pr_]
            abg = abg_2[pr_]
            abgT = abgT_2[pr_]
            Zs = Zs_2[pr_]
            fwp = FWP(fw, "_p%d" % pr_)
            i16v = i16f[:].rearrange("p (h c) k -> p h c k", c=2)

            def top16_batch(items, n, tag, src_res):
                K_ = len(items)
                s2v = s2[:].rearrange("p (k n) -> p k n", k=K_)
                for k, (src2d, mv, iv) in enumerate(items):
                    fwp.op("dve", lambda: V.max(out=mv[:, 0:8], in_=src2d), reads=[src_res], writes=["%sm%d" % (tag, k)])
                for k, (src2d, mv, iv) in enumerate(items):
                    fwp.op("dve", lambda: V.match_replace(out=s2v[:, k, :], in_to_replace=mv[:, 0:8], in_values=src2d, imm_value=-1e30),
                          reads=[src_res, "%sm%d" % (tag, k)], writes=["s2_%d" % k])
                for k, (src2d, mv, iv) in enumerate(items):
                    fwp.op("dve", lambda: V.max(out=mv[:, 8:16], in_=s2v[:, k, :]), reads=["s2_%d" % k], writes=["%sn%d" % (tag, k)])
                for k, (src2d, mv, iv) in enumerate(items):
                    fwp.op("dve", lambda: V.max_index(out=iv[:, 0:8], in_max=mv[:, 0:8], in_values=src2d), reads=[src_res, "%sm%d" % (tag, k)], writes=["%si%d" % (tag, k)])
                for k, (src2d, mv, iv) in enumerate(items):
                    fwp.op("dve", lambda: V.max_index(out=iv[:, 8:16], in_max=mv[:, 8:16], in_values=src2d), reads=[src_res, "%sn%d" % (tag, k)], writes=["%sj%d" % (tag, k)])
                return (["%sm%d" % (tag, k) for k in range(K_)] + ["%sn%d" % (tag, k) for k in range(K_)],
                        ["%si%d" % (tag, k) for k in range(K_)] + ["%sj%d" % (tag, k) for k in range(K_)])

            mres2, ires2 = top16_batch([(scr[:, h * 256:(h + 1) * 256], vs[:, h, :], ci[:, h, :]) for h in range(8)], 256, "b", "scr")
            gt_ = abg[:, 2, :].rearrange("p (h k) -> p h k", h=8)
            fwp.op("dve", lambda: V.tensor_tensor(out=gt_, in0=vs[:], in1=vs[:, :, 0:1].to_broadcast([128, 8, 16]), op=ALU.subtract), reads=mres2, writes=["abg"])
            fwp.op("act", lambda: S.activation(out=abg[:, 2, :], in_=abg[:, 2, :], func=AF.Exp), reads=["abg"], writes=["abg"])
            fwp.op("dve", lambda: V.tensor_reduce(out=Zs[:], in_=gt_, axis=AX.X, op=ALU.add), reads=["abg"], writes=["Zs"])
            fwp.op("dve", lambda: V.reciprocal(out=Zs[:], in_=Zs[:]), reads=["Zs"], writes=["Zs"])
            fwp.op("dve", lambda: V.tensor_tensor(out=gt_, in0=gt_, in1=Zs[:].unsqueeze(2).to_broadcast([128, 8, 16]), op=ALU.mult), reads=["abg", "Zs"], writes=["abg"])
            fwp.op("dve", lambda: V.tensor_single_scalar(out=hi[:], in_=ci[:], scalar=4, op=ALU.logical_shift_right), reads=ires2, writes=["hi"])
            fwp.op("dve", lambda: V.tensor_single_scalar(out=lo[:], in_=ci[:], scalar=15, op=ALU.bitwise_and), reads=ires2, writes=["lo"])
            fwp.op("dve", lambda: V.tensor_copy(out=hif[:], in_=hi[:]), reads=["hi"], writes=["hif"])
            fwp.op("dve", lambda: V.tensor_copy(out=lof[:], in_=lo[:]), reads=["lo"], writes=["lof"])
            eq = scr[:].rearrange("p (h k a) -> p h k a", h=8, k=16)
            io_b = io16[:].unsqueeze(1).unsqueeze(1).to_broadcast([128, 8, 16, 16])
            for which, srcf in ((0, hif), (1, lof)):
                fwp.op("dve", lambda: V.tensor_tensor(out=eq, in0=srcf[:].unsqueeze(3).to_broadcast([128, 8, 16, 16]), in1=io_b, op=ALU.is_equal),
                      reads=["hif", "lof", "io16", "scr"], writes=["scr"])
                fwp.op("dve", lambda: V.tensor_tensor(out=eq, in0=eq, in1=i16v[:, :, which, :].unsqueeze(2).to_broadcast([128, 8, 16, 16]), op=ALU.mult),
                      reads=["scr", "i16f"], writes=["scr"])
                fwp.op("dve", lambda: V.tensor_reduce(out=abg[:, which, :].rearrange("p (h k) -> p h k", h=8), in_=eq, axis=AX.X, op=ALU.add), reads=["scr"], writes=["abg"])
            for w_ in range(3):
                fwp.op("pe", lambda: PE.transpose(out=banks[7][:, w_ * 128:(w_ + 1) * 128], in_=abg[:, w_, :], identity=ident_f[:]), reads=["abg", "ident_f"], writes=[BK[7]])
            fwp.op("dve", lambda: V.tensor_copy(out=abgT[:], in_=banks[7][:, 0:384].rearrange("p (w t) -> p w t", w=3)), reads=[BK[7]], writes=["abgT"])
            fwp.dma("sp", ABG[:, :, r0:r0 + 128], abgT[:], reads=["abgT"], writes=["ABG"])

        front_a(0)
        front_b(0)
        for ti in range(NT):
            if ti + 1 < NT:
                front_a(ti + 1)
            back_a(ti)
            if ti + 1 < NT:
                front_b(ti + 1)
            back_b(ti)
        fw.barrier()
    stop_here(51)

    NTG = 3
    with ExitStack() as pp:
        def sb(name, shape, dt):
            return pp.enter_context(nc.sbuf_tensor("%s_S%d" % (name, L), list(shape), dt))
        Gt = sb("Gt", [128, NTG, 128, 128], BF16)
        W1_2 = [sb("W1_%d" % i_, [128, 32, 128], BF16) for i_ in range(2)]
        W2_2 = [sb("W2_%d" % i_, [128, 32, 128], BF16) for i_ in range(2)]
        wblk = 0
        xg = sb("xg", [128, 8, NTG * 128], BF16)
        ab = sb("ab", [128, 3, NTG * 128], F32)
        x1g = [sb("x1g%d" % j, [128, D], F32) for j in range(2)]
        ut = [sb("ut%d" % i, [128, D], BF16) for i in range(3)]
        vc = [sb("vc%d" % i, [128, D], BF16) for i in range(3)]
        gl = [sb("gl%d" % i, [128, NTG * 128], F32) for i in range(2)]
        Ab = [sb("Ab%d" % i, [128, NTG * 128], BF16) for i in range(2)]
        io128i = sb("io128i", [128, 128], I32)
        io128 = sb("io128", [128, 128], F32)
        junk = sb("pjunk", [128, D], F32)
        ss = sb("pss", [128, 16], F32)
        fg = sb("fg", [128, D], F32)
        fw.op("pool", lambda: G.iota(out=io128i[:], pattern=[[1, 128]], base=0, channel_multiplier=0), writes=["io128i"])
        fw.op("dve", lambda: V.tensor_copy(out=io128[:], in_=io128i[:]), reads=["io128i"], writes=["io128"])
        if last:
            fw.dma("sp", fg[:], final_g.rearrange("(o n) -> o n", o=1).to_broadcast([128, D]), writes=["fg"])
        gcount = 0
        hcount = 0
        for j0 in range(0, NT, NTG):
            n = min(NTG, NT - j0)
            N = n * 128
            c0 = j0 * 128
            fw.dma("sp", xg[:, :, 0:N], XN2T[:, :, c0:c0 + N], reads=["XN2T"], writes=["xg"])
            fw.dma("sp", ab[:, :, 0:N], ABG[:, :, c0:c0 + N], reads=["ABG"], writes=["ab"])
            for j in range(n):
                for qq in range(4):
                    t0 = j * 128 + qq * 32
                    wp_ = wblk % 2
                    wblk += 1
                    W1, W2 = W1_2[wp_], W2_2[wp_]
                    w1n, w2n = "W1_%d" % wp_, "W2_%d" % wp_
                    io_b = io128[:].unsqueeze(1).to_broadcast([128, 32, 128])
                    fw.op("dve", lambda: V.tensor_tensor(out=W1[:], in0=ab[:, 0, t0:t0 + 32].unsqueeze(2).to_broadcast([128, 32, 128]), in1=io_b, op=ALU.is_equal),
                          reads=["ab", "io128"], writes=[w1n])
                    fw.op("dve", lambda: V.tensor_tensor(out=W1[:], in0=W1[:], in1=ab[:, 2, t0:t0 + 32].unsqueeze(2).to_broadcast([128, 32, 128]), op=ALU.mult),
                          reads=["ab", w1n], writes=[w1n])
                    fw.op("dve", lambda: V.tensor_tensor(out=W2[:], in0=ab[:, 1, t0:t0 + 32].unsqueeze(2).to_broadcast([128, 32, 128]), in1=io_b, op=ALU.is_equal),
                          reads=["ab", "io128"], writes=[w2n])
                    for q4 in range(8):
                        bk_ = 6 + (gcount % 2)
                        gcount += 1
                        for u_ in range(4):
                            tt = q4 * 4 + u_
                            fw.op("pe", lambda: PE.matmul(banks[bk_][:, u_ * 128:(u_ + 1) * 128], lhsT=W1[:, tt, :], rhs=W2[:, tt, :], start=True, stop=True),
                                  reads=[w1n, w2n], writes=[BK[bk_]])
                        dst = Gt[:, j, qq * 32 + q4 * 4:qq * 32 + q4 * 4 + 4, :]
                        fw.op("act", lambda: S.copy(out=dst, in_=banks[bk_][:].rearrange("p (u i) -> p u i", u=4)), reads=[BK[bk_]], writes=["Gt"])
            hbs = {}

            def emit_H(i2):
                nonlocal hcount
                b_ = i2 % 2
                d_ = i2 % 3
                fw.dma("sp", ut[d_][:], UTs[i2], reads=["UTs%d" % i2], writes=["ut%d" % d_])
                fw.dma("sp", vc[d_][:], Vs[i2], reads=["Vs%d" % i2], writes=["vc%d" % d_])
                hb = 6 + (hcount % 2)
                hcount += 1
                for c in range(8):
                    fw.op("pe", lambda: PE.matmul(banks[hb][:, 0:N], lhsT=ut[d_][:, c * 128:(c + 1) * 128], rhs=xg[:, c, 0:N], start=(c == 0), stop=(c == 7)),
                          reads=["ut%d" % d_, "xg"], writes=[BK[hb]])
                fw.op("act", lambda: S.activation(out=gl[b_][:, 0:N], in_=banks[hb][:, 0:N], func=AF.Gelu), reads=[BK[hb]], writes=["gl%d" % b_])
                fw.op("dve", lambda: V.tensor_tensor(out=Ab[b_][:, 0:N].rearrange("p (j t) -> p j t", j=n), in0=gl[b_][:, 0:N].rearrange("p (j t) -> p j t", j=n),
                                                     in1=Gt[:, 0:n, :, i2], op=ALU.mult), reads=["gl%d" % b_, "Gt"], writes=["Ab%d" % b_])

            def emit_Y(i2):
                b_ = i2 % 2
                d_ = i2 % 3
                for j in range(n):
                    for hf in range(2):
                        fw.op("pe", lambda: PE.matmul(banks[2 * j + hf][:], lhsT=Ab[b_][:, j * 128:(j + 1) * 128], rhs=vc[d_][:, hf * 512:(hf + 1) * 512],
                                                      start=(i2 == 0), stop=(i2 == 127)), reads=["Ab%d" % b_, "vc%d" % d_], writes=[BK[2 * j + hf]])

            emit_H(0)
            for i2 in range(128):
                if i2 + 1 < 128:
                    emit_H(i2 + 1)
                emit_Y(i2)
            for j in range(n):
                ti = j0 + j
                xb = x1g[ti % 2]
                xbn = "x1g%d" % (ti % 2)
                fw.dma("sp", xb[:], X[ti * 128:(ti + 1) * 128, :], reads=["X"], writes=[xbn])
                for hf in range(2):
                    fw.op("dve", lambda: V.tensor_tensor(out=xb[:, hf * 512:(hf + 1) * 512], in0=banks[2 * j + hf][:], in1=xb[:, hf * 512:(hf + 1) * 512], op=ALU.add),
                          reads=[BK[2 * j + hf], xbn], writes=[xbn])
                if last:
                    final_norm(fw, nc, xb, xbn, junk, ss, fg, ti, NTP, y_p, y_s)
                else:
                    fw.dma("sp", X[ti * 128:(ti + 1) * 128, :], xb[:], reads=[xbn], writes=["X"])
        fw.barrier()


def final_norm(fw, nc, xx, xn_, junk, ss, fg, ti, NTP, y_p, y_s):
    V, S = nc.vector, nc.scalar
    fw.op("act", lambda: S.activation(out=junk[:], in_=xx[:], func=AF.Square, accum_out=ss[:, 0:1]), reads=[xn_], writes=["pjunk", "pss"])
    fw.op("dve", lambda: V.tensor_scalar(out=ss[:, 0:1], in0=ss[:, 0:1], scalar1=1.0 / D, scalar2=EPS, op0=ALU.mult, op1=ALU.add), reads=["pss"], writes=["pss"])
    fw.op("act", lambda: S.activation(out=ss[:, 0:1], in_=ss[:, 0:1], func=AF.Sqrt), reads=["pss"], writes=["pss"])
    fw.op("dve", lambda: V.reciprocal(out=ss[:, 0:1], in_=ss[:, 0:1]), reads=["pss"], writes=["pss"])
    fw.op("dve", lambda: V.scalar_tensor_tensor(out=junk[:], in0=xx[:], scalar=ss[:, 0:1], in1=fg[:], op0=ALU.mult, op1=ALU.mult),
          reads=[xn_, "pss", "fg"], writes=["pjunk"])
    if ti == NTP:
        fw.dma("sp", y_s[:, :], junk[0:64, :], reads=["pjunk"], writes=["o_y"])
    else:
        fw.dma("sp", y_p[ti * 128:(ti + 1) * 128, :], junk[:], reads=["pjunk"], writes=["o_y"])


_CACHE = {}


def make_in_maps(inp, n_cores, nb_prompt):
    f = lambda a: np.ascontiguousarray(np.asarray(a, dtype=np.float32))
    rel = 127 - np.arange(383)
    bk = rel_bucket_np(rel)
    oh = np.zeros((32, 383), np.float32)
    oh[bk, np.arange(383)] = 1.0
    shared = {
        "norm1_g": f(inp["norm1_g"]), "w_in": f(inp["w_in"]), "da_lambda": f(inp["da_lambda"]),
        "da_subln_g": f(inp["da_subln_g"]), "rel_tab": f(inp["rel_bias_table"]),
        "ml_conv_w": f(inp["ml_conv_w"]), "ml_conv_b": f(inp["ml_conv_b"]), "ml_wq": f(inp["ml_wq"]), "ml_wk": f(inp["ml_wk"]),
        "ml_gate_b": f(inp["ml_gate_b"]).reshape(2, 8), "ml_norm_g": f(inp["ml_norm_g"]), "ml_skip": f(inp["ml_skip"]),
        "cm_norm_g": f(inp["cm_norm_g"]), "cm_ws": f(inp["cm_ws"]), "cm_b": f(inp["cm_b"]),
        "w_out": f(inp["w_out"]), "norm2_g": f(inp["norm2_g"]), "peer_wq": f(inp["peer_wq"]),
        "peer_keys": f(inp["peer_keys"]).reshape(2, 16, 128, 128), "peer_u": f(inp["peer_u"]), "peer_v": f(inp["peer_v"]),
        "final_g": f(inp["final_g"]), "ohrel": oh,
    }
    maps = []
    for c in range(n_cores):
        m = dict(shared)
        m["xp"] = f(inp["x_prompt"][c % nb_prompt])
        m["xs"] = f(inp["x_sample"][c])
        m["ck"] = f(inp["cache_k"][:, c]).reshape(2, 1024, 512)
        m["cv"] = f(inp["cache_v"][:, c]).reshape(2, 1024, 512)
        m["s_c"] = f(inp["state_mlstm_c"][:, c]); m["s_n"] = f(inp["state_mlstm_n"][:, c])
        m["s_m"] = f(inp["state_mlstm_m"][:, c]); m["s_conv"] = f(inp["state_mlstm_conv"][:, c])
        maps.append(m)
    return maps


def run(inp, n_cores=8, do_peer=True):
    xpr = np.asarray(inp["x_prompt"])
    B, SEQ = xpr.shape[0], xpr.shape[1]
    NTP = SEQ // 128
    key = (NTP, do_peer)
    if key not in _CACHE:
        _CACHE[key] = build(NTP, do_peer=do_peer)
    nc = _CACHE[key]
    maps = make_in_maps(inp, n_cores, B)
    res = run_bass_kernel_spmd(nc, maps, core_ids=list(range(n_cores)), trace=bool(os.environ.get("KTRACE")))
    if os.environ.get("KTRACE"):
        print("EXEC_TIME_NS", res.exec_time_ns)
    R = res.results
    DB = n_cores
    st = lambda name, cores: np.stack([np.asarray(R[c][name]) for c in cores], axis=0)
    pc = list(range(B))
    sc = list(range(DB))
    y_prompt = st("y_p", pc)
    y_sample = st("y_s", sc)
    mv = lambda a: np.ascontiguousarray(np.moveaxis(a, 0, 1))
    out = (
        y_prompt, y_sample,
        mv(st("nk_p", pc)).reshape(2, B, SEQ, 4, 128), mv(st("nv_p", pc)).reshape(2, B, SEQ, 4, 128),
        mv(st("nc_p", pc)), mv(st("nn_p", pc)), mv(st("nm_p", pc)), mv(st("nconv_p", pc)),
        mv(st("nk_s", sc)).reshape(2, DB, 64, 4, 128), mv(st("nv_s", sc)).reshape(2, DB, 64, 4, 128),
        mv(st("nc_s", sc)), mv(st("nn_s", sc)), mv(st("nm_s", sc)), mv(st("nconv_s", sc)), mv(st("ncmv_s", sc)),
    )
    return tuple(np.asarray(o, dtype=np.float32) for o in out)


def kernel(**inputs):
    return run(inputs, n_cores=8, do_peer=True)
```
